# Optimizing a Trainium2 kernel written in Bass

```python
import jax, jax.numpy as jnp
from jax import lax
import numpy as np

D_MODEL = 1024
BATCH = 2
SEQ = 16384
DEPTH = 1

N_MLSTM_HEADS = 4
MLSTM_WIDTH = D_MODEL // 2
V_HEAD_DIM = MLSTM_WIDTH // N_MLSTM_HEADS
QK_HEAD_DIM = V_HEAD_DIM // 2
QK_WIDTH = N_MLSTM_HEADS * QK_HEAD_DIM
N_GATES = 4 * N_MLSTM_HEADS
CHUNK = 128
N_FOURIER_GROUPS = 4
FOURIER_WIDTH = D_MODEL - MLSTM_WIDTH
FOURIER_GROUP_DIM = FOURIER_WIDTH // N_FOURIER_GROUPS
MIX_WIDTH = MLSTM_WIDTH + FOURIER_WIDTH
IN_SPLITS = [QK_WIDTH, QK_WIDTH, MLSTM_WIDTH, MLSTM_WIDTH, FOURIER_WIDTH, N_GATES]
IN_COLS = sum(IN_SPLITS)
D_FF = 4 * D_MODEL
N_MOD = 6
ALPHA = (2 * DEPTH) ** 0.25
BETA = (8 * DEPTH) ** -0.25
LN_EPS = 1e-5

kernel_name = "hybrid_mlstm_fnet_deepnorm_adaln_block"


def _ln_plain(x):
    xf = x.astype(jnp.float32)
    mu = xf.mean(-1, keepdims=True)
    var = jnp.square(xf - mu).mean(-1, keepdims=True)
    return ((xf - mu) * lax.rsqrt(var + LN_EPS)).astype(x.dtype)


def _ln_affine(x, g, b):
    xf = x.astype(jnp.float32)
    mu = xf.mean(-1, keepdims=True)
    var = jnp.square(xf - mu).mean(-1, keepdims=True)
    y = (xf - mu) * lax.rsqrt(var + LN_EPS) * g.astype(jnp.float32) + b.astype(jnp.float32)
    return y.astype(x.dtype)


def _mlstm_chunkwise(q, k, v, log_i, log_f):
    B, H, S, dk = q.shape
    dv = v.shape[-1]
    nc = S // CHUNK

    def to_chunks(a):
        return jnp.moveaxis(a.reshape((B, H, nc, CHUNK) + a.shape[3:]), 2, 0)

    xs = tuple(to_chunks(a) for a in (q, k, v, log_i, log_f))
    lower = jnp.tril(jnp.ones((CHUNK, CHUNK), dtype=bool))

    def step(carry, blk):
        C, n, m = carry
        qb, kb, vb, ib, fb = blk
        b = jnp.cumsum(fb, axis=-1)
        D = jnp.where(lower, b[..., :, None] - b[..., None, :] + ib[..., None, :], -jnp.inf)
        inter = b + m[..., None]
        m_t = jnp.maximum(inter, D.max(-1))
        w_intra = jnp.exp(D - m_t[..., None])
        w_inter = jnp.exp(inter - m_t)
        s = jnp.einsum('bhtd,bhsd->bhts', qb, kb) * w_intra
        num = jnp.einsum('bhts,bhsv->bhtv', s, vb) + w_inter[..., None] * jnp.einsum('bhtd,bhdv->bhtv', qb, C)
        den = s.sum(-1) + w_inter * jnp.einsum('bhtd,bhd->bht', qb, n)
        h = num / jnp.maximum(jnp.abs(den), jnp.exp(-m_t))[..., None]
        bL = b[..., -1]
        g = bL[..., None] - b + ib
        m_new = jnp.maximum(bL + m, g.max(-1))
        wk = jnp.exp(g - m_new[..., None])
        decay = jnp.exp(bL + m - m_new)
        C_new = decay[..., None, None] * C + jnp.einsum('bhs,bhsd,bhsv->bhdv', wk, kb, vb)
        n_new = decay[..., None] * n + jnp.einsum('bhs,bhsd->bhd', wk, kb)
        return (C_new, n_new, m_new), h

    init = (jnp.zeros((B, H, dk, dv), jnp.float32), jnp.zeros((B, H, dk), jnp.float32),
            jnp.zeros((B, H), jnp.float32))
    _, hs = lax.scan(step, init, xs)
    return jnp.moveaxis(hs, 0, 2).reshape(B, H, S, dv)


def _mlstm_group(q, k, v, o, gates, b_gate, norm_w):
    B, S, _ = q.shape
    H = N_MLSTM_HEADS
    f32 = jnp.float32
    qh = q.astype(f32).reshape(B, S, H, QK_HEAD_DIM).transpose(0, 2, 1, 3) * (QK_HEAD_DIM ** -0.5)
    kh = k.astype(f32).reshape(B, S, H, QK_HEAD_DIM).transpose(0, 2, 1, 3)
    vh = v.astype(f32).reshape(B, S, H, V_HEAD_DIM).transpose(0, 2, 1, 3)
    gt = (gates.astype(f32) + b_gate.astype(f32)).reshape(B, S, 4, H).transpose(2, 0, 3, 1)
    log_i_f, log_f_f = gt[0], jax.nn.log_sigmoid(gt[1])
    log_i_b, log_f_b = gt[2], jax.nn.log_sigmoid(gt[3])
    h_fwd = _mlstm_chunkwise(qh, kh, vh, log_i_f, log_f_f)
    rev = lambda a: jnp.flip(a, axis=2)
    h_bwd = rev(_mlstm_chunkwise(rev(qh), rev(kh), rev(vh), rev(log_i_b), rev(log_f_b)))
    h = h_fwd + h_bwd
    mu = h.mean(-1, keepdims=True)
    var = jnp.square(h - mu).mean(-1, keepdims=True)
    h = (h - mu) * lax.rsqrt(var + LN_EPS) * norm_w.astype(f32).reshape(H, 1, V_HEAD_DIM)
    h = h.transpose(0, 2, 1, 3).reshape(B, S, MLSTM_WIDTH)
    return (h * jax.nn.sigmoid(o.astype(f32))).astype(q.dtype)


def _fourier_group(z):
    B, S, _ = z.shape
    zg = z.astype(jnp.float32).reshape(B, S, N_FOURIER_GROUPS, FOURIER_GROUP_DIM)
    y = jnp.fft.fft2(zg, axes=(1, 3), norm='ortho').real
    return y.reshape(B, S, FOURIER_WIDTH).astype(z.dtype)


def _layer(x, c_act, w_ada, b_ada, w_in, b_gate, mlstm_norm_w, w_out, ln1_g, ln1_b,
           w_ff1, b_ff1, w_ff2, b_ff2, ln2_g, ln2_b):
    mod = (c_act @ w_ada + b_ada)[:, None, :]
    sh1, sc1, g1, sh2, sc2, g2 = jnp.split(mod, N_MOD, axis=-1)
    h = _ln_plain(x) * (1 + sc1) + sh1
    proj = h @ w_in
    q, k, v, o, fz, gates = jnp.split(proj, np.cumsum(IN_SPLITS)[:-1].tolist(), axis=-1)
    y_mlstm = _mlstm_group(q, k, v, o, gates, b_gate, mlstm_norm_w)
    y_fourier = _fourier_group(fz)
    mix = jnp.concatenate([y_mlstm, y_fourier], axis=-1) @ w_out
    x = _ln_affine(ALPHA * x + (1 + g1) * mix, ln1_g, ln1_b)
    h2 = _ln_plain(x) * (1 + sc2) + sh2
    ff = jnp.square(jax.nn.relu(h2 @ w_ff1 + b_ff1)) @ w_ff2 + b_ff2
    return _ln_affine(ALPHA * x + (1 + g2) * ff, ln2_g, ln2_b)


def setup_inputs(seed: int = 0) -> dict:
    key = jax.random.key(seed)
    ks = jax.random.split(key, 20)
    f32 = jnp.float32
    nrm = lambda k, shape, s: jax.random.normal(k, shape, f32) * s
    f_bias = jnp.linspace(3.0, 6.0, N_MLSTM_HEADS, dtype=f32)
    gate_base = jnp.concatenate([jnp.zeros((N_MLSTM_HEADS,), f32), f_bias,
                                 jnp.zeros((N_MLSTM_HEADS,), f32), f_bias])
    return {
        "x": nrm(ks[0], (BATCH, SEQ, D_MODEL), 1.0),
        "c": nrm(ks[1], (BATCH, D_MODEL), 1.0),
        "w_ada": nrm(ks[2], (DEPTH, D_MODEL, N_MOD * D_MODEL), 0.1 * D_MODEL ** -0.5),
        "b_ada": nrm(ks[3], (DEPTH, N_MOD * D_MODEL), 0.01),
        "w_in": nrm(ks[4], (DEPTH, D_MODEL, IN_COLS), D_MODEL ** -0.5),
        "b_gate": gate_base[None, :] + nrm(ks[5], (DEPTH, N_GATES), 0.1),
        "mlstm_norm_w": 1.0 + nrm(ks[6], (DEPTH, MLSTM_WIDTH), 0.02),
        "w_out": nrm(ks[7], (DEPTH, MIX_WIDTH, D_MODEL), BETA * MIX_WIDTH ** -0.5),
        "ln1_g": 1.0 + nrm(ks[8], (DEPTH, D_MODEL), 0.02),
        "ln1_b": nrm(ks[9], (DEPTH, D_MODEL), 0.02),
        "w_ff1": nrm(ks[10], (DEPTH, D_MODEL, D_FF), D_MODEL ** -0.5),
        "b_ff1": nrm(ks[11], (DEPTH, D_FF), 0.02),
        "w_ff2": nrm(ks[12], (DEPTH, D_FF, D_MODEL), BETA * D_FF ** -0.5),
        "b_ff2": nrm(ks[13], (DEPTH, D_MODEL), 0.02),
        "ln2_g": 1.0 + nrm(ks[14], (DEPTH, D_MODEL), 0.02),
        "ln2_b": nrm(ks[15], (DEPTH, D_MODEL), 0.02),
    }


def reference(x, c, w_ada, b_ada, w_in, b_gate, mlstm_norm_w, w_out, ln1_g, ln1_b,
              w_ff1, b_ff1, w_ff2, b_ff2, ln2_g, ln2_b):
    c_act = jax.nn.silu(c)
    for l in range(DEPTH):
        x = _layer(x, c_act, w_ada[l], b_ada[l], w_in[l], b_gate[l], mlstm_norm_w[l], w_out[l],
                   ln1_g[l], ln1_b[l], w_ff1[l], b_ff1[l], w_ff2[l], b_ff2[l], ln2_g[l], ln2_b[l])
    return x
```

```python
import contextlib
import numpy as np
import concourse.bass as bass
import concourse.mybir as mybir
from concourse.bass_utils import run_bass_kernel_spmd

F32 = mybir.dt.float32
BF16 = mybir.dt.bfloat16
AF = mybir.ActivationFunctionType
ALU = mybir.AluOpType

D = 1024
DFF = 4096
LN_EPS = 1e-5
ALPHA = 2.0 ** 0.25
NCORES = 8


class Prog:
    ENG = ('pe', 'act', 'dve', 'pool', 'sp')

    def __init__(self):
        self.streams = {e: [] for e in self.ENG}
        self.cnt = {}
        self.lastw = {}
        self.rd = {}
        self.waited = {e: {} for e in self.ENG}
        self.semnames = ['S_' + e for e in self.ENG]
        self.psum_names = set()
        self.pacc = {}
        self._rec = None

    def record(self, body):
        assert self._rec is None
        self._rec = []
        body()
        r = self._rec
        self._rec = None
        return r

    def replay(self, *lists):
        lists = [l for l in lists if l]
        pos = [0] * len(lists)
        total = sum(len(l) for l in lists)
        for _ in range(total):
            best = None
            for i, l in enumerate(lists):
                if pos[i] < len(l):
                    frac = pos[i] / len(l)
                    if best is None or frac < best[0]:
                        best = (frac, i)
            i = best[1]
            self.op(*lists[i][pos[i]])
            pos[i] += 1

    def op(self, eng, fn, reads=(), writes=(), dsem=None, inc=None):
        if self._rec is not None:
            self._rec.append((eng, fn, tuple(reads), tuple(writes), dsem, inc))
            return None
        d = {}

        def add(t):
            if t is None:
                return
            s, v = t
            if d.get(s, 0) < v:
                d[s] = v
        for r in reads:
            add(self.lastw.get(r))
        for w in writes:
            add(self.lastw.get(w))
            for s, v in self.rd.get(w, {}).items():
                add((s, v))
        for n in list(reads) + list(writes):
            if n in self.psum_names:
                for e2, t2 in self.pacc.get(n, {}).items():
                    if e2 != eng:
                        add(t2)
        waits = []
        for s, v in d.items():
            if eng == 'pe' and s == 'S_pe':
                continue
            if self.waited[eng].get(s, 0) >= v:
                continue
            self.waited[eng][s] = v
            waits.append((s, v))
        if dsem is None:
            s = 'S_' + eng
            inc = 1
        else:
            s = dsem
            inc = 16 if inc is None else inc
            if s not in self.semnames:
                self.semnames.append(s)
        self.cnt[s] = self.cnt.get(s, 0) + inc
        t = (s, self.cnt[s])
        for r in reads:
            m = self.rd.setdefault(r, {})
            if m.get(s, 0) < t[1]:
                m[s] = t[1]
        for w in writes:
            self.lastw[w] = t
            self.rd[w] = {}
        for n in list(reads) + list(writes):
            if n in self.psum_names:
                self.pacc.setdefault(n, {})[eng] = t
        self.streams[eng].append((waits, fn, s, inc))
        return t

    def barrier_all(self, eng):
        waits = []
        for s, v in self.cnt.items():
            if self.waited[eng].get(s, 0) >= v:
                continue
            self.waited[eng][s] = v
            waits.append((s, v))
        self.streams[eng].append((waits, None, None, 0))

    def emit(self, nc, sems, block):
        def run(engname):
            def body(eng):
                for waits, fn, s, inc in self.streams[engname]:
                    for ws, wv in waits:
                        eng.wait_ge(sems[ws], wv)
                    if fn is None:
                        continue
                    inst = fn(eng)
                    inst.then_inc(sems[s], inc)
            return body
        block.tensor(run('pe'))
        block.scalar(run('act'))
        block.vector(run('dve'))
        block.gpsimd(run('pool'))
        block.sync(run('sp'))


def build_nc(N1, stop_after=None, fused=False):
    S = 128 * N1
    TQ = S // 4
    nc = bass.Bass("TRN2", target_bir_lowering=False)
    P = Prog()

    def din(name, shape, dt=F32):
        return nc.dram_tensor(name, shape, dt, kind="ExternalInput").ap()

    xb = din("xb", [S, D])
    cT = din("cT", [128, 8])
    w_ada = din("w_ada", [D, 6 * D])
    b_adaT = din("b_adaT", [128, 48])
    w_in = din("w_in", [D, 516])
    b_gate = din("b_gate", [1, 4])
    nw = din("nw", [1, 128])
    c_ident = din("c_ident", [128, 128])
    c_maskf = din("c_maskf", [128, 128])
    c_maskb = din("c_maskb", [128, 128])
    c_ule = din("c_ule", [128, 128])
    c_uge = din("c_uge", [128, 128])
    c_cs = din("c_cs", [128, 256])
    c_a1 = din("c_a1", [N1, 2 * N1])
    c_a2 = din("c_a2", [N1, 2 * N1])
    c_tw = din("c_tw", [128, 4 * N1])
    c_cn = din("c_cn", [128, 256])

    dbg = None
    if not fused:
        dbg = nc.dram_tensor("dbg", [S, 256], BF16, kind="ExternalOutput").ap()
    else:
        xq = din("xq", [TQ, D])
        b_ada_row = din("b_ada_row", [1, 6 * D])
        w_out = din("w_out", [D, D])
        ln1_g = din("ln1_g", [1, D]); ln1_b = din("ln1_b", [1, D])
        w_ff1 = din("w_ff1", [D, DFF]); b_ff1T = din("b_ff1T", [128, 32])
        w_ff2 = din("w_ff2", [DFF, D]); b_ff2 = din("b_ff2", [1, D])
        ln2_g = din("ln2_g", [1, D]); ln2_b = din("ln2_b", [1, D])
        gidx = din("gidx", [128, 4 * (TQ // 128)], mybir.dt.int32)
        out = nc.dram_tensor("out", [TQ, D], F32, kind="ExternalOutput").ap()

    so_d = nc.dram_tensor("so_d", [S, 128], F32).ap()
    hf_d = nc.dram_tensor("hf_d", [S, 128], F32).ap()
    mixg_m = nc.dram_tensor("mixg_m", [S, 128], BF16)
    mixg_f = nc.dram_tensor("mixg_f", [S, 128], BF16)
    gath_m = nc.dram_tensor("gath_m", [4 * S, 128], BF16)
    gath_f = nc.dram_tensor("gath_f", [4 * S, 128], BF16)
    NP_ = max(1, S // 2048)
    R_ = S // NP_
    CPP = R_ // 128

    import contextlib
    es = contextlib.ExitStack()
    with es:
        cur = [es]
        sems = {}

        def sb(name, shape, dt=F32):
            return cur[-1].enter_context(nc.sbuf_tensor(name, shape, dt))

        def flush():
            P.barrier_all('sp')
            for sname in P.semnames:
                if sname not in sems:
                    sems[sname] = es.enter_context(nc.semaphore(sname))
            with nc.Block() as block:
                P.emit(nc, sems, block)
            for e_ in P.ENG:
                P.streams[e_] = []
            for e_ in P.ENG:
                P.barrier_all(e_)

        psA = contextlib.ExitStack()
        psA.__enter__()

        def ps(name, shape, dt=F32):
            return psA.enter_context(nc.psum_tensor(name, shape, dt))

        ident = sb("ident", [128, 128], BF16)
        ones = sb("ones", [128, 128], F32)
        modT = sb("modT", [128, 48], F32)
        sc2p = sb("sc2p", [128, 8], F32)
        epsT = sb("epsT", [128, 1], F32)
        sGA = contextlib.ExitStack(); sGA.__enter__(); cur.append(sGA)
        maskf = sb("maskf", [128, 128], F32)
        maskb = sb("maskb", [128, 128], F32)
        ule = sb("ule", [128, 128], F32)
        uge = sb("uge", [128, 128], F32)
        bg = sb("bg", [128, 4], F32)
        nwb = sb("nwb", [128, 128], F32)
        cact = sb("cact", [128, 8], F32)
        cact_b = sb("cact_b", [128, 8], BF16)
        badaT = sb("badaT", [128, 48], F32)
        sc1p = sb("sc1p", [128, 8], F32)
        w_sb = sb("w_sb", [128, 8, 516], BF16)

        def ld(eng, dst, src, name, sem, reads=()):
            P.op(eng, lambda e: e.dma_start(out=dst, in_=src), reads=reads, writes=[name], dsem=sem)

        ld('pool', ident[:], c_ident, 'ident', 'D_c0')
        ld('sp', maskf[:], c_maskf, 'maskf', 'D_c1')
        ld('sp', maskb[:], c_maskb, 'maskb', 'D_c2')
        ld('sp', ule[:], c_ule, 'ule', 'D_c3')
        ld('sp', uge[:], c_uge, 'uge', 'D_c4')
        ld('sp', bg[:], b_gate.broadcast_to([128, 4]), 'bg', 'D_c5')
        ld('sp', nwb[:], nw.broadcast_to([128, 128]), 'nwb', 'D_c6')
        ld('sp', cact[:], cT, 'cact', 'D_c7')
        ld('sp', badaT[:], b_adaT, 'badaT', 'D_c8')
        ld('pool', w_sb[:], w_in.rearrange("(j p) n -> p j n", p=128), 'w_sb', 'D_c9')
        P.op('pool', lambda e: e.memset(ones[:], 1.0), writes=['ones'])
        P.op('pool', lambda e: e.memset(epsT[:], LN_EPS), writes=['epsT'])

        def finish():
            flush()
            return nc

        if stop_after == 'C':
            return finish()
        P.op('act', lambda e: e.activation(out=cact[:], in_=cact[:], func=AF.Silu), reads=['cact'], writes=['cact'])
        s0 = contextlib.ExitStack()
        s0.__enter__()
        cur.append(s0)
        wa = [sb(f"wa{i}", [128, 8, 1024], F32) for i in range(2)]
        if fused:
            g1b = sb("g1b", [128, 1024], F32)
            g2b = sb("g2b", [128, 1024], F32)
            g1row = nc.dram_tensor("g1row", [1, 1024], F32).ap()
            g2row = nc.dram_tensor("g2row", [1, 1024], F32).ap()
        pT = ps("pT", [128, 2, 4, 2, 128], BF16)
        pFZ = ps("pFZ", [128, 512], F32)
        pTMb = ps("pTM", [128, 512], F32)
        pTM = [pTMb, pTMb]
        pQ = ps("pQ", [128, 1024], BF16)
        pS = ps("pS", [128, 512], F32)
        pKV = ps("pKV", [128, 512], F32)
        pND = ps("pND", [128, 512], F32)
        P.psum_names |= {'pTa', 'pTb', 'pFZ', 'pTM', 'pQ', 'pS', 'pKV', 'pND'}
        pmod = pFZ
        pmod2 = pTM[0]
        crep = sb("crep", [128, 8, 128], F32)
        def mk_crep(e):
            i = None
            for j in range(8):
                i = e.tensor_scalar(out=crep[:, j, :], in0=ones[:], scalar1=cact[:, j:j + 1], scalar2=None,
                                    op0=ALU.mult)
            return i
        P.op('dve', mk_crep, reads=['cact', 'ones'], writes=['crep'])

        def load_wa(i, part):
            ld('sp', wa[i][:], w_ada[:, part * 1024:(part + 1) * 1024].rearrange("(j p) n -> p j n", p=128),
               f'wa{i}', f'D_wa{i}')

        def mod_featmajor(i, part):
            def f(e):
                inst = None
                for m in range(8):
                    for j in range(8):
                        inst = e.matmul(pmod[:, m:m + 1], lhsT=wa[i][:, j, m * 128:(m + 1) * 128],
                                        rhs=cact[:, j:j + 1], start=(j == 0), stop=(j == 7))
                return inst
            P.op('pe', f, reads=[f'wa{i}', 'cact'], writes=['pFZ'])
            P.op('dve', lambda e: e.tensor_tensor(out=modT[:, part * 8:(part + 1) * 8], in0=pmod[:, 0:8],
                                                  in1=badaT[:, part * 8:(part + 1) * 8], op=ALU.add),
                 reads=['pFZ', 'badaT'], writes=[f'modT{part}'])

        def mod_rowbcast(i, part, dst, dname):
            ld('sp', dst[:], b_ada_row[:, part * 1024:(part + 1) * 1024].broadcast_to([128, 1024]), dname,
               f'D_{dname}')
            for h in range(2):
                def f(e, h=h):
                    inst = None
                    for j in range(8):
                        inst = e.matmul(pmod2[:, :], lhsT=crep[:, j, :], rhs=wa[i][:, j, h * 512:(h + 1) * 512],
                                        start=(j == 0), stop=(j == 7))
                    return inst
                P.op('pe', f, reads=[f'wa{i}', 'crep'], writes=['pTM'])
                P.op('dve', lambda e, h=h: e.scalar_tensor_tensor(
                    out=dst[:, h * 512:(h + 1) * 512], in0=dst[:, h * 512:(h + 1) * 512], scalar=1.0,
                    in1=pmod2[:, :], op0=ALU.add, op1=ALU.add), reads=['pTM', dname], writes=[dname])

        load_wa(0, 0); load_wa(1, 1)
        mod_featmajor(0, 0)
        mod_featmajor(1, 1)
        P.op('dve', lambda e: e.tensor_scalar(out=sc1p[:], in0=modT[:, 8:16], scalar1=1.0, scalar2=None, op0=ALU.add),
             reads=['modT1'], writes=['sc1p'])
        if fused:
            load_wa(0, 2); load_wa(1, 3)
            mod_rowbcast(0, 2, g1b, 'g1b')
            mod_featmajor(1, 3)
            load_wa(0, 4); load_wa(1, 5)
            mod_featmajor(0, 4)
            P.op('dve', lambda e: e.tensor_scalar(out=sc2p[:], in0=modT[:, 32:40], scalar1=1.0, scalar2=None,
                                                  op0=ALU.add), reads=['modT4'], writes=['sc2p'])
            mod_rowbcast(1, 5, g2b, 'g2b')
            P.op('sp', lambda e: e.dma_start(out=g1row, in_=g1b[0:1, :]), reads=['g1b'], writes=['g1row'], dsem='D_g1r')
            P.op('sp', lambda e: e.dma_start(out=g2row, in_=g2b[0:1, :]), reads=['g2b'], writes=['g2row'], dsem='D_g2r')
        flush()
        cur.pop()
        s0.__exit__(None, None, None)
        sAF = contextlib.ExitStack(); sAF.__enter__(); cur.append(sAF)
        fzT = sb("fzT", [128, S], BF16)
        sA = contextlib.ExitStack(); sA.__enter__(); cur.append(sA)
        if stop_after == 'P0':
            return finish()
        NG = N1 // 2
        xt = [sb(f"xt{i}", [128, 1024], F32) for i in range(4)]
        xn = [sb(f"xn{i}", [128, 1024], BF16) for i in range(2)]
        bst = [sb(f"bst{i}", [128, 2, 6], F32) for i in range(2)]
        mv = [sb(f"mv{i}", [128, 2], F32) for i in range(2)]
        rs = [sb(f"rs{i}", [128, 2], F32) for i in range(2)]
        hT = [sb(f"hT{i}", [128, 8, 256], BF16) for i in range(2)]
        qk_tm = sb("qk_tm", [128, N1, 128], BF16)
        vp = sb("vp", [128, N1, 130], BF16)
        gt = sb("gt", [128, N1, 4], F32)
        scal = sb("scal", [128, NG, 16], F32)
        so_t = [sb(f"so{i}", [128, 128], F32) for i in range(2)]
        hf_t = [sb(f"hf{i}", [128, 128], F32) for i in range(2)]
        qkT = [sb(f"qkT{i}", [64, 2, 128], BF16) for i in range(2)]
        PTs = [sb(f"PTs{i}", [128, 128], BF16) for i in range(2)]
        ku = [sb(f"ku{i}", [128, 64], BF16) for i in range(2)]
        Dst = sb("Dst", [64, 129], F32)
        Cb = [sb(f"Cb{i}", [64, 129], BF16) for i in range(2)]
        ee = [sb(f"ee{i}", [128, 2, 2], F32) for i in range(2)]
        sp_ = [sb(f"sp{i}", [128, 2, 2], F32) for i in range(2)]
        ein = [sb(f"ein{i}", [128, 4, 4], F32) for i in range(2)]
        dd = [sb(f"dd{i}", [128, 2], F32) for i in range(2)]

        P.op('pool', lambda e: e.memset(vp[:, :, 128:130], 1.0), writes=['vp_ones'])

        def ln_stats(xtile, xname, k, par):
            def f(e):
                e.bn_stats(out=bst[par][:, 0, :], in_=xtile[:, 0:512])
                return e.bn_stats(out=bst[par][:, 1, :], in_=xtile[:, 512:1024])
            P.op('dve', f, reads=[xname], writes=[f'bst{par}'])
            P.op('dve', lambda e: e.bn_aggr(out=mv[par][:], in_=bst[par][:].rearrange("p a b -> p (a b)")),
                 reads=[f'bst{par}'], writes=[f'mv{par}'])
            P.op('act', lambda e: e.activation(out=rs[par][:, 0:1], in_=mv[par][:, 1:2], func=AF.Sqrt,
                                               bias=epsT[:, 0:1]),
                 reads=[f'mv{par}', 'epsT'], writes=[f'rs{par}a'])
            P.op('dve', lambda e: e.reciprocal(out=rs[par][:, 0:1], in_=rs[par][:, 0:1]),
                 reads=[f'rs{par}a'], writes=[f'rs{par}a'])
            P.op('dve', lambda e: e.tensor_scalar(out=rs[par][:, 1:2], in0=mv[par][:, 0:1], scalar1=rs[par][:, 0:1],
                                                  scalar2=-1.0, op0=ALU.mult, op1=ALU.mult),
                 reads=[f'rs{par}a', f'mv{par}'], writes=[f'rs{par}b'])

        def gate_scalars(g):
            gp = g % 2
            c0 = 2 * g
            def f1(e):
                e.activation(out=ee[gp][:, 0, :], in_=gt[:, c0:c0 + 2, 1], func=AF.Exp, scale=-1.0)
                return e.activation(out=ee[gp][:, 1, :], in_=gt[:, c0:c0 + 2, 3], func=AF.Exp, scale=-1.0)
            P.op('act', f1, reads=[f'gt{c0}', f'gt{c0 + 1}'], writes=[f'ee{gp}'])
            P.op('act', lambda e: e.activation(out=sp_[gp][:].rearrange("p a b -> p (a b)"),
                                               in_=ee[gp][:].rearrange("p a b -> p (a b)"), func=AF.Ln, bias=1.0),
                 reads=[f'ee{gp}'], writes=[f'sp{gp}'])
            def f2(e):
                e.matmul(pKV[:, 384:386], lhsT=ule[:], rhs=sp_[gp][:, 0, :], start=True, stop=True)
                e.matmul(pKV[:, 386:388], lhsT=uge[:], rhs=sp_[gp][:, 1, :], start=True, stop=True)
                return e.matmul(pKV[:, 388:392], lhsT=ones[:], rhs=sp_[gp][:].rearrange("p a b -> p (a b)"),
                                start=True, stop=True)
            P.op('pe', f2, reads=[f'sp{gp}', 'ule', 'uge', 'ones'], writes=['pKV'])
            def f3(e):
                e.tensor_tensor(out=ein[gp][:, 0, 0:2], in0=gt[:, c0:c0 + 2, 0], in1=pKV[:, 384:386], op=ALU.add)
                e.tensor_tensor(out=ein[gp][:, 0, 2:4], in0=gt[:, c0:c0 + 2, 2], in1=pKV[:, 386:388], op=ALU.add)
                e.tensor_copy(out=ein[gp][:, 1, :], in_=pKV[:, 384:388])
                return e.tensor_scalar(out=ein[gp][:, 3, :], in0=pKV[:, 388:392], scalar1=-1.0, scalar2=None,
                                       op0=ALU.mult)
            P.op('dve', f3, reads=['pKV', f'gt{c0}', f'gt{c0 + 1}'], writes=[f'ein{gp}a'])
            P.op('dve', lambda e: e.tensor_tensor(out=ein[gp][:, 2, :], in0=ein[gp][:, 0, :], in1=pKV[:, 388:392],
                                                  op=ALU.subtract),
                 reads=['pKV', f'ein{gp}a'], writes=[f'ein{gp}'])
            P.op('act', lambda e: e.activation(out=scal[:, g, :], in_=ein[gp][:].rearrange("p a b -> p (a b)"),
                                               func=AF.Exp),
                 reads=[f'ein{gp}', f'ein{gp}a'], writes=[f'scal{g}'])

        def sc_ap(g, k, d, ci, rows=128):
            i = k * 4 + d * 2 + ci
            return scal[0:rows, g, i:i + 1]

        def mlstm_chunk(c, d, part='both'):
            g = c // 2
            ci = c % 2
            par = c % 2
            mask, mname = (maskf, 'maskf') if d == 0 else (maskb, 'maskb')
            kv = pKV[0:64, par * 129:(par + 1) * 129]
            kvn = 'pKV'
            nd = pND[:, 0:129]
            cbi = c % 2
            if part in ('front', 'both'):
                mlstm_front(c, d, g, ci, par, mask, mname, kv, kvn)
            if part in ('tail', 'both'):
                mlstm_tail(c, d, g, ci, par, kv, kvn, nd, cbi)
            return nd, par

        def mlstm_front(c, d, g, ci, par, mask, mname, kv, kvn):
            def ftr(e):
                e.transpose(out=pQ[0:64, 0:128], in_=qk_tm[:, c, 0:64], identity=ident[:])
                return e.transpose(out=pQ[0:64, 128:256], in_=qk_tm[:, c, 64:128],
                                   identity=ident[:])
            P.op('pe', ftr, reads=[f'qk{c}', 'ident'], writes=['pQ'])
            P.op('act', lambda e: e.activation(out=qkT[par][:].rearrange("p a b -> p (a b)"),
                                               in_=pQ[0:64, 0:256], func=AF.Copy),
                 reads=['pQ'], writes=[f'qkT{par}'])
            P.op('pe', lambda e: e.matmul(pS[:, 0:128], lhsT=qkT[par][:, 1, :],
                                          rhs=qkT[par][:, 0, :], start=True, stop=True),
                 reads=[f'qkT{par}'], writes=['pS'])
            P.op('dve', lambda e: e.scalar_tensor_tensor(out=PTs[par][:], in0=pS[:, 0:128],
                                                         scalar=sc_ap(g, 0, d, ci), in1=mask[:], op0=ALU.mult,
                                                         op1=ALU.mult),
                 reads=['pS', f'scal{g}', mname], writes=[f'PTs{par}'])
            P.op('pool', lambda e: e.tensor_scalar(out=ku[par][:], in0=qk_tm[:, c, 64:128], scalar1=sc_ap(g, 2, d, ci),
                                                   scalar2=None, op0=ALU.mult),
                 reads=[f'qk{c}', f'scal{g}'], writes=[f'ku{par}'])
            P.op('pe', lambda e: e.matmul(kv, lhsT=ku[par][:], rhs=vp[:, c, 0:129], start=True, stop=True),
                 reads=[f'ku{par}', f'vp{c}', 'vp_ones'], writes=[kvn])

        def mlstm_tail(c, d, g, ci, par, kv, kvn, nd, cbi):
            def fnd(e):
                e.matmul(nd, lhsT=PTs[par][:], rhs=vp[:, c, 0:129], start=True, stop=False)
                return e.matmul(nd, lhsT=qkT[par][:, 0, :], rhs=Cb[cbi][:], start=False, stop=True)
            P.op('pe', fnd, reads=[f'PTs{par}', f'vp{c}', 'vp_ones', f'qkT{par}', f'Cb{cbi}'], writes=['pND'])
            P.op('dve', lambda e: e.tensor_scalar(out=dd[par][:, 0:1], in0=nd[:, 128:129], scalar1=-1.0, scalar2=None,
                                                  op0=ALU.mult),
                 reads=['pND'], writes=[f'dd{par}'])
            P.op('dve', lambda e: e.tensor_tensor(out=dd[par][:, 0:1], in0=nd[:, 128:129], in1=dd[par][:, 0:1],
                                                  op=ALU.max),
                 reads=['pND', f'dd{par}'], writes=[f'dd{par}'])
            P.op('dve', lambda e: e.tensor_scalar(out=dd[par][:, 0:1], in0=dd[par][:, 0:1], scalar1=sc_ap(g, 1, d, ci),
                                                  scalar2=None, op0=ALU.max),
                 reads=[f'dd{par}', f'scal{g}'], writes=[f'dd{par}'])
            P.op('dve', lambda e: e.reciprocal(out=dd[par][:, 1:2], in_=dd[par][:, 0:1]),
                 reads=[f'dd{par}'], writes=[f'ddr{par}'])
            P.op('dve', lambda e: e.scalar_tensor_tensor(out=Dst[:], in0=Dst[:], scalar=sc_ap(g, 3, d, ci, 64),
                                                         in1=kv, op0=ALU.mult, op1=ALU.add),
                 reads=[kvn, 'Dst', f'scal{g}'], writes=['Dst'])
            P.op('act', lambda e: e.activation(out=Cb[1 - cbi][:], in_=Dst[:], func=AF.Copy, scale=0.125),
                 reads=['Dst'], writes=[f'Cb{1 - cbi}'])

        def stageA(g):
            gp = g % 2
            for t in range(2):
                c = 2 * g + t
                xi = c % 4
                ld('sp', xt[xi][:], xb[c * 128:(c + 1) * 128, :], f'xt{xi}', f'D_x{xi}')
                ln_stats(xt[xi], f'xt{xi}', c, t)
                P.op('act', lambda e, xi=xi, t=t: e.activation(out=xn[t][:], in_=xt[xi][:], func=AF.Identity,
                                                               scale=rs[t][:, 0:1], bias=rs[t][:, 1:2]),
                     reads=[f'xt{xi}', f'rs{t}a', f'rs{t}b'], writes=[f'xn{t}'])
                def ftr(e, t=t):
                    inst = None
                    for j in range(8):
                        inst = e.transpose(out=pT[:, j // 4, j % 4, t, :], in_=xn[t][:, j * 128:(j + 1) * 128],
                                           identity=ident[:])
                    return inst
                P.op('pe', ftr, reads=[f'xn{t}', 'ident'], writes=['pTa', 'pTb'])
            for j in range(8):
                src = pT[:, j // 4, j % 4, :, :].rearrange("p a b -> p (a b)")
                dst = hT[gp][:, j, :]
                if j < 4:
                    P.op('act', lambda e, src=src, dst=dst, j=j: e.activation(out=dst, in_=src, func=AF.Identity,
                                                                              scale=sc1p[:, j:j + 1],
                                                                              bias=modT[:, j:j + 1]),
                         reads=['pTa', 'sc1p', 'modT0'], writes=[f'hT{gp}_{j}'])
                else:
                    P.op('dve', lambda e, src=src, dst=dst, j=j: e.tensor_scalar(out=dst, in0=src,
                                                                                 scalar1=sc1p[:, j:j + 1],
                                                                                 scalar2=modT[:, j:j + 1],
                                                                                 op0=ALU.mult, op1=ALU.add),
                         reads=['pTb', 'sc1p', 'modT0'], writes=[f'hT{gp}_{j}'])
            hnames = [f'hT{gp}_{j}' for j in range(8)]
            for t in range(2):
                c = 2 * g + t
                def ftm(e, t=t, gp=gp):
                    inst = None
                    for j in range(8):
                        inst = e.matmul(pTM[t][:, 0:388], lhsT=hT[gp][:, j, t * 128:(t + 1) * 128],
                                        rhs=w_sb[:, j, 0:388], start=(j == 0), stop=(j == 7))
                    return inst
                P.op('pe', ftm, reads=hnames + ['w_sb'], writes=['pTM'])
                P.op('act', lambda e, t=t, c=c: e.activation(out=vp[:, c, 0:128], in_=pTM[t][:, 128:256], func=AF.Copy),
                     reads=['pTM'], writes=[f'vp{c}'])
                P.op('act', lambda e, t=t: e.activation(out=so_t[t][:], in_=pTM[t][:, 256:384], func=AF.Sigmoid),
                     reads=['pTM'], writes=[f'so{t}'])
                P.op('dve', lambda e, t=t, c=c: e.tensor_copy(out=qk_tm[:, c, :], in_=pTM[t][:, 0:128]),
                     reads=['pTM'], writes=[f'qk{c}'])
                P.op('dve', lambda e, t=t, c=c: e.tensor_tensor(out=gt[:, c, :], in0=pTM[t][:, 384:388], in1=bg[:],
                                                                op=ALU.add),
                     reads=['pTM', 'bg'], writes=[f'gt{c}'])
                P.op('pool', lambda e, t=t, c=c: e.dma_start(out=so_d[c * 128:(c + 1) * 128, :], in_=so_t[t][:]),
                     reads=[f'so{t}'], writes=[f'so_d{c}'], dsem=f'D_so{t}')
                if t == 0:
                    def ffz(e, gp=gp):
                        inst = None
                        for j in range(8):
                            inst = e.matmul(pFZ[:, 0:256], lhsT=w_sb[:, j, 388:516], rhs=hT[gp][:, j, :],
                                            start=(j == 0), stop=(j == 7))
                        return inst
                    P.op('pe', ffz, reads=hnames + ['w_sb'], writes=['pFZ'])
                    P.op('act', lambda e, g=g: e.activation(out=fzT[:, g * 256:(g + 1) * 256], in_=pFZ[:, 0:256],
                                                            func=AF.Copy),
                         reads=['pFZ'], writes=[f'fzT{g}'])
            gate_scalars(g)
        def hf_out(c, nd, par):
            P.op('act', lambda e: e.activation(out=hf_t[par][:], in_=nd[:, 0:128], func=AF.Copy, scale=dd[par][:, 1:2]),
                 reads=['pND', f'ddr{par}'], writes=[f'hf{par}'])
            P.op('pool', lambda e: e.dma_start(out=hf_d[c * 128:(c + 1) * 128, :], in_=hf_t[par][:]),
                 reads=[f'hf{par}'], writes=[f'hf_d{c}'], dsem=f'D_hf{par}')

        P.op('pool', lambda e: e.memset(Dst[:], 0.0), writes=['Dst'])
        P.op('pool', lambda e: e.memset(Cb[0][:], 0.0), writes=['Cb0'])
        P.replay(P.record(lambda: stageA(0)))
        for g in range(NG):
            c0 = 2 * g
            f0 = P.record(lambda: mlstm_chunk(c0, 0, 'front'))
            t0_ = P.record(lambda: (mlstm_chunk(c0, 0, 'tail'), hf_out(c0, pND[:, 0:129], c0 % 2)))
            f1 = P.record(lambda: mlstm_chunk(c0 + 1, 0, 'front'))
            t1_ = P.record(lambda: (mlstm_chunk(c0 + 1, 0, 'tail'), hf_out(c0 + 1, pND[:, 0:129], (c0 + 1) % 2)))
            nxt = P.record(lambda: stageA(g + 1)) if g + 1 < NG else []
            n3 = len(nxt) // 3
            P.replay(f0, nxt[:n3])
            P.replay(t0_, f1, nxt[n3:2 * n3])
            P.replay(t1_, nxt[2 * n3:])
        if stop_after == 'P1':
            return finish()
        hs = [sb(f"hs{i}", [128, 128], F32) for i in range(2)]
        hn = [sb(f"hn{i}", [128, 128], F32) for i in range(2)]
        ym = [sb(f"ym{i}", [128, 128], BF16) for i in range(2)]
        sol = [sb(f"sol{i}", [128, 128], F32) for i in range(2)]
        hfl = [sb(f"hfl{i}", [128, 128], F32) for i in range(2)]
        bs2 = [sb(f"bs2{i}", [128, 6], F32) for i in range(2)]
        mv2 = [sb(f"mv2{i}", [128, 2], F32) for i in range(2)]
        rs2 = [sb(f"rs2{i}", [128, 2], F32) for i in range(2)]

        P.op('pool', lambda e: e.memset(Dst[:], 0.0), reads=[], writes=['Dst'])
        cb_first = (N1 - 1) % 2
        P.op('pool', lambda e: e.memset(Cb[cb_first][:], 0.0), writes=[f'Cb{cb_first}'])
        mt_m = dbg[:, 0:128] if not fused else mixg_m.ap()
        mt_f = dbg[:, 128:256] if not fused else mixg_f.ap()
        def s2_front(c):
            par = c % 2
            ld('sp', sol[par][:], so_d[c * 128:(c + 1) * 128, :], f'sol{par}', f'D_sol{par}', reads=[f'so_d{c}'])
            ld('sp', hfl[par][:], hf_d[c * 128:(c + 1) * 128, :], f'hfl{par}', f'D_hfl{par}', reads=[f'hf_d{c}'])
            P.op('pool', lambda e, par=par: e.tensor_tensor(out=sol[par][:], in0=sol[par][:], in1=nwb[:], op=ALU.mult),
                 reads=[f'sol{par}', 'nwb'], writes=[f'sol{par}'])
            mlstm_chunk(c, 1, 'front')

        def s2_tail(c):
            par = c % 2
            nd, _ = mlstm_chunk(c, 1, 'tail')
            P.op('dve', lambda e, nd=nd, par=par: e.scalar_tensor_tensor(out=hs[par][:], in0=nd[:, 0:128],
                                                                         scalar=dd[par][:, 1:2], in1=hfl[par][:],
                                                                         op0=ALU.mult, op1=ALU.add),
                 reads=['pND', f'ddr{par}', f'hfl{par}'], writes=[f'hs{par}'])
            P.op('dve', lambda e, par=par: e.bn_stats(out=bs2[par][:], in_=hs[par][:]),
                 reads=[f'hs{par}'], writes=[f'bs2{par}'])
            P.op('dve', lambda e, par=par: e.bn_aggr(out=mv2[par][:], in_=bs2[par][:]),
                 reads=[f'bs2{par}'], writes=[f'mv2{par}'])
            P.op('act', lambda e, par=par: e.activation(out=rs2[par][:, 0:1], in_=mv2[par][:, 1:2], func=AF.Sqrt,
                                                        bias=epsT[:, 0:1]),
                 reads=[f'mv2{par}', 'epsT'], writes=[f'rs2{par}a'])
            P.op('dve', lambda e, par=par: e.reciprocal(out=rs2[par][:, 0:1], in_=rs2[par][:, 0:1]),
                 reads=[f'rs2{par}a'], writes=[f'rs2{par}a'])
            P.op('dve', lambda e, par=par: e.tensor_scalar(out=rs2[par][:, 1:2], in0=mv2[par][:, 0:1],
                                                           scalar1=rs2[par][:, 0:1], scalar2=-1.0, op0=ALU.mult,
                                                           op1=ALU.mult),
                 reads=[f'rs2{par}a', f'mv2{par}'], writes=[f'rs2{par}b'])
            P.op('act', lambda e, par=par: e.activation(out=hn[par][:], in_=hs[par][:], func=AF.Identity,
                                                        scale=rs2[par][:, 0:1], bias=rs2[par][:, 1:2]),
                 reads=[f'hs{par}', f'rs2{par}a', f'rs2{par}b'], writes=[f'hn{par}'])
            P.op('pool', lambda e, par=par: e.tensor_tensor(out=ym[par][:], in0=hn[par][:], in1=sol[par][:],
                                                            op=ALU.mult),
                 reads=[f'hn{par}', f'sol{par}'], writes=[f'ym{par}'])
            P.op('pool', lambda e, par=par, c=c: e.dma_start(out=mt_m[c * 128:(c + 1) * 128, :],
                                                             in_=ym[par][:]),
                 reads=[f'ym{par}'], writes=[f'mixg_m_c{c}'], dsem=f'D_ym{par}')
            if fused and c % CPP == 0:
                i_ = c // CPP
                P.op('pool', lambda e, i_=i_: e.collective_compute(
                    "AllGather", ALU.bypass, replica_groups=[[0, 1, 2, 3], [4, 5, 6, 7]],
                    ins=[mixg_m.ap()[i_ * R_:(i_ + 1) * R_, :].opt()],
                    outs=[gath_m.ap()[i_ * 4 * R_:(i_ + 1) * 4 * R_, :].opt()]),
                    reads=[f'mixg_m_c{cc}' for cc in range(c, c + CPP)], writes=['gath_m'], dsem='CC', inc=1)

        P.replay(P.record(lambda: s2_front(N1 - 1)))
        for c in range(N1 - 1, -1, -1):
            tl = P.record(lambda: s2_tail(c))
            fr = P.record(lambda: s2_front(c - 1)) if c > 0 else []
            P.replay(tl, fr)

        if stop_after == 'P2':
            return finish()
        flush()
        cur.pop(); sA.__exit__(None, None, None)
        sF = contextlib.ExitStack(); sF.__enter__(); cur.append(sF)
        cs_b = sb("cs_b", [128, 256], BF16)
        a1_b = sb("a1_b", [N1, 2 * N1], BF16)
        a2_b = sb("a2_b", [N1, 2 * N1], BF16)
        cn_b = sb("cn_b", [128, 256], BF16)
        tw = sb("tw", [128, 4 * N1], F32)
        ld('pool', cs_b[:], c_cs, 'cs_b', 'D_f0')
        ld('pool', a1_b[:], c_a1, 'a1_b', 'D_f1')
        ld('pool', a2_b[:], c_a2, 'a2_b', 'D_f2')
        ld('pool', cn_b[:], c_cn, 'cn_b', 'D_f3')
        ld('sp', tw[:], c_tw, 'tw', 'D_f4')
        G = sb("G", [128, 128, 256], BF16)
        Yr = sb("Yr", [128, N1, 128], BF16)
        Qp = sb("Qp", [128, N1, 128], BF16)
        yf = G[:].rearrange("p a b -> p (a b)")[:, 0:N1 * 128].rearrange("p (a b) -> p a b", b=128)
        t1 = [sb(f"t1_{i}", [128, 2 * N1], F32) for i in range(2)]
        t2 = [sb(f"t2_{i}", [128, 2 * N1], F32) for i in range(2)]
        pG = [pFZ, pTMb, pS, pKV]
        pGn = ['pFZ', 'pTM', 'pS', 'pKV']
        fz_all = [f'fzT{g}' for g in range(NG)]
        for i in range(64):
            k = i % 4
            eng = 'act' if k < 2 else 'dve'
            def f0(e, i=i, k=k):
                inst = None
                for q in range(2):
                    n2 = 2 * i + q
                    inst = e.matmul(pG[k][0:N1, q * 256:(q + 1) * 256], lhsT=fzT[:, n2:S:128], rhs=cs_b[:],
                                    start=True, stop=True)
                return inst
            P.op('pe', f0, reads=fz_all + ['cs_b'], writes=[pGn[k]])
            dst = G[0:N1, 2 * i:2 * i + 2, :].rearrange("p a b -> p (a b)")
            if eng == 'act':
                P.op('act', lambda e, k=k, dst=dst: e.activation(out=dst, in_=pG[k][0:N1, :], func=AF.Copy),
                     reads=[pGn[k]], writes=[f'G{i}'])
            else:
                P.op('dve', lambda e, k=k, dst=dst: e.tensor_copy(out=dst, in_=pG[k][0:N1, :]),
                     reads=[pGn[k]], writes=[f'G{i}'])
        g_all = [f'G{i}' for i in range(64)]
        pY = [pFZ, pTMb]
        pYn = ['pFZ', 'pTM']
        for j in range(128):
            k = j % 2
            def fa(e, j=j, k=k):
                e.matmul(pY[k][:, 0:2 * N1], lhsT=G[0:N1, :, j], rhs=a1_b[:], start=True, stop=False)
                return e.matmul(pY[k][:, 0:2 * N1], lhsT=G[0:N1, :, 128 + j], rhs=a2_b[:], start=False, stop=True)
            P.op('pe', fa, reads=g_all + ['a1_b', 'a2_b'], writes=[pYn[k]])
            P.op('dve', lambda e, k=k: e.tensor_tensor(out=t1[k][:], in0=pY[k][:, 0:2 * N1], in1=tw[:, 0:2 * N1],
                                                       op=ALU.mult),
                 reads=[pYn[k], 'tw'], writes=[f't1_{k}'])
            P.op('dve', lambda e, k=k: e.tensor_tensor(out=t2[k][:], in0=pY[k][:, 0:2 * N1],
                                                       in1=tw[:, 2 * N1:4 * N1], op=ALU.mult),
                 reads=[pYn[k], 'tw'], writes=[f't2_{k}'])
            P.op('pool', lambda e, k=k, j=j: e.tensor_tensor(out=Yr[:, :, j], in0=t1[k][:, 0:N1],
                                                             in1=t2[k][:, N1:2 * N1], op=ALU.subtract),
                 reads=[f't1_{k}', f't2_{k}'], writes=[f'Yr{j}'])
            P.op('pool', lambda e, k=k, j=j: e.tensor_tensor(out=Qp[:, :, j], in0=t1[k][:, N1:2 * N1],
                                                             in1=t2[k][:, 0:N1], op=ALU.add),
                 reads=[f't1_{k}', f't2_{k}'], writes=[f'Qp{j}'])
        y_all = [f'Yr{j}' for j in range(128)] + [f'Qp{j}' for j in range(128)]
        NB = N1 // 4
        pX = [pS, pKV]
        pXn = ['pS', 'pKV']
        for bi in range(NB):
            k = bi % 2
            def fc(e, bi=bi, k=k):
                e.matmul(pX[k][:, :], lhsT=cn_b[:, 0:128], rhs=Yr[:, 4 * bi:4 * bi + 4, :].rearrange("p a b -> p (a b)"),
                         start=True, stop=False)
                return e.matmul(pX[k][:, :], lhsT=cn_b[:, 128:256],
                                rhs=Qp[:, 4 * bi:4 * bi + 4, :].rearrange("p a b -> p (a b)"), start=False, stop=True)
            P.op('pe', fc, reads=y_all + ['cn_b'], writes=[pXn[k]])
            P.op('act', lambda e, bi=bi, k=k: e.activation(
                out=yf[:, 4 * bi:4 * bi + 4, :].rearrange("p a b -> p (a b)"), in_=pX[k][:, :], func=AF.Copy),
                reads=[pXn[k]], writes=[f'yf{bi}'])
        mt3 = mt_f.rearrange("(a b) j -> a b j", b=N1)
        npc = 4 if N1 >= 4 else 1
        for pc in range(npc):
            lo, hi = pc * N1 // npc, (pc + 1) * N1 // npc
            P.op('pool', lambda e, lo=lo, hi=hi: e.dma_start(out=mt3[:, lo:hi, :], in_=yf[:, lo:hi, :]),
                 reads=[f'yf{bi}' for bi in range(NB)], writes=['mixg_f'], dsem='D_yf')
        flush()
        cur.pop(); sF.__exit__(None, None, None)
        cur.pop(); sAF.__exit__(None, None, None)
        cur.pop(); sGA.__exit__(None, None, None)
        psA.__exit__(None, None, None)
        if not fused:
            return nc
        ctx = dict(nc=nc, P=P, es=es, cur=cur, flush=flush, ld=ld, sb=sb, xq=xq, w_out=w_out, ln1_g=ln1_g, ln1_b=ln1_b,
                   w_ff1=w_ff1, b_ff1T=b_ff1T, w_ff2=w_ff2, b_ff2=b_ff2, ln2_g=ln2_g, ln2_b=ln2_b, gidx=gidx, out=out,
                   gath_m=gath_m, gath_f=gath_f, mixg_f=mixg_f, ident=ident, ones=ones, epsT=epsT, modT=modT, sc2p=sc2p, g1row=g1row, g2row=g2row)
        build_B(N1, ctx)
        return nc


def build_B(N1, ctx=None):
    S = 128 * N1
    TQ = S // 4
    NT = TQ // 128
    NGB = NT // 2
    fusedB = ctx is not None
    if not fusedB:
        nc = bass.Bass("TRN2", target_bir_lowering=False)
        P = Prog()

        def din(name, shape, dt=F32):
            return nc.dram_tensor(name, shape, dt, kind="ExternalInput").ap()
        xq = din("xq", [TQ, D])
        mixq = din("mixq", [TQ, D], BF16)
        cT = din("cT", [128, 8])
        w_ada = din("w_ada", [D, 6 * D])
        b_adaT = din("b_adaT", [128, 48])
        b_ada_row = din("b_ada_row", [1, 6 * D])
        w_out = din("w_out", [D, D])
        ln1_g = din("ln1_g", [1, D]); ln1_b = din("ln1_b", [1, D])
        w_ff1 = din("w_ff1", [D, DFF]); b_ff1T = din("b_ff1T", [128, 32])
        w_ff2 = din("w_ff2", [DFF, D]); b_ff2 = din("b_ff2", [1, D])
        ln2_g = din("ln2_g", [1, D]); ln2_b = din("ln2_b", [1, D])
        c_ident = din("c_ident", [128, 128])
        out = nc.dram_tensor("out", [TQ, D], F32, kind="ExternalOutput").ap()
        es = contextlib.ExitStack()
        es.__enter__()
        cur = [es]
        sems = {}

        def sb(name, shape, dt=F32):
            return cur[-1].enter_context(nc.sbuf_tensor(name, shape, dt))

        def flush():
            P.barrier_all('sp')
            for sname in P.semnames:
                if sname not in sems:
                    sems[sname] = es.enter_context(nc.semaphore(sname))
            with nc.Block() as block:
                P.emit(nc, sems, block)
            for e_ in P.ENG:
                P.streams[e_] = []
            for e_ in P.ENG:
                P.barrier_all(e_)

        def ld(eng, dst, src, name, sem, reads=()):
            P.op(eng, lambda e: e.dma_start(out=dst, in_=src), reads=reads, writes=[name], dsem=sem)
    else:
        nc = ctx['nc']; P = ctx['P']; es = ctx['es']; cur = ctx['cur']; flush = ctx['flush']; ld = ctx['ld']; sb = ctx['sb']
        xq = ctx['xq']; w_out = ctx['w_out']; ln1_g = ctx['ln1_g']; ln1_b = ctx['ln1_b']; w_ff1 = ctx['w_ff1']
        b_ff1T = ctx['b_ff1T']; w_ff2 = ctx['w_ff2']; b_ff2 = ctx['b_ff2']; ln2_g = ctx['ln2_g']; ln2_b = ctx['ln2_b']
        gidx = ctx['gidx']; out = ctx['out']; gath_m = ctx['gath_m']; gath_f = ctx['gath_f']; mixg_f = ctx['mixg_f']
    psB = contextlib.ExitStack()
    psB.__enter__()

    def ps(name, shape, dt=F32):
        return psB.enter_context(nc.psum_tensor(name, shape, dt))
    if True:
        if fusedB:
            ident = ctx['ident']; ones = ctx['ones']; epsT = ctx['epsT']; modT = ctx['modT']; sc2p = ctx['sc2p']
            g1row = ctx['g1row']; g2row = ctx['g2row']
        else:
            ident = sb("ident", [128, 128], BF16)
            ones = sb("ones", [128, 128], F32)
            cact = sb("cact", [128, 8], F32)
            modT = sb("modT", [128, 48], F32)
            badaT = sb("badaT", [128, 48], F32)
            sc2p = sb("sc2p", [128, 8], F32)
            epsT = sb("epsT", [128, 1], F32)
            g1row = nc.dram_tensor("g1row", [1, 1024], F32).ap()
            g2row = nc.dram_tensor("g2row", [1, 1024], F32).ap()
        ones_b = sb("ones_b", [1, 128], BF16)
        b1T = sb("b1T", [128, 32], F32)
        bff2_b = sb("bff2_b", [1, 1024], BF16)
        l1g = sb("l1g", [128, 1024], F32); l1b = sb("l1b", [128, 1024], F32)
        l2g = sb("l2g", [128, 1024], F32); l2b = sb("l2b", [128, 1024], F32)
        pb_ = [ps(f"pb{i}", [128, 512], F32) for i in range(7)]
        pTr = ps("pTr", [128, 8, 128], BF16)
        P.psum_names |= {f'pb{i}' for i in range(7)} | {'pTr'}
        ld('sp', b1T[:], b_ff1T, 'b1T', 'D_b1')
        ld('pool', bff2_b[:], b_ff2, 'bff2_b', 'D_b2')
        ld('sp', l1g[:], ln1_g.broadcast_to([128, 1024]), 'l1g', 'D_l1g')
        ld('sp', l1b[:], ln1_b.broadcast_to([128, 1024]), 'l1b', 'D_l1b')
        ld('sp', l2g[:], ln2_g.broadcast_to([128, 1024]), 'l2g', 'D_l2g')
        ld('sp', l2b[:], ln2_b.broadcast_to([128, 1024]), 'l2b', 'D_l2b')
        P.op('pool', lambda e: e.memset(ones_b[:], 1.0), writes=['ones_b'])
        if fusedB:
            gix = sb("gix", [128, 4 * NT], mybir.dt.int32)
            ld('sp', gix[:], gidx, 'gix', 'D_gix')
        if not fusedB:
            ld('pool', ident[:], c_ident, 'ident', 'D_c0')
            ld('sp', cact[:], cT, 'cact', 'D_c7')
            ld('sp', badaT[:], b_adaT, 'badaT', 'D_c8')
            P.op('pool', lambda e: e.memset(ones[:], 1.0), writes=['ones'])
            P.op('pool', lambda e: e.memset(epsT[:], LN_EPS), writes=['epsT'])
            P.op('act', lambda e: e.activation(out=cact[:], in_=cact[:], func=AF.Silu), reads=['cact'], writes=['cact'])
            s0 = contextlib.ExitStack(); s0.__enter__(); cur.append(s0)
            wa = [sb(f"wa{i}", [128, 8, 1024], F32) for i in range(2)]
            crep = sb("crep", [128, 8, 128], F32)
            g1b = sb("g1b", [128, 1024], F32)
            g2b = sb("g2b", [128, 1024], F32)

            def mk_crep(e):
                i = None
                for j in range(8):
                    i = e.tensor_scalar(out=crep[:, j, :], in0=ones[:], scalar1=cact[:, j:j + 1], scalar2=None,
                                        op0=ALU.mult)
                return i
            P.op('dve', mk_crep, reads=['cact', 'ones'], writes=['crep'])

            def load_wa(i, part):
                ld('sp', wa[i][:], w_ada[:, part * 1024:(part + 1) * 1024].rearrange("(j p) n -> p j n", p=128),
                   f'wa{i}', f'D_wa{i}')

            def mod_featmajor(i, part):
                def f(e):
                    inst = None
                    for m in range(8):
                        for j in range(8):
                            inst = e.matmul(pb_[0][:, m:m + 1], lhsT=wa[i][:, j, m * 128:(m + 1) * 128],
                                            rhs=cact[:, j:j + 1], start=(j == 0), stop=(j == 7))
                    return inst
                P.op('pe', f, reads=[f'wa{i}', 'cact'], writes=['pb0'])
                P.op('dve', lambda e: e.tensor_tensor(out=modT[:, part * 8:(part + 1) * 8], in0=pb_[0][:, 0:8],
                                                      in1=badaT[:, part * 8:(part + 1) * 8], op=ALU.add),
                     reads=['pb0', 'badaT'], writes=[f'modT{part}'])

            def mod_rowbcast(i, part, dst, dname):
                ld('sp', dst[:], b_ada_row[:, part * 1024:(part + 1) * 1024].broadcast_to([128, 1024]), dname,
                   f'D_{dname}')
                for h in range(2):
                    def f(e, h=h):
                        inst = None
                        for j in range(8):
                            inst = e.matmul(pb_[1][:, :], lhsT=crep[:, j, :], rhs=wa[i][:, j, h * 512:(h + 1) * 512],
                                            start=(j == 0), stop=(j == 7))
                        return inst
                    P.op('pe', f, reads=[f'wa{i}', 'crep'], writes=['pb1'])
                    P.op('dve', lambda e, h=h: e.scalar_tensor_tensor(
                        out=dst[:, h * 512:(h + 1) * 512], in0=dst[:, h * 512:(h + 1) * 512], scalar=1.0,
                        in1=pb_[1][:, :], op0=ALU.add, op1=ALU.add), reads=['pb1', dname], writes=[dname])

            load_wa(0, 2); load_wa(1, 3)
            mod_rowbcast(0, 2, g1b, 'g1b')
            mod_featmajor(1, 3)
            load_wa(0, 4); load_wa(1, 5)
            mod_featmajor(0, 4)
            P.op('dve', lambda e: e.tensor_scalar(out=sc2p[:], in0=modT[:, 32:40], scalar1=1.0, scalar2=None,
                                                  op0=ALU.add), reads=['modT4'], writes=['sc2p'])
            mod_rowbcast(1, 5, g2b, 'g2b')
            P.op('sp', lambda e: e.dma_start(out=g1row, in_=g1b[0:1, :]), reads=['g1b'], writes=['g1row'], dsem='D_g1r')
            P.op('sp', lambda e: e.dma_start(out=g2row, in_=g2b[0:1, :]), reads=['g2b'], writes=['g2row'], dsem='D_g2r')
            flush()
            cur.pop(); s0.__exit__(None, None, None)

        wout = sb("wout", [128, 8, 1024], BF16)
        wff1 = sb("wff1", [128, 8, 4096], BF16)
        wff2 = sb("wff2", [128, 32, 1024], BF16)
        if fusedB:
            NP_ = max(1, S // 2048)
            R_ = S // NP_
            for i_ in range(NP_):
                P.op('pool', lambda e, i_=i_: e.collective_compute(
                    "AllGather", ALU.bypass, replica_groups=[[0, 1, 2, 3], [4, 5, 6, 7]],
                    ins=[mixg_f.ap()[i_ * R_:(i_ + 1) * R_, :].opt()],
                    outs=[gath_f.ap()[i_ * 4 * R_:(i_ + 1) * 4 * R_, :].opt()]),
                    reads=['mixg_f'], writes=['gath_f'], dsem='CC', inc=1)
        sT = contextlib.ExitStack(); sT.__enter__(); cur.append(sT)
        tg1 = sb("tg1", [128, 1024], F32)
        tg2 = sb("tg2", [128, 1024], F32)
        stg = [sb(f"stg{i}", [128, 2048], F32) for i in range(3)]
        ld('sp', tg1[:], g1row.broadcast_to([128, 1024]), 'tg1', 'D_tg1', reads=['g1row'])
        ld('sp', tg2[:], g2row.broadcast_to([128, 1024]), 'tg2', 'D_tg2', reads=['g2row'])
        si = [0]

        def stage(src_ap, shape3=None):
            k = si[0] % 3
            si[0] += 1
            dst = stg[k][:] if shape3 is None else stg[k][:].rearrange("p (a b) -> p a b", a=shape3)
            ld('sp', dst, src_ap, f'stg{k}', f'D_stg{k}')
            return k
        for jj in range(4):
            k = stage(w_out[jj * 256:(jj + 1) * 256, :].rearrange("(a p) n -> p a n", p=128), 2)
            for a in range(2):
                P.op('dve', lambda e, k=k, a=a, jj=jj: e.tensor_tensor(out=wout[:, 2 * jj + a, :],
                                                                     in0=stg[k][:, a * 1024:(a + 1) * 1024],
                                                                     in1=tg1[:], op=ALU.mult),
                     reads=[f'stg{k}', 'tg1'], writes=['wout'])
        for j in range(8):
            for hh in range(2):
                k = stage(w_ff1[j * 128:(j + 1) * 128, hh * 2048:(hh + 1) * 2048])
                P.op('act', lambda e, k=k, j=j, hh=hh: e.activation(out=wff1[:, j, hh * 2048:(hh + 1) * 2048],
                                                                  in_=stg[k][:], func=AF.Copy),
                     reads=[f'stg{k}'], writes=['wff1'])
        for ff in range(16):
            k = stage(w_ff2[ff * 256:(ff + 1) * 256, :].rearrange("(a p) n -> p a n", p=128), 2)
            for a in range(2):
                f_ = 2 * ff + a
                eng_ = 'dve' if a == 0 else 'pool'
                P.op(eng_, lambda e, k=k, a=a, f_=f_: e.tensor_tensor(out=wff2[:, f_, :],
                                                                     in0=stg[k][:, a * 1024:(a + 1) * 1024],
                                                                     in1=tg2[:], op=ALU.mult),
                     reads=[f'stg{k}', 'tg2'], writes=[f'wff2_{f_}'])
        P.op('dve', lambda e: e.tensor_tensor(out=bff2_b[:], in0=bff2_b[:], in1=tg2[0:1, :], op=ALU.mult),
             reads=['bff2_b', 'tg2'], writes=['bff2_b'])
        w2n = [f'wff2_{f_}' for f_ in range(32)]
        flush()
        cur.pop(); sT.__exit__(None, None, None)

        xt = sb("xt", [128, 1024], F32)
        mx = sb("mx", [128, 1024], BF16)
        x1 = sb("x1", [128, 4, 1024], F32)
        xn = sb("xn", [128, 1024], BF16)
        mixT = xn[:].rearrange("p (a b) -> p a b", b=128)
        h2T = [sb(f"h2TB{i}", [128, 8, 256], BF16) for i in range(2)]
        uT = sb("uT", [128, 8, 256], BF16)
        rt = [sb(f"rtB{i}", [128, 256], F32) for i in range(2)]
        bst = [sb(f"bstB{i}", [128, 2, 6], F32) for i in range(2)]
        mv = [sb(f"mvB{i}", [128, 2], F32) for i in range(2)]
        rs = [sb(f"rsB{i}", [128, 2], F32) for i in range(2)]

        def ln_stats(src, sname, k):
            def f(e):
                e.bn_stats(out=bst[k][:, 0, :], in_=src[:, 0:512])
                return e.bn_stats(out=bst[k][:, 1, :], in_=src[:, 512:1024])
            P.op('dve', f, reads=[sname], writes=[f'bst{k}'])
            yield
            P.op('dve', lambda e: e.bn_aggr(out=mv[k][:], in_=bst[k][:].rearrange("p a b -> p (a b)")),
                 reads=[f'bst{k}'], writes=[f'mv{k}'])
            yield
            P.op('act', lambda e: e.activation(out=rs[k][:, 0:1], in_=mv[k][:, 1:2], func=AF.Sqrt, bias=epsT[:, 0:1]),
                 reads=[f'mv{k}', 'epsT'], writes=[f'rsa{k}'])
            yield
            P.op('dve', lambda e: e.reciprocal(out=rs[k][:, 0:1], in_=rs[k][:, 0:1]), reads=[f'rsa{k}'],
                 writes=[f'rsa{k}'])
            yield
            P.op('dve', lambda e: e.tensor_scalar(out=rs[k][:, 1:2], in0=mv[k][:, 0:1], scalar1=rs[k][:, 0:1],
                                                  scalar2=-1.0, op0=ALU.mult, op1=ALU.mult),
                 reads=[f'rsa{k}', f'mv{k}'], writes=[f'rsb{k}'])
            yield

        def prologue(g):
            hp = g % 2
            for t in range(2):
                r0 = (2 * g + t) * 128
                xi = (2 * g + t) % 4
                x1t = x1[:, xi, :]
                xname = f'x1_{xi}'
                ld('sp', xt[:], xq[r0:r0 + 128, :], 'xt', 'D_x')
                if not fusedB:
                    ld('sp', mx[:], mixq[r0:r0 + 128, :], 'mx', 'D_m')
                else:
                    T_ = 2 * g + t
                    for s_ in range(8):
                        gsrc = gath_m if s_ < 4 else gath_f
                        sr = s_ % 4
                        P.op('pool', lambda e, s_=s_, sr=sr, T_=T_, gsrc=gsrc: e.indirect_dma_start(
                            out=mx[:, s_ * 128:(s_ + 1) * 128], out_offset=None, in_=gsrc.ap()[:, :],
                            in_offset=bass.IndirectOffsetOnAxis(ap=gix[:, sr * NT + T_:sr * NT + T_ + 1], axis=0)),
                            reads=['gath_m', 'gath_f', 'gix'], writes=['mx'], dsem='D_m')
                yield

                def ftr(e):
                    inst = None
                    for j in range(8):
                        inst = e.transpose(out=pTr[:, j, :], in_=mx[:, j * 128:(j + 1) * 128], identity=ident[:])
                    return inst
                P.op('pe', ftr, reads=['mx', 'ident'], writes=['pTr'])
                yield
                P.op('act', lambda e: e.activation(out=xn[:], in_=pTr[:].rearrange("p a b -> p (a b)"), func=AF.Copy),
                     reads=['pTr'], writes=['xn'])
                yield
                for hf in range(2):
                    def fo(e, hf=hf):
                        inst = None
                        for j in range(8):
                            inst = e.matmul(pb_[0][:, :], lhsT=mixT[:, j, :], rhs=wout[:, j, hf * 512:(hf + 1) * 512],
                                            start=(j == 0), stop=(j == 7))
                        return inst
                    P.op('pe', fo, reads=['xn', 'wout'], writes=['pb0'])
                    yield
                    P.op('dve', lambda e, hf=hf: e.scalar_tensor_tensor(
                        out=xt[:, hf * 512:(hf + 1) * 512], in0=xt[:, hf * 512:(hf + 1) * 512], scalar=ALPHA,
                        in1=pb_[0][:, :], op0=ALU.mult, op1=ALU.add), reads=['pb0', 'xt'], writes=['xt'])
                    yield
                yield from ln_stats(xt, 'xt', 0)
                P.op('act', lambda e, x1t=x1t: e.activation(out=x1t, in_=xt[:], func=AF.Identity, scale=rs[0][:, 0:1],
                                                            bias=rs[0][:, 1:2]),
                     reads=['xt', 'rsa0', 'rsb0'], writes=[xname])
                yield
                P.op('pool', lambda e, x1t=x1t: e.tensor_tensor(out=x1t, in0=x1t, in1=l1g[:], op=ALU.mult),
                     reads=[xname, 'l1g'], writes=[xname])
                yield
                P.op('pool', lambda e, x1t=x1t: e.tensor_tensor(out=x1t, in0=x1t, in1=l1b[:], op=ALU.add),
                     reads=[xname, 'l1b'], writes=[xname])
                yield
                yield from ln_stats(x1t, xname, 0)
                P.op('act', lambda e, x1t=x1t: e.activation(out=xn[:], in_=x1t, func=AF.Identity, scale=rs[0][:, 0:1],
                                                            bias=rs[0][:, 1:2]),
                     reads=[xname, 'rsa0', 'rsb0'], writes=['xn'])
                yield

                def ftr2(e):
                    inst = None
                    for j in range(8):
                        inst = e.transpose(out=pTr[:, j, :], in_=xn[:, j * 128:(j + 1) * 128], identity=ident[:])
                    return inst
                P.op('pe', ftr2, reads=['xn', 'ident'], writes=['pTr'])
                yield
                for j in range(8):
                    P.op('act', lambda e, j=j, t=t, hp=hp: e.activation(out=h2T[hp][:, j, t * 128:(t + 1) * 128],
                                                                       in_=pTr[:, j, :], func=AF.Identity,
                                                                       scale=sc2p[:, j:j + 1],
                                                                       bias=modT[:, 24 + j:25 + j]),
                         reads=['pTr', 'sc2p', 'modT3'], writes=[f'h2T{hp}_{j}'])
                    if j % 2 == 1:
                        yield

        def ffn(g):
            hp = g % 2
            hn = [f'h2T{hp}_{j}' for j in range(8)]
            for fh in range(4):
                for fc in range(8):
                    f_ = fh * 8 + fc
                    k = 1 + fc % 2

                    def f1(e, f_=f_, k=k):
                        inst = None
                        for j in range(8):
                            inst = e.matmul(pb_[k][:, 0:256], lhsT=wff1[:, j, f_ * 128:(f_ + 1) * 128],
                                            rhs=h2T[hp][:, j, :], start=(j == 0), stop=(j == 7))
                        return inst
                    P.op('pe', f1, reads=hn + ['wff1'], writes=[f'pb{k}'])
                    P.op('act', lambda e, f_=f_, k=k: e.activation(out=rt[k - 1][:], in_=pb_[k][:, 0:256], func=AF.Relu,
                                                                   bias=b1T[:, f_:f_ + 1]),
                         reads=[f'pb{k}', 'b1T'], writes=[f'rt{k}'])
                    sq_eng = 'dve' if fc % 2 == 0 else 'pool'
                    P.op(sq_eng, lambda e, fc=fc, k=k: e.tensor_tensor(out=uT[:, fc, :], in0=rt[k - 1][:],
                                                                      in1=rt[k - 1][:], op=ALU.mult),
                         reads=[f'rt{k}'], writes=[f'uT{fc}'])
                    yield
                un = [f'uT{fc}' for fc in range(8)]
                for t in range(2):
                    for ch in range(2):
                        bk = 3 + t * 2 + ch

                        def f2(e, t=t, ch=ch, bk=bk, fh=fh):
                            inst = None
                            for fc in range(8):
                                f_ = fh * 8 + fc
                                inst = e.matmul(pb_[bk][:, :], lhsT=uT[:, fc, t * 128:(t + 1) * 128],
                                                rhs=wff2[:, f_, ch * 512:(ch + 1) * 512],
                                                start=(f_ == 0), stop=False, skip_group_check=True)
                            if fh == 3:
                                inst = e.matmul(pb_[bk][:, :], lhsT=ones_b[0:1, :],
                                                rhs=bff2_b[0:1, ch * 512:(ch + 1) * 512],
                                                start=False, stop=True, skip_group_check=True)
                            return inst
                        P.op('pe', f2, reads=un + w2n + ['ones_b', 'bff2_b'], writes=[f'pb{bk}'])
                        yield

        def epilogue(g):
            for t in range(2):
                r0 = (2 * g + t) * 128
                xi = (2 * g + t) % 4
                x1t = x1[:, xi, :]
                xname = f'x1_{xi}'
                for ch in range(2):
                    bk = 3 + t * 2 + ch
                    P.op('dve', lambda e, ch=ch, bk=bk, xi=xi: e.scalar_tensor_tensor(
                        out=x1[:, xi, ch * 512:(ch + 1) * 512], in0=x1[:, xi, ch * 512:(ch + 1) * 512], scalar=ALPHA,
                        in1=pb_[bk][:, :], op0=ALU.mult, op1=ALU.add), reads=[f'pb{bk}', xname], writes=[xname])
                    yield
                yield from ln_stats(x1t, xname, 1)
                P.op('act', lambda e, x1t=x1t: e.activation(out=x1t, in_=x1t, func=AF.Identity, scale=rs[1][:, 0:1],
                                                            bias=rs[1][:, 1:2]),
                     reads=[xname, 'rsa1', 'rsb1'], writes=[xname])
                yield
                P.op('pool', lambda e, x1t=x1t: e.tensor_tensor(out=x1t, in0=x1t, in1=l2g[:], op=ALU.mult),
                     reads=[xname, 'l2g'], writes=[xname])
                yield
                P.op('pool', lambda e, x1t=x1t: e.tensor_tensor(out=x1t, in0=x1t, in1=l2b[:], op=ALU.add),
                     reads=[xname, 'l2b'], writes=[xname])
                yield
                P.op('sp', lambda e, r0=r0, x1t=x1t: e.dma_start(out=out[r0:r0 + 128, :], in_=x1t),
                     reads=[xname], writes=['out'], dsem=f'D_out{xi}')
                yield

        def interleave(*gens):
            gens = list(gens)
            while gens:
                for gen in list(gens):
                    try:
                        next(gen)
                    except StopIteration:
                        gens.remove(gen)

        interleave(prologue(0))
        for g in range(NGB):
            if g + 1 < NGB:
                interleave(ffn(g), prologue(g + 1))
            else:
                interleave(ffn(g))
            interleave(epilogue(g))
        flush()
    psB.__exit__(None, None, None)
    if not fusedB:
        es.__exit__(None, None, None)
    return nc


def _consts(N1):
    S = 128 * N1
    i = np.arange(128)
    ident = np.eye(128, dtype=np.float32)
    le = (i[:, None] <= i[None, :]).astype(np.float32)
    ge = (i[:, None] >= i[None, :]).astype(np.float32)
    ang = 2 * np.pi * np.outer(i, i) / 128.0
    c128, s128 = np.cos(ang), np.sin(ang)
    a = np.arange(N1)
    angA = 2 * np.pi * np.outer(a, a) / N1
    cA, sA = np.cos(angA), np.sin(angA)
    angT = 2 * np.pi * np.outer(i, a) / S
    tc, ts = np.cos(angT), np.sin(angT)
    nrm = 1.0 / np.sqrt(S * 128.0)
    return {
        "c_ident": ident, "c_maskf": 0.125 * le, "c_maskb": 0.125 * ge, "c_ule": le, "c_uge": ge,
        "c_cs": np.concatenate([c128, s128], 1).astype(np.float32),
        "c_a1": np.concatenate([cA, sA], 1).astype(np.float32),
        "c_a2": np.concatenate([-sA, cA], 1).astype(np.float32),
        "c_tw": np.concatenate([tc, tc, ts, ts], 1).astype(np.float32),
        "c_cn": (np.concatenate([c128, -s128], 1) * nrm).astype(np.float32),
    }


def make_in_maps(inp, N1, fused=False):
    S = 128 * N1
    TQ = S // 4
    cst = _consts(N1)
    f = lambda a: np.ascontiguousarray(np.asarray(a, dtype=np.float32))
    maps = []
    for core in range(NCORES):
        b, g = core // 4, core % 4
        cols = np.concatenate([
            np.arange(64) + 64 * g,
            256 + np.arange(64) + 64 * g,
            512 + np.arange(128) + 128 * g,
            1024 + np.arange(128) + 128 * g,
            2048 + np.array([g, 4 + g, 8 + g, 12 + g]),
            1536 + np.arange(128) + 128 * g,
        ])
        m = {
            "xb": f(inp["x"][b]),
            "xq": f(inp["x"][b, g * TQ:(g + 1) * TQ]),
            "cT": f(inp["c"][b].reshape(8, 128).T),
            "w_ada": f(inp["w_ada"][0]),
            "b_adaT": f(inp["b_ada"][0].reshape(48, 128).T),
            "b_ada_row": f(inp["b_ada"][0].reshape(1, -1)),
            "w_in": f(inp["w_in"][0][:, cols]),
            "b_gate": f(inp["b_gate"][0][[g, 4 + g, 8 + g, 12 + g]].reshape(1, 4)),
            "nw": f(inp["mlstm_norm_w"][0][128 * g:128 * (g + 1)].reshape(1, 128)),
            "w_out": f(inp["w_out"][0]),
            "ln1_g": f(inp["ln1_g"][0].reshape(1, -1)), "ln1_b": f(inp["ln1_b"][0].reshape(1, -1)),
            "w_ff1": f(inp["w_ff1"][0]), "b_ff1T": f(inp["b_ff1"][0].reshape(32, 128).T),
            "w_ff2": f(inp["w_ff2"][0]), "b_ff2": f(inp["b_ff2"][0].reshape(1, -1)),
            "ln2_g": f(inp["ln2_g"][0].reshape(1, -1)), "ln2_b": f(inp["ln2_b"][0].reshape(1, -1)),
        }
        if fused:
            NT = TQ // 128
            NP_ = max(1, S // 2048)
            R_ = S // NP_
            gi = np.empty((128, 4 * NT), np.int32)
            for s_ in range(4):
                for T_ in range(NT):
                    n_ = g * TQ + T_ * 128 + np.arange(128)
                    gi[:, s_ * NT + T_] = (n_ // R_) * 4 * R_ + s_ * R_ + n_ % R_
            m["gidx"] = gi
        m.update(cst)
        maps.append(m)
    return maps


def _bf16_to_mixq(results, N1):
    S = 128 * N1
    TQ = S // 4
    mixqs = []
    for core in range(NCORES):
        b, r = core // 4, core % 4
        parts_m = [results[4 * b + g]["dbg"][r * TQ:(r + 1) * TQ, 0:128] for g in range(4)]
        parts_f = [results[4 * b + g]["dbg"][r * TQ:(r + 1) * TQ, 128:256] for g in range(4)]
        mixqs.append(np.ascontiguousarray(np.concatenate(parts_m + parts_f, axis=1)))
    return mixqs


def kernel(**inputs):
    N1 = 128
    S = 128 * N1
    TQ = S // 4
    maps = make_in_maps(inputs, N1, fused=True)
    nc = build_nc(N1, fused=True)
    res = run_bass_kernel_spmd(nc, maps, core_ids=list(range(NCORES)))
    outp = np.empty((2, S, D), np.float32)
    for core in range(NCORES):
        b, g = core // 4, core % 4
        outp[b, g * TQ:(g + 1) * TQ] = np.asarray(res.results[core]["out"], dtype=np.float32)
    return outp
```

```python
import contextlib
import numpy as np
import concourse.bass as bass
import concourse.mybir as mybir
from concourse.bass_utils import run_bass_kernel_spmd

F32 = mybir.dt.float32
BF16 = mybir.dt.bfloat16
AF = mybir.ActivationFunctionType
ALU = mybir.AluOpType

D = 1024
DFF = 4096
LN_EPS = 1e-5
ALPHA = 2.0 ** 0.25
NCORES = 8


class Prog:
    ENG = ('pe', 'act', 'dve', 'pool', 'sp')

    def __init__(self):
        self.streams = {e: [] for e in self.ENG}
        self.cnt = {}
        self.lastw = {}
        self.rd = {}
        self.waited = {e: {} for e in self.ENG}
        self.semnames = ['S_' + e for e in self.ENG]
        self.psum_names = set()
        self.pacc = {}
        self._rec = None

    def record(self, body):
        assert self._rec is None
        self._rec = []
        body()
        r = self._rec
        self._rec = None
        return r

    def replay(self, *lists):
        lists = [l for l in lists if l]
        pos = [0] * len(lists)
        total = sum(len(l) for l in lists)
        for _ in range(total):
            best = None
            for i, l in enumerate(lists):
                if pos[i] < len(l):
                    frac = pos[i] / len(l)
                    if best is None or frac < best[0]:
                        best = (frac, i)
            i = best[1]
            self.op(*lists[i][pos[i]])
            pos[i] += 1

    def op(self, eng, fn, reads=(), writes=(), dsem=None, inc=None):
        if self._rec is not None:
            self._rec.append((eng, fn, tuple(reads), tuple(writes), dsem, inc))
            return None
        d = {}

        def add(t):
            if t is None:
                return
            s, v = t
            if d.get(s, 0) < v:
                d[s] = v
        for r in reads:
            add(self.lastw.get(r))
        for w in writes:
            add(self.lastw.get(w))
            for s, v in self.rd.get(w, {}).items():
                add((s, v))
        for n in list(reads) + list(writes):
            if n in self.psum_names:
                for e2, t2 in self.pacc.get(n, {}).items():
                    if e2 != eng:
                        add(t2)
        waits = []
        for s, v in d.items():
            if eng == 'pe' and s == 'S_pe':
                continue
            if self.waited[eng].get(s, 0) >= v:
                continue
            self.waited[eng][s] = v
            waits.append((s, v))
        if dsem is None:
            s = 'S_' + eng
            inc = 1
        else:
            s = dsem
            inc = 16 if inc is None else inc
            if s not in self.semnames:
                self.semnames.append(s)
        self.cnt[s] = self.cnt.get(s, 0) + inc
        t = (s, self.cnt[s])
        for r in reads:
            m = self.rd.setdefault(r, {})
            if m.get(s, 0) < t[1]:
                m[s] = t[1]
        for w in writes:
            self.lastw[w] = t
            self.rd[w] = {}
        for n in list(reads) + list(writes):
            if n in self.psum_names:
                self.pacc.setdefault(n, {})[eng] = t
        self.streams[eng].append((waits, fn, s, inc))
        return t

    def barrier_all(self, eng):
        waits = []
        for s, v in self.cnt.items():
            if self.waited[eng].get(s, 0) >= v:
                continue
            self.waited[eng][s] = v
            waits.append((s, v))
        self.streams[eng].append((waits, None, None, 0))

    def emit(self, nc, sems, block):
        def run(engname):
            def body(eng):
                for waits, fn, s, inc in self.streams[engname]:
                    for ws, wv in waits:
                        eng.wait_ge(sems[ws], wv)
                    if fn is None:
                        continue
                    inst = fn(eng)
                    inst.then_inc(sems[s], inc)
            return body
        block.tensor(run('pe'))
        block.scalar(run('act'))
        block.vector(run('dve'))
        block.gpsimd(run('pool'))
        block.sync(run('sp'))


def build_nc(N1, stop_after=None, fused=False):
    S = 128 * N1
    TQ = S // 4
    nc = bass.Bass("TRN2", target_bir_lowering=False)
    P = Prog()

    def din(name, shape, dt=F32):
        return nc.dram_tensor(name, shape, dt, kind="ExternalInput").ap()

    xb = din("xb", [S, D])
    cT = din("cT", [128, 8])
    w_ada = din("w_ada", [D, 6 * D])
    b_adaT = din("b_adaT", [128, 48])
    w_in = din("w_in", [D, 516])
    b_gate = din("b_gate", [1, 4])
    nw = din("nw", [1, 128])
    c_ident = din("c_ident", [128, 128])
    c_maskf = din("c_maskf", [128, 128])
    c_maskb = din("c_maskb", [128, 128])
    c_ule = din("c_ule", [128, 128])
    c_uge = din("c_uge", [128, 128])
    c_cs = din("c_cs", [128, 256])
    c_a1 = din("c_a1", [N1, 2 * N1])
    c_a2 = din("c_a2", [N1, 2 * N1])
    c_tw = din("c_tw", [128, 4 * N1])
    c_cn = din("c_cn", [128, 256])

    dbg = None
    if not fused:
        dbg = nc.dram_tensor("dbg", [S, 256], BF16, kind="ExternalOutput").ap()
    else:
        xq = din("xq", [TQ, D])
        b_ada_row = din("b_ada_row", [1, 6 * D])
        w_out = din("w_out", [D, D])
        ln1_g = din("ln1_g", [1, D]); ln1_b = din("ln1_b", [1, D])
        w_ff1 = din("w_ff1", [D, DFF]); b_ff1T = din("b_ff1T", [128, 32])
        w_ff2 = din("w_ff2", [DFF, D]); b_ff2 = din("b_ff2", [1, D])
        ln2_g = din("ln2_g", [1, D]); ln2_b = din("ln2_b", [1, D])
        gidx = din("gidx", [128, 4 * (TQ // 128)], mybir.dt.int32)
        out = nc.dram_tensor("out", [TQ, D], F32, kind="ExternalOutput").ap()

    so_d = nc.dram_tensor("so_d", [S, 128], F32).ap()
    hf_d = nc.dram_tensor("hf_d", [S, 128], F32).ap()
    mixg_m = nc.dram_tensor("mixg_m", [S, 128], BF16)
    mixg_f = nc.dram_tensor("mixg_f", [S, 128], BF16)
    gath_m = nc.dram_tensor("gath_m", [4 * S, 128], BF16)
    gath_f = nc.dram_tensor("gath_f", [4 * S, 128], BF16)
    NP_ = max(1, S // 2048)
    R_ = S // NP_
    CPP = R_ // 128

    import contextlib
    es = contextlib.ExitStack()
    with es:
        cur = [es]
        sems = {}

        def sb(name, shape, dt=F32):
            return cur[-1].enter_context(nc.sbuf_tensor(name, shape, dt))

        def flush():
            P.barrier_all('sp')
            for sname in P.semnames:
                if sname not in sems:
                    sems[sname] = es.enter_context(nc.semaphore(sname))
            with nc.Block() as block:
                P.emit(nc, sems, block)
            for e_ in P.ENG:
                P.streams[e_] = []
            for e_ in P.ENG:
                P.barrier_all(e_)

        psA = contextlib.ExitStack()
        psA.__enter__()

        def ps(name, shape, dt=F32):
            return psA.enter_context(nc.psum_tensor(name, shape, dt))

        ident = sb("ident", [128, 128], BF16)
        ones = sb("ones", [128, 128], F32)
        modT = sb("modT", [128, 48], F32)
        sc2p = sb("sc2p", [128, 8], F32)
        epsT = sb("epsT", [128, 1], F32)
        sGA = contextlib.ExitStack(); sGA.__enter__(); cur.append(sGA)
        maskf = sb("maskf", [128, 128], F32)
        maskb = sb("maskb", [128, 128], F32)
        ule = sb("ule", [128, 128], F32)
        uge = sb("uge", [128, 128], F32)
        bg = sb("bg", [128, 4], F32)
        nwb = sb("nwb", [128, 128], F32)
        cact = sb("cact", [128, 8], F32)
        cact_b = sb("cact_b", [128, 8], BF16)
        badaT = sb("badaT", [128, 48], F32)
        sc1p = sb("sc1p", [128, 8], F32)
        w_sb = sb("w_sb", [128, 8, 516], BF16)

        def ld(eng, dst, src, name, sem, reads=()):
            P.op(eng, lambda e: e.dma_start(out=dst, in_=src), reads=reads, writes=[name], dsem=sem)

        ld('pool', ident[:], c_ident, 'ident', 'D_c0')
        ld('sp', maskf[:], c_maskf, 'maskf', 'D_c1')
        ld('sp', maskb[:], c_maskb, 'maskb', 'D_c2')
        ld('sp', ule[:], c_ule, 'ule', 'D_c3')
        ld('sp', uge[:], c_uge, 'uge', 'D_c4')
        ld('sp', bg[:], b_gate.broadcast_to([128, 4]), 'bg', 'D_c5')
        ld('sp', nwb[:], nw.broadcast_to([128, 128]), 'nwb', 'D_c6')
        ld('sp', cact[:], cT, 'cact', 'D_c7')
        ld('sp', badaT[:], b_adaT, 'badaT', 'D_c8')
        ld('pool', w_sb[:], w_in.rearrange("(j p) n -> p j n", p=128), 'w_sb', 'D_c9')
        P.op('pool', lambda e: e.memset(ones[:], 1.0), writes=['ones'])
        P.op('pool', lambda e: e.memset(epsT[:], LN_EPS), writes=['epsT'])

        def finish():
            flush()
            return nc

        if stop_after == 'C':
            return finish()
        P.op('act', lambda e: e.activation(out=cact[:], in_=cact[:], func=AF.Silu), reads=['cact'], writes=['cact'])
        s0 = contextlib.ExitStack()
        s0.__enter__()
        cur.append(s0)
        wa = [sb(f"wa{i}", [128, 8, 1024], F32) for i in range(2)]
        if fused:
            g1b = sb("g1b", [128, 1024], F32)
            g2b = sb("g2b", [128, 1024], F32)
            g1row = nc.dram_tensor("g1row", [1, 1024], F32).ap()
            g2row = nc.dram_tensor("g2row", [1, 1024], F32).ap()
        pT = ps("pT", [128, 2, 4, 2, 128], BF16)
        pFZ = ps("pFZ", [128, 512], F32)
        pTMb = ps("pTM", [128, 512], F32)
        pTM = [pTMb, pTMb]
        pQ = ps("pQ", [128, 1024], BF16)
        pS = ps("pS", [128, 512], F32)
        pKV = ps("pKV", [128, 512], F32)
        pND = ps("pND", [128, 512], F32)
        P.psum_names |= {'pTa', 'pTb', 'pFZ', 'pTM', 'pQ', 'pS', 'pKV', 'pND'}
        pmod = pFZ
        pmod2 = pTM[0]
        crep = sb("crep", [128, 8, 128], F32)
        def mk_crep(e):
            i = None
            for j in range(8):
                i = e.tensor_scalar(out=crep[:, j, :], in0=ones[:], scalar1=cact[:, j:j + 1], scalar2=None,
                                    op0=ALU.mult)
            return i
        P.op('dve', mk_crep, reads=['cact', 'ones'], writes=['crep'])

        def load_wa(i, part):
            ld('sp', wa[i][:], w_ada[:, part * 1024:(part + 1) * 1024].rearrange("(j p) n -> p j n", p=128),
               f'wa{i}', f'D_wa{i}')

        def mod_featmajor(i, part):
            def f(e):
                inst = None
                for m in range(8):
                    for j in range(8):
                        inst = e.matmul(pmod[:, m:m + 1], lhsT=wa[i][:, j, m * 128:(m + 1) * 128],
                                        rhs=cact[:, j:j + 1], start=(j == 0), stop=(j == 7))
                return inst
            P.op('pe', f, reads=[f'wa{i}', 'cact'], writes=['pFZ'])
            P.op('dve', lambda e: e.tensor_tensor(out=modT[:, part * 8:(part + 1) * 8], in0=pmod[:, 0:8],
                                                  in1=badaT[:, part * 8:(part + 1) * 8], op=ALU.add),
                 reads=['pFZ', 'badaT'], writes=[f'modT{part}'])

        def mod_rowbcast(i, part, dst, dname):
            ld('sp', dst[:], b_ada_row[:, part * 1024:(part + 1) * 1024].broadcast_to([128, 1024]), dname,
               f'D_{dname}')
            for h in range(2):
                def f(e, h=h):
                    inst = None
                    for j in range(8):
                        inst = e.matmul(pmod2[:, :], lhsT=crep[:, j, :], rhs=wa[i][:, j, h * 512:(h + 1) * 512],
                                        start=(j == 0), stop=(j == 7))
                    return inst
                P.op('pe', f, reads=[f'wa{i}', 'crep'], writes=['pTM'])
                P.op('dve', lambda e, h=h: e.scalar_tensor_tensor(
                    out=dst[:, h * 512:(h + 1) * 512], in0=dst[:, h * 512:(h + 1) * 512], scalar=1.0,
                    in1=pmod2[:, :], op0=ALU.add, op1=ALU.add), reads=['pTM', dname], writes=[dname])

        load_wa(0, 0); load_wa(1, 1)
        mod_featmajor(0, 0)
        mod_featmajor(1, 1)
        P.op('dve', lambda e: e.tensor_scalar(out=sc1p[:], in0=modT[:, 8:16], scalar1=1.0, scalar2=None, op0=ALU.add),
             reads=['modT1'], writes=['sc1p'])
        if fused:
            load_wa(0, 2); load_wa(1, 3)
            mod_rowbcast(0, 2, g1b, 'g1b')
            mod_featmajor(1, 3)
            load_wa(0, 4); load_wa(1, 5)
            mod_featmajor(0, 4)
            P.op('dve', lambda e: e.tensor_scalar(out=sc2p[:], in0=modT[:, 32:40], scalar1=1.0, scalar2=None,
                                                  op0=ALU.add), reads=['modT4'], writes=['sc2p'])
            mod_rowbcast(1, 5, g2b, 'g2b')
            P.op('sp', lambda e: e.dma_start(out=g1row, in_=g1b[0:1, :]), reads=['g1b'], writes=['g1row'], dsem='D_g1r')
            P.op('sp', lambda e: e.dma_start(out=g2row, in_=g2b[0:1, :]), reads=['g2b'], writes=['g2row'], dsem='D_g2r')
        flush()
        cur.pop()
        s0.__exit__(None, None, None)
        sAF = contextlib.ExitStack(); sAF.__enter__(); cur.append(sAF)
        fzT = sb("fzT", [128, S], BF16)
        sA = contextlib.ExitStack(); sA.__enter__(); cur.append(sA)
        if stop_after == 'P0':
            return finish()
        NG = N1 // 2
        xt = [sb(f"xt{i}", [128, 1024], F32) for i in range(4)]
        xn = [sb(f"xn{i}", [128, 1024], BF16) for i in range(2)]
        bst = [sb(f"bst{i}", [128, 2, 6], F32) for i in range(2)]
        mv = [sb(f"mv{i}", [128, 2], F32) for i in range(2)]
        rs = [sb(f"rs{i}", [128, 2], F32) for i in range(2)]
        hT = [sb(f"hT{i}", [128, 8, 256], BF16) for i in range(2)]
        qk_tm = sb("qk_tm", [128, N1, 128], BF16)
        vp = sb("vp", [128, N1, 130], BF16)
        gt = sb("gt", [128, N1, 4], F32)
        scal = sb("scal", [128, NG, 16], F32)
        so_t = [sb(f"so{i}", [128, 128], F32) for i in range(2)]
        hf_t = [sb(f"hf{i}", [128, 128], F32) for i in range(2)]
        qkT = [sb(f"qkT{i}", [64, 2, 128], BF16) for i in range(2)]
        PTs = [sb(f"PTs{i}", [128, 128], BF16) for i in range(2)]
        ku = [sb(f"ku{i}", [128, 64], BF16) for i in range(2)]
        Dst = sb("Dst", [64, 129], F32)
        Cb = [sb(f"Cb{i}", [64, 129], BF16) for i in range(2)]
        ee = [sb(f"ee{i}", [128, 2, 2], F32) for i in range(2)]
        sp_ = [sb(f"sp{i}", [128, 2, 2], F32) for i in range(2)]
        ein = [sb(f"ein{i}", [128, 4, 4], F32) for i in range(2)]
        dd = [sb(f"dd{i}", [128, 2], F32) for i in range(2)]

        P.op('pool', lambda e: e.memset(vp[:, :, 128:130], 1.0), writes=['vp_ones'])

        def ln_stats(xtile, xname, k, par):
            def f(e):
                e.bn_stats(out=bst[par][:, 0, :], in_=xtile[:, 0:512])
                return e.bn_stats(out=bst[par][:, 1, :], in_=xtile[:, 512:1024])
            P.op('dve', f, reads=[xname], writes=[f'bst{par}'])
            P.op('dve', lambda e: e.bn_aggr(out=mv[par][:], in_=bst[par][:].rearrange("p a b -> p (a b)")),
                 reads=[f'bst{par}'], writes=[f'mv{par}'])
            P.op('act', lambda e: e.activation(out=rs[par][:, 0:1], in_=mv[par][:, 1:2], func=AF.Sqrt,
                                               bias=epsT[:, 0:1]),
                 reads=[f'mv{par}', 'epsT'], writes=[f'rs{par}a'])
            P.op('dve', lambda e: e.reciprocal(out=rs[par][:, 0:1], in_=rs[par][:, 0:1]),
                 reads=[f'rs{par}a'], writes=[f'rs{par}a'])
            P.op('dve', lambda e: e.tensor_scalar(out=rs[par][:, 1:2], in0=mv[par][:, 0:1], scalar1=rs[par][:, 0:1],
                                                  scalar2=-1.0, op0=ALU.mult, op1=ALU.mult),
                 reads=[f'rs{par}a', f'mv{par}'], writes=[f'rs{par}b'])

        def gate_scalars(g):
            gp = g % 2
            c0 = 2 * g
            def f1(e):
                e.activation(out=ee[gp][:, 0, :], in_=gt[:, c0:c0 + 2, 1], func=AF.Exp, scale=-1.0)
                return e.activation(out=ee[gp][:, 1, :], in_=gt[:, c0:c0 + 2, 3], func=AF.Exp, scale=-1.0)
            P.op('act', f1, reads=[f'gt{c0}', f'gt{c0 + 1}'], writes=[f'ee{gp}'])
            P.op('act', lambda e: e.activation(out=sp_[gp][:].rearrange("p a b -> p (a b)"),
                                               in_=ee[gp][:].rearrange("p a b -> p (a b)"), func=AF.Ln, bias=1.0),
                 reads=[f'ee{gp}'], writes=[f'sp{gp}'])
            def f2(e):
                e.matmul(pKV[:, 384:386], lhsT=ule[:], rhs=sp_[gp][:, 0, :], start=True, stop=True)
                e.matmul(pKV[:, 386:388], lhsT=uge[:], rhs=sp_[gp][:, 1, :], start=True, stop=True)
                return e.matmul(pKV[:, 388:392], lhsT=ones[:], rhs=sp_[gp][:].rearrange("p a b -> p (a b)"),
                                start=True, stop=True)
            P.op('pe', f2, reads=[f'sp{gp}', 'ule', 'uge', 'ones'], writes=['pKV'])
            def f3(e):
                e.tensor_tensor(out=ein[gp][:, 0, 0:2], in0=gt[:, c0:c0 + 2, 0], in1=pKV[:, 384:386], op=ALU.add)
                e.tensor_tensor(out=ein[gp][:, 0, 2:4], in0=gt[:, c0:c0 + 2, 2], in1=pKV[:, 386:388], op=ALU.add)
                e.tensor_copy(out=ein[gp][:, 1, :], in_=pKV[:, 384:388])
                return e.tensor_scalar(out=ein[gp][:, 3, :], in0=pKV[:, 388:392], scalar1=-1.0, scalar2=None,
                                       op0=ALU.mult)
            P.op('dve', f3, reads=['pKV', f'gt{c0}', f'gt{c0 + 1}'], writes=[f'ein{gp}a'])
            P.op('dve', lambda e: e.tensor_tensor(out=ein[gp][:, 2, :], in0=ein[gp][:, 0, :], in1=pKV[:, 388:392],
                                                  op=ALU.subtract),
                 reads=['pKV', f'ein{gp}a'], writes=[f'ein{gp}'])
            P.op('act', lambda e: e.activation(out=scal[:, g, :], in_=ein[gp][:].rearrange("p a b -> p (a b)"),
                                               func=AF.Exp),
                 reads=[f'ein{gp}', f'ein{gp}a'], writes=[f'scal{g}'])

        def sc_ap(g, k, d, ci, rows=128):
            i = k * 4 + d * 2 + ci
            return scal[0:rows, g, i:i + 1]

        def mlstm_chunk(c, d, part='both'):
            g = c // 2
            ci = c % 2
            par = c % 2
            mask, mname = (maskf, 'maskf') if d == 0 else (maskb, 'maskb')
            kv = pKV[0:64, par * 129:(par + 1) * 129]
            kvn = 'pKV'
            nd = pND[:, 0:129]
            cbi = c % 2
            if part in ('front', 'both'):
                mlstm_front(c, d, g, ci, par, mask, mname, kv, kvn)
            if part in ('tail', 'both'):
                mlstm_tail(c, d, g, ci, par, kv, kvn, nd, cbi)
            return nd, par

        def mlstm_front(c, d, g, ci, par, mask, mname, kv, kvn):
            def ftr(e):
                e.transpose(out=pQ[0:64, 0:128], in_=qk_tm[:, c, 0:64], identity=ident[:])
                return e.transpose(out=pQ[0:64, 128:256], in_=qk_tm[:, c, 64:128],
                                   identity=ident[:])
            P.op('pe', ftr, reads=[f'qk{c}', 'ident'], writes=['pQ'])
            P.op('act', lambda e: e.activation(out=qkT[par][:].rearrange("p a b -> p (a b)"),
                                               in_=pQ[0:64, 0:256], func=AF.Copy),
                 reads=['pQ'], writes=[f'qkT{par}'])
            P.op('pe', lambda e: e.matmul(pS[:, 0:128], lhsT=qkT[par][:, 1, :],
                                          rhs=qkT[par][:, 0, :], start=True, stop=True),
                 reads=[f'qkT{par}'], writes=['pS'])
            P.op('dve', lambda e: e.scalar_tensor_tensor(out=PTs[par][:], in0=pS[:, 0:128],
                                                         scalar=sc_ap(g, 0, d, ci), in1=mask[:], op0=ALU.mult,
                                                         op1=ALU.mult),
                 reads=['pS', f'scal{g}', mname], writes=[f'PTs{par}'])
            P.op('pool', lambda e: e.tensor_scalar(out=ku[par][:], in0=qk_tm[:, c, 64:128], scalar1=sc_ap(g, 2, d, ci),
                                                   scalar2=None, op0=ALU.mult),
                 reads=[f'qk{c}', f'scal{g}'], writes=[f'ku{par}'])
            P.op('pe', lambda e: e.matmul(kv, lhsT=ku[par][:], rhs=vp[:, c, 0:129], start=True, stop=True),
                 reads=[f'ku{par}', f'vp{c}', 'vp_ones'], writes=[kvn])

        def mlstm_tail(c, d, g, ci, par, kv, kvn, nd, cbi):
            def fnd(e):
                e.matmul(nd, lhsT=PTs[par][:], rhs=vp[:, c, 0:129], start=True, stop=False)
                return e.matmul(nd, lhsT=qkT[par][:, 0, :], rhs=Cb[cbi][:], start=False, stop=True)
            P.op('pe', fnd, reads=[f'PTs{par}', f'vp{c}', 'vp_ones', f'qkT{par}', f'Cb{cbi}'], writes=['pND'])
            P.op('dve', lambda e: e.tensor_scalar(out=dd[par][:, 0:1], in0=nd[:, 128:129], scalar1=-1.0, scalar2=None,
                                                  op0=ALU.mult),
                 reads=['pND'], writes=[f'dd{par}'])
            P.op('dve', lambda e: e.tensor_tensor(out=dd[par][:, 0:1], in0=nd[:, 128:129], in1=dd[par][:, 0:1],
                                                  op=ALU.max),
                 reads=['pND', f'dd{par}'], writes=[f'dd{par}'])
            P.op('dve', lambda e: e.tensor_scalar(out=dd[par][:, 0:1], in0=dd[par][:, 0:1], scalar1=sc_ap(g, 1, d, ci),
                                                  scalar2=None, op0=ALU.max),
                 reads=[f'dd{par}', f'scal{g}'], writes=[f'dd{par}'])
            P.op('dve', lambda e: e.reciprocal(out=dd[par][:, 1:2], in_=dd[par][:, 0:1]),
                 reads=[f'dd{par}'], writes=[f'ddr{par}'])
            P.op('dve', lambda e: e.scalar_tensor_tensor(out=Dst[:], in0=Dst[:], scalar=sc_ap(g, 3, d, ci, 64),
                                                         in1=kv, op0=ALU.mult, op1=ALU.add),
                 reads=[kvn, 'Dst', f'scal{g}'], writes=['Dst'])
            P.op('act', lambda e: e.activation(out=Cb[1 - cbi][:], in_=Dst[:], func=AF.Copy, scale=0.125),
                 reads=['Dst'], writes=[f'Cb{1 - cbi}'])

        def stageA(g):
            gp = g % 2
            for t in range(2):
                c = 2 * g + t
                xi = c % 4
                ld('sp', xt[xi][:], xb[c * 128:(c + 1) * 128, :], f'xt{xi}', f'D_x{xi}')
                ln_stats(xt[xi], f'xt{xi}', c, t)
                P.op('act', lambda e, xi=xi, t=t: e.activation(out=xn[t][:], in_=xt[xi][:], func=AF.Identity,
                                                               scale=rs[t][:, 0:1], bias=rs[t][:, 1:2]),
                     reads=[f'xt{xi}', f'rs{t}a', f'rs{t}b'], writes=[f'xn{t}'])
                def ftr(e, t=t):
                    inst = None
                    for j in range(8):
                        inst = e.transpose(out=pT[:, j // 4, j % 4, t, :], in_=xn[t][:, j * 128:(j + 1) * 128],
                                           identity=ident[:])
                    return inst
                P.op('pe', ftr, reads=[f'xn{t}', 'ident'], writes=['pTa', 'pTb'])
            for j in range(8):
                src = pT[:, j // 4, j % 4, :, :].rearrange("p a b -> p (a b)")
                dst = hT[gp][:, j, :]
                if j < 4:
                    P.op('act', lambda e, src=src, dst=dst, j=j: e.activation(out=dst, in_=src, func=AF.Identity,
                                                                              scale=sc1p[:, j:j + 1],
                                                                              bias=modT[:, j:j + 1]),
                         reads=['pTa', 'sc1p', 'modT0'], writes=[f'hT{gp}_{j}'])
                else:
                    P.op('dve', lambda e, src=src, dst=dst, j=j: e.tensor_scalar(out=dst, in0=src,
                                                                                 scalar1=sc1p[:, j:j + 1],
                                                                                 scalar2=modT[:, j:j + 1],
                                                                                 op0=ALU.mult, op1=ALU.add),
                         reads=['pTb', 'sc1p', 'modT0'], writes=[f'hT{gp}_{j}'])
            hnames = [f'hT{gp}_{j}' for j in range(8)]
            for t in range(2):
                c = 2 * g + t
                def ftm(e, t=t, gp=gp):
                    inst = None
                    for j in range(8):
                        inst = e.matmul(pTM[t][:, 0:388], lhsT=hT[gp][:, j, t * 128:(t + 1) * 128],
                                        rhs=w_sb[:, j, 0:388], start=(j == 0), stop=(j == 7))
                    return inst
                P.op('pe', ftm, reads=hnames + ['w_sb'], writes=['pTM'])
                P.op('act', lambda e, t=t, c=c: e.activation(out=vp[:, c, 0:128], in_=pTM[t][:, 128:256], func=AF.Copy),
                     reads=['pTM'], writes=[f'vp{c}'])
                P.op('act', lambda e, t=t: e.activation(out=so_t[t][:], in_=pTM[t][:, 256:384], func=AF.Sigmoid),
                     reads=['pTM'], writes=[f'so{t}'])
                P.op('dve', lambda e, t=t, c=c: e.tensor_copy(out=qk_tm[:, c, :], in_=pTM[t][:, 0:128]),
                     reads=['pTM'], writes=[f'qk{c}'])
                P.op('dve', lambda e, t=t, c=c: e.tensor_tensor(out=gt[:, c, :], in0=pTM[t][:, 384:388], in1=bg[:],
                                                                op=ALU.add),
                     reads=['pTM', 'bg'], writes=[f'gt{c}'])
                P.op('pool', lambda e, t=t, c=c: e.dma_start(out=so_d[c * 128:(c + 1) * 128, :], in_=so_t[t][:]),
                     reads=[f'so{t}'], writes=[f'so_d{c}'], dsem=f'D_so{t}')
                if t == 0:
                    def ffz(e, gp=gp):
                        inst = None
                        for j in range(8):
                            inst = e.matmul(pFZ[:, 0:256], lhsT=w_sb[:, j, 388:516], rhs=hT[gp][:, j, :],
                                            start=(j == 0), stop=(j == 7))
                        return inst
                    P.op('pe', ffz, reads=hnames + ['w_sb'], writes=['pFZ'])
                    P.op('act', lambda e, g=g: e.activation(out=fzT[:, g * 256:(g + 1) * 256], in_=pFZ[:, 0:256],
                                                            func=AF.Copy),
                         reads=['pFZ'], writes=[f'fzT{g}'])
            gate_scalars(g)
        def hf_out(c, nd, par):
            P.op('act', lambda e: e.activation(out=hf_t[par][:], in_=nd[:, 0:128], func=AF.Copy, scale=dd[par][:, 1:2]),
                 reads=['pND', f'ddr{par}'], writes=[f'hf{par}'])
            P.op('pool', lambda e: e.dma_start(out=hf_d[c * 128:(c + 1) * 128, :], in_=hf_t[par][:]),
                 reads=[f'hf{par}'], writes=[f'hf_d{c}'], dsem=f'D_hf{par}')

        P.op('pool', lambda e: e.memset(Dst[:], 0.0), writes=['Dst'])
        P.op('pool', lambda e: e.memset(Cb[0][:], 0.0), writes=['Cb0'])
        P.replay(P.record(lambda: stageA(0)))
        for g in range(NG):
            c0 = 2 * g
            f0 = P.record(lambda: mlstm_chunk(c0, 0, 'front'))
            t0_ = P.record(lambda: (mlstm_chunk(c0, 0, 'tail'), hf_out(c0, pND[:, 0:129], c0 % 2)))
            f1 = P.record(lambda: mlstm_chunk(c0 + 1, 0, 'front'))
            t1_ = P.record(lambda: (mlstm_chunk(c0 + 1, 0, 'tail'), hf_out(c0 + 1, pND[:, 0:129], (c0 + 1) % 2)))
            nxt = P.record(lambda: stageA(g + 1)) if g + 1 < NG else []
            n3 = len(nxt) // 3
            P.replay(f0, nxt[:n3])
            P.replay(t0_, f1, nxt[n3:2 * n3])
            P.replay(t1_, nxt[2 * n3:])
        if stop_after == 'P1':
            return finish()
        hs = [sb(f"hs{i}", [128, 128], F32) for i in range(2)]
        hn = [sb(f"hn{i}", [128, 128], F32) for i in range(2)]
        ym = [sb(f"ym{i}", [128, 128], BF16) for i in range(2)]
        sol = [sb(f"sol{i}", [128, 128], F32) for i in range(2)]
        hfl = [sb(f"hfl{i}", [128, 128], F32) for i in range(2)]
        bs2 = [sb(f"bs2{i}", [128, 6], F32) for i in range(2)]
        mv2 = [sb(f"mv2{i}", [128, 2], F32) for i in range(2)]
        rs2 = [sb(f"rs2{i}", [128, 2], F32) for i in range(2)]

        P.op('pool', lambda e: e.memset(Dst[:], 0.0), reads=[], writes=['Dst'])
        cb_first = (N1 - 1) % 2
        P.op('pool', lambda e: e.memset(Cb[cb_first][:], 0.0), writes=[f'Cb{cb_first}'])
        mt_m = dbg[:, 0:128] if not fused else mixg_m.ap()
        mt_f = dbg[:, 128:256] if not fused else mixg_f.ap()
        def s2_front(c):
            par = c % 2
            ld('sp', sol[par][:], so_d[c * 128:(c + 1) * 128, :], f'sol{par}', f'D_sol{par}', reads=[f'so_d{c}'])
            ld('sp', hfl[par][:], hf_d[c * 128:(c + 1) * 128, :], f'hfl{par}', f'D_hfl{par}', reads=[f'hf_d{c}'])
            P.op('pool', lambda e, par=par: e.tensor_tensor(out=sol[par][:], in0=sol[par][:], in1=nwb[:], op=ALU.mult),
                 reads=[f'sol{par}', 'nwb'], writes=[f'sol{par}'])
            mlstm_chunk(c, 1, 'front')

        def s2_tail(c):
            par = c % 2
            nd, _ = mlstm_chunk(c, 1, 'tail')
            P.op('dve', lambda e, nd=nd, par=par: e.scalar_tensor_tensor(out=hs[par][:], in0=nd[:, 0:128],
                                                                         scalar=dd[par][:, 1:2], in1=hfl[par][:],
                                                                         op0=ALU.mult, op1=ALU.add),
                 reads=['pND', f'ddr{par}', f'hfl{par}'], writes=[f'hs{par}'])
            P.op('dve', lambda e, par=par: e.bn_stats(out=bs2[par][:], in_=hs[par][:]),
                 reads=[f'hs{par}'], writes=[f'bs2{par}'])
            P.op('dve', lambda e, par=par: e.bn_aggr(out=mv2[par][:], in_=bs2[par][:]),
                 reads=[f'bs2{par}'], writes=[f'mv2{par}'])
            P.op('act', lambda e, par=par: e.activation(out=rs2[par][:, 0:1], in_=mv2[par][:, 1:2], func=AF.Sqrt,
                                                        bias=epsT[:, 0:1]),
                 reads=[f'mv2{par}', 'epsT'], writes=[f'rs2{par}a'])
            P.op('dve', lambda e, par=par: e.reciprocal(out=rs2[par][:, 0:1], in_=rs2[par][:, 0:1]),
                 reads=[f'rs2{par}a'], writes=[f'rs2{par}a'])
            P.op('dve', lambda e, par=par: e.tensor_scalar(out=rs2[par][:, 1:2], in0=mv2[par][:, 0:1],
                                                           scalar1=rs2[par][:, 0:1], scalar2=-1.0, op0=ALU.mult,
                                                           op1=ALU.mult),
                 reads=[f'rs2{par}a', f'mv2{par}'], writes=[f'rs2{par}b'])
            P.op('act', lambda e, par=par: e.activation(out=hn[par][:], in_=hs[par][:], func=AF.Identity,
                                                        scale=rs2[par][:, 0:1], bias=rs2[par][:, 1:2]),
                 reads=[f'hs{par}', f'rs2{par}a', f'rs2{par}b'], writes=[f'hn{par}'])
            P.op('pool', lambda e, par=par: e.tensor_tensor(out=ym[par][:], in0=hn[par][:], in1=sol[par][:],
                                                            op=ALU.mult),
                 reads=[f'hn{par}', f'sol{par}'], writes=[f'ym{par}'])
            P.op('pool', lambda e, par=par, c=c: e.dma_start(out=mt_m[c * 128:(c + 1) * 128, :],
                                                             in_=ym[par][:]),
                 reads=[f'ym{par}'], writes=[f'mixg_m_c{c}'], dsem=f'D_ym{par}')

        P.replay(P.record(lambda: s2_front(N1 - 1)))
        for c in range(N1 - 1, -1, -1):
            tl = P.record(lambda: s2_tail(c))
            fr = P.record(lambda: s2_front(c - 1)) if c > 0 else []
            P.replay(tl, fr)

        if stop_after == 'P2':
            return finish()
        flush()
        cur.pop(); sA.__exit__(None, None, None)
        sF = contextlib.ExitStack(); sF.__enter__(); cur.append(sF)
        if fused:
            for i_ in range(NP_):
                P.op('pool', lambda e, i_=i_: e.collective_compute(
                    "AllGather", ALU.bypass, replica_groups=[[0, 1, 2, 3], [4, 5, 6, 7]],
                    ins=[mixg_m.ap()[i_ * R_:(i_ + 1) * R_, :].opt()],
                    outs=[gath_m.ap()[i_ * 4 * R_:(i_ + 1) * 4 * R_, :].opt()]),
                    reads=[f'mixg_m_c{cc}' for cc in range(i_ * CPP, (i_ + 1) * CPP)], writes=['gath_m'],
                    dsem='CC', inc=1)
        cs_b = sb("cs_b", [128, 256], BF16)
        a1_b = sb("a1_b", [N1, 2 * N1], BF16)
        a2_b = sb("a2_b", [N1, 2 * N1], BF16)
        cn_b = sb("cn_b", [128, 256], BF16)
        tw = sb("tw", [128, 4 * N1], F32)
        ld('pool', cs_b[:], c_cs, 'cs_b', 'D_f0')
        ld('pool', a1_b[:], c_a1, 'a1_b', 'D_f1')
        ld('pool', a2_b[:], c_a2, 'a2_b', 'D_f2')
        ld('pool', cn_b[:], c_cn, 'cn_b', 'D_f3')
        ld('sp', tw[:], c_tw, 'tw', 'D_f4')
        G = sb("G", [128, 128, 256], BF16)
        Yr = sb("Yr", [128, N1, 128], BF16)
        Qp = sb("Qp", [128, N1, 128], BF16)
        yf = G[:].rearrange("p a b -> p (a b)")[:, 0:N1 * 128].rearrange("p (a b) -> p a b", b=128)
        t1 = [sb(f"t1_{i}", [128, 2 * N1], F32) for i in range(2)]
        t2 = [sb(f"t2_{i}", [128, 2 * N1], F32) for i in range(2)]
        pG = [pFZ, pTMb, pS, pKV]
        pGn = ['pFZ', 'pTM', 'pS', 'pKV']
        fz_all = [f'fzT{g}' for g in range(NG)]
        for i in range(64):
            k = i % 4
            eng = 'act' if k < 2 else 'dve'
            def f0(e, i=i, k=k):
                inst = None
                for q in range(2):
                    n2 = 2 * i + q
                    inst = e.matmul(pG[k][0:N1, q * 256:(q + 1) * 256], lhsT=fzT[:, n2:S:128], rhs=cs_b[:],
                                    start=True, stop=True)
                return inst
            P.op('pe', f0, reads=fz_all + ['cs_b'], writes=[pGn[k]])
            dst = G[0:N1, 2 * i:2 * i + 2, :].rearrange("p a b -> p (a b)")
            if eng == 'act':
                P.op('act', lambda e, k=k, dst=dst: e.activation(out=dst, in_=pG[k][0:N1, :], func=AF.Copy),
                     reads=[pGn[k]], writes=[f'G{i}'])
            else:
                P.op('dve', lambda e, k=k, dst=dst: e.tensor_copy(out=dst, in_=pG[k][0:N1, :]),
                     reads=[pGn[k]], writes=[f'G{i}'])
        g_all = [f'G{i}' for i in range(64)]
        pY = [pFZ, pTMb]
        pYn = ['pFZ', 'pTM']
        for j in range(128):
            k = j % 2
            def fa(e, j=j, k=k):
                e.matmul(pY[k][:, 0:2 * N1], lhsT=G[0:N1, :, j], rhs=a1_b[:], start=True, stop=False)
                return e.matmul(pY[k][:, 0:2 * N1], lhsT=G[0:N1, :, 128 + j], rhs=a2_b[:], start=False, stop=True)
            P.op('pe', fa, reads=g_all + ['a1_b', 'a2_b'], writes=[pYn[k]])
            P.op('dve', lambda e, k=k: e.tensor_tensor(out=t1[k][:], in0=pY[k][:, 0:2 * N1], in1=tw[:, 0:2 * N1],
                                                       op=ALU.mult),
                 reads=[pYn[k], 'tw'], writes=[f't1_{k}'])
            P.op('dve', lambda e, k=k: e.tensor_tensor(out=t2[k][:], in0=pY[k][:, 0:2 * N1],
                                                       in1=tw[:, 2 * N1:4 * N1], op=ALU.mult),
                 reads=[pYn[k], 'tw'], writes=[f't2_{k}'])
            P.op('pool', lambda e, k=k, j=j: e.tensor_tensor(out=Yr[:, :, j], in0=t1[k][:, 0:N1],
                                                             in1=t2[k][:, N1:2 * N1], op=ALU.subtract),
                 reads=[f't1_{k}', f't2_{k}'], writes=[f'Yr{j}'])
            P.op('pool', lambda e, k=k, j=j: e.tensor_tensor(out=Qp[:, :, j], in0=t1[k][:, N1:2 * N1],
                                                             in1=t2[k][:, 0:N1], op=ALU.add),
                 reads=[f't1_{k}', f't2_{k}'], writes=[f'Qp{j}'])
        y_all = [f'Yr{j}' for j in range(128)] + [f'Qp{j}' for j in range(128)]
        NB = N1 // 4
        pX = [pS, pKV]
        pXn = ['pS', 'pKV']
        for bi in range(NB):
            k = bi % 2
            def fc(e, bi=bi, k=k):
                e.matmul(pX[k][:, :], lhsT=cn_b[:, 0:128], rhs=Yr[:, 4 * bi:4 * bi + 4, :].rearrange("p a b -> p (a b)"),
                         start=True, stop=False)
                return e.matmul(pX[k][:, :], lhsT=cn_b[:, 128:256],
                                rhs=Qp[:, 4 * bi:4 * bi + 4, :].rearrange("p a b -> p (a b)"), start=False, stop=True)
            P.op('pe', fc, reads=y_all + ['cn_b'], writes=[pXn[k]])
            P.op('act', lambda e, bi=bi, k=k: e.activation(
                out=yf[:, 4 * bi:4 * bi + 4, :].rearrange("p a b -> p (a b)"), in_=pX[k][:, :], func=AF.Copy),
                reads=[pXn[k]], writes=[f'yf{bi}'])
        mt3 = mt_f.rearrange("(a b) j -> a b j", b=N1)
        npc = 4 if N1 >= 4 else 1
        for pc in range(npc):
            lo, hi = pc * N1 // npc, (pc + 1) * N1 // npc
            P.op('pool', lambda e, lo=lo, hi=hi: e.dma_start(out=mt3[:, lo:hi, :], in_=yf[:, lo:hi, :]),
                 reads=[f'yf{bi}' for bi in range(NB)], writes=['mixg_f'], dsem='D_yf')
        flush()
        cur.pop(); sF.__exit__(None, None, None)
        cur.pop(); sAF.__exit__(None, None, None)
        cur.pop(); sGA.__exit__(None, None, None)
        psA.__exit__(None, None, None)
        if not fused:
            return nc
        ctx = dict(nc=nc, P=P, es=es, cur=cur, flush=flush, ld=ld, sb=sb, xq=xq, w_out=w_out, ln1_g=ln1_g, ln1_b=ln1_b,
                   w_ff1=w_ff1, b_ff1T=b_ff1T, w_ff2=w_ff2, b_ff2=b_ff2, ln2_g=ln2_g, ln2_b=ln2_b, gidx=gidx, out=out,
                   gath_m=gath_m, gath_f=gath_f, mixg_f=mixg_f, ident=ident, ones=ones, epsT=epsT, modT=modT, sc2p=sc2p, g1row=g1row, g2row=g2row)
        build_B(N1, ctx)
        return nc


def build_B(N1, ctx=None):
    S = 128 * N1
    TQ = S // 4
    NT = TQ // 128
    NGB = NT // 2
    fusedB = ctx is not None
    if not fusedB:
        nc = bass.Bass("TRN2", target_bir_lowering=False)
        P = Prog()

        def din(name, shape, dt=F32):
            return nc.dram_tensor(name, shape, dt, kind="ExternalInput").ap()
        xq = din("xq", [TQ, D])
        mixq = din("mixq", [TQ, D], BF16)
        cT = din("cT", [128, 8])
        w_ada = din("w_ada", [D, 6 * D])
        b_adaT = din("b_adaT", [128, 48])
        b_ada_row = din("b_ada_row", [1, 6 * D])
        w_out = din("w_out", [D, D])
        ln1_g = din("ln1_g", [1, D]); ln1_b = din("ln1_b", [1, D])
        w_ff1 = din("w_ff1", [D, DFF]); b_ff1T = din("b_ff1T", [128, 32])
        w_ff2 = din("w_ff2", [DFF, D]); b_ff2 = din("b_ff2", [1, D])
        ln2_g = din("ln2_g", [1, D]); ln2_b = din("ln2_b", [1, D])
        c_ident = din("c_ident", [128, 128])
        out = nc.dram_tensor("out", [TQ, D], F32, kind="ExternalOutput").ap()
        es = contextlib.ExitStack()
        es.__enter__()
        cur = [es]
        sems = {}

        def sb(name, shape, dt=F32):
            return cur[-1].enter_context(nc.sbuf_tensor(name, shape, dt))

        def flush():
            P.barrier_all('sp')
            for sname in P.semnames:
                if sname not in sems:
                    sems[sname] = es.enter_context(nc.semaphore(sname))
            with nc.Block() as block:
                P.emit(nc, sems, block)
            for e_ in P.ENG:
                P.streams[e_] = []
            for e_ in P.ENG:
                P.barrier_all(e_)

        def ld(eng, dst, src, name, sem, reads=()):
            P.op(eng, lambda e: e.dma_start(out=dst, in_=src), reads=reads, writes=[name], dsem=sem)
    else:
        nc = ctx['nc']; P = ctx['P']; es = ctx['es']; cur = ctx['cur']; flush = ctx['flush']; ld = ctx['ld']; sb = ctx['sb']
        xq = ctx['xq']; w_out = ctx['w_out']; ln1_g = ctx['ln1_g']; ln1_b = ctx['ln1_b']; w_ff1 = ctx['w_ff1']
        b_ff1T = ctx['b_ff1T']; w_ff2 = ctx['w_ff2']; b_ff2 = ctx['b_ff2']; ln2_g = ctx['ln2_g']; ln2_b = ctx['ln2_b']
        gidx = ctx['gidx']; out = ctx['out']; gath_m = ctx['gath_m']; gath_f = ctx['gath_f']; mixg_f = ctx['mixg_f']
    psB = contextlib.ExitStack()
    psB.__enter__()

    def ps(name, shape, dt=F32):
        return psB.enter_context(nc.psum_tensor(name, shape, dt))
    if True:
        if fusedB:
            ident = ctx['ident']; ones = ctx['ones']; epsT = ctx['epsT']; modT = ctx['modT']; sc2p = ctx['sc2p']
            g1row = ctx['g1row']; g2row = ctx['g2row']
        else:
            ident = sb("ident", [128, 128], BF16)
            ones = sb("ones", [128, 128], F32)
            cact = sb("cact", [128, 8], F32)
            modT = sb("modT", [128, 48], F32)
            badaT = sb("badaT", [128, 48], F32)
            sc2p = sb("sc2p", [128, 8], F32)
            epsT = sb("epsT", [128, 1], F32)
            g1row = nc.dram_tensor("g1row", [1, 1024], F32).ap()
            g2row = nc.dram_tensor("g2row", [1, 1024], F32).ap()
        ones_b = sb("ones_b", [1, 128], BF16)
        b1T = sb("b1T", [128, 32], F32)
        bff2_b = sb("bff2_b", [1, 1024], BF16)
        l1g = sb("l1g", [128, 1024], F32); l1b = sb("l1b", [128, 1024], F32)
        l2g = sb("l2g", [128, 1024], F32); l2b = sb("l2b", [128, 1024], F32)
        pb_ = [ps(f"pb{i}", [128, 512], F32) for i in range(7)]
        pTr = ps("pTr", [128, 8, 128], BF16)
        P.psum_names |= {f'pb{i}' for i in range(7)} | {'pTr'}
        ld('sp', b1T[:], b_ff1T, 'b1T', 'D_b1')
        ld('pool', bff2_b[:], b_ff2, 'bff2_b', 'D_b2')
        ld('sp', l1g[:], ln1_g.broadcast_to([128, 1024]), 'l1g', 'D_l1g')
        ld('sp', l1b[:], ln1_b.broadcast_to([128, 1024]), 'l1b', 'D_l1b')
        ld('sp', l2g[:], ln2_g.broadcast_to([128, 1024]), 'l2g', 'D_l2g')
        ld('sp', l2b[:], ln2_b.broadcast_to([128, 1024]), 'l2b', 'D_l2b')
        P.op('pool', lambda e: e.memset(ones_b[:], 1.0), writes=['ones_b'])
        if fusedB:
            gix = sb("gix", [128, 4 * NT], mybir.dt.int32)
            ld('sp', gix[:], gidx, 'gix', 'D_gix')
        if not fusedB:
            ld('pool', ident[:], c_ident, 'ident', 'D_c0')
            ld('sp', cact[:], cT, 'cact', 'D_c7')
            ld('sp', badaT[:], b_adaT, 'badaT', 'D_c8')
            P.op('pool', lambda e: e.memset(ones[:], 1.0), writes=['ones'])
            P.op('pool', lambda e: e.memset(epsT[:], LN_EPS), writes=['epsT'])
            P.op('act', lambda e: e.activation(out=cact[:], in_=cact[:], func=AF.Silu), reads=['cact'], writes=['cact'])
            s0 = contextlib.ExitStack(); s0.__enter__(); cur.append(s0)
            wa = [sb(f"wa{i}", [128, 8, 1024], F32) for i in range(2)]
            crep = sb("crep", [128, 8, 128], F32)
            g1b = sb("g1b", [128, 1024], F32)
            g2b = sb("g2b", [128, 1024], F32)

            def mk_crep(e):
                i = None
                for j in range(8):
                    i = e.tensor_scalar(out=crep[:, j, :], in0=ones[:], scalar1=cact[:, j:j + 1], scalar2=None,
                                        op0=ALU.mult)
                return i
            P.op('dve', mk_crep, reads=['cact', 'ones'], writes=['crep'])

            def load_wa(i, part):
                ld('sp', wa[i][:], w_ada[:, part * 1024:(part + 1) * 1024].rearrange("(j p) n -> p j n", p=128),
                   f'wa{i}', f'D_wa{i}')

            def mod_featmajor(i, part):
                def f(e):
                    inst = None
                    for m in range(8):
                        for j in range(8):
                            inst = e.matmul(pb_[0][:, m:m + 1], lhsT=wa[i][:, j, m * 128:(m + 1) * 128],
                                            rhs=cact[:, j:j + 1], start=(j == 0), stop=(j == 7))
                    return inst
                P.op('pe', f, reads=[f'wa{i}', 'cact'], writes=['pb0'])
                P.op('dve', lambda e: e.tensor_tensor(out=modT[:, part * 8:(part + 1) * 8], in0=pb_[0][:, 0:8],
                                                      in1=badaT[:, part * 8:(part + 1) * 8], op=ALU.add),
                     reads=['pb0', 'badaT'], writes=[f'modT{part}'])

            def mod_rowbcast(i, part, dst, dname):
                ld('sp', dst[:], b_ada_row[:, part * 1024:(part + 1) * 1024].broadcast_to([128, 1024]), dname,
                   f'D_{dname}')
                for h in range(2):
                    def f(e, h=h):
                        inst = None
                        for j in range(8):
                            inst = e.matmul(pb_[1][:, :], lhsT=crep[:, j, :], rhs=wa[i][:, j, h * 512:(h + 1) * 512],
                                            start=(j == 0), stop=(j == 7))
                        return inst
                    P.op('pe', f, reads=[f'wa{i}', 'crep'], writes=['pb1'])
                    P.op('dve', lambda e, h=h: e.scalar_tensor_tensor(
                        out=dst[:, h * 512:(h + 1) * 512], in0=dst[:, h * 512:(h + 1) * 512], scalar=1.0,
                        in1=pb_[1][:, :], op0=ALU.add, op1=ALU.add), reads=['pb1', dname], writes=[dname])

            load_wa(0, 2); load_wa(1, 3)
            mod_rowbcast(0, 2, g1b, 'g1b')
            mod_featmajor(1, 3)
            load_wa(0, 4); load_wa(1, 5)
            mod_featmajor(0, 4)
            P.op('dve', lambda e: e.tensor_scalar(out=sc2p[:], in0=modT[:, 32:40], scalar1=1.0, scalar2=None,
                                                  op0=ALU.add), reads=['modT4'], writes=['sc2p'])
            mod_rowbcast(1, 5, g2b, 'g2b')
            P.op('sp', lambda e: e.dma_start(out=g1row, in_=g1b[0:1, :]), reads=['g1b'], writes=['g1row'], dsem='D_g1r')
            P.op('sp', lambda e: e.dma_start(out=g2row, in_=g2b[0:1, :]), reads=['g2b'], writes=['g2row'], dsem='D_g2r')
            flush()
            cur.pop(); s0.__exit__(None, None, None)

        wout = sb("wout", [128, 8, 1024], BF16)
        wff1 = sb("wff1", [128, 8, 4096], BF16)
        wff2 = sb("wff2", [128, 32, 1024], BF16)
        if fusedB:
            NP_ = max(1, S // 2048)
            R_ = S // NP_
            for i_ in range(NP_):
                P.op('pool', lambda e, i_=i_: e.collective_compute(
                    "AllGather", ALU.bypass, replica_groups=[[0, 1, 2, 3], [4, 5, 6, 7]],
                    ins=[mixg_f.ap()[i_ * R_:(i_ + 1) * R_, :].opt()],
                    outs=[gath_f.ap()[i_ * 4 * R_:(i_ + 1) * 4 * R_, :].opt()]),
                    reads=['mixg_f'], writes=['gath_f'], dsem='CC', inc=1)
        sT = contextlib.ExitStack(); sT.__enter__(); cur.append(sT)
        tg1 = sb("tg1", [128, 1024], F32)
        tg2 = sb("tg2", [128, 1024], F32)
        stg = [sb(f"stg{i}", [128, 2048], F32) for i in range(3)]
        ld('sp', tg1[:], g1row.broadcast_to([128, 1024]), 'tg1', 'D_tg1', reads=['g1row'])
        ld('sp', tg2[:], g2row.broadcast_to([128, 1024]), 'tg2', 'D_tg2', reads=['g2row'])
        si = [0]

        def stage(src_ap, shape3=None):
            k = si[0] % 3
            si[0] += 1
            dst = stg[k][:] if shape3 is None else stg[k][:].rearrange("p (a b) -> p a b", a=shape3)
            ld('sp', dst, src_ap, f'stg{k}', f'D_stg{k}')
            return k
        for jj in range(4):
            k = stage(w_out[jj * 256:(jj + 1) * 256, :].rearrange("(a p) n -> p a n", p=128), 2)
            for a in range(2):
                P.op('dve', lambda e, k=k, a=a, jj=jj: e.tensor_tensor(out=wout[:, 2 * jj + a, :],
                                                                     in0=stg[k][:, a * 1024:(a + 1) * 1024],
                                                                     in1=tg1[:], op=ALU.mult),
                     reads=[f'stg{k}', 'tg1'], writes=['wout'])
        for j in range(8):
            for hh in range(2):
                k = stage(w_ff1[j * 128:(j + 1) * 128, hh * 2048:(hh + 1) * 2048])
                P.op('act', lambda e, k=k, j=j, hh=hh: e.activation(out=wff1[:, j, hh * 2048:(hh + 1) * 2048],
                                                                  in_=stg[k][:], func=AF.Copy),
                     reads=[f'stg{k}'], writes=['wff1'])
        for ff in range(16):
            k = stage(w_ff2[ff * 256:(ff + 1) * 256, :].rearrange("(a p) n -> p a n", p=128), 2)
            for a in range(2):
                f_ = 2 * ff + a
                eng_ = 'dve' if a == 0 else 'pool'
                P.op(eng_, lambda e, k=k, a=a, f_=f_: e.tensor_tensor(out=wff2[:, f_, :],
                                                                     in0=stg[k][:, a * 1024:(a + 1) * 1024],
                                                                     in1=tg2[:], op=ALU.mult),
                     reads=[f'stg{k}', 'tg2'], writes=[f'wff2_{f_}'])
        P.op('dve', lambda e: e.tensor_tensor(out=bff2_b[:], in0=bff2_b[:], in1=tg2[0:1, :], op=ALU.mult),
             reads=['bff2_b', 'tg2'], writes=['bff2_b'])
        w2n = [f'wff2_{f_}' for f_ in range(32)]
        flush()
        cur.pop(); sT.__exit__(None, None, None)

        xt = sb("xt", [128, 1024], F32)
        mxs = [sb(f"mx{i}", [128, 1024], BF16) for i in range(2)]
        x1 = sb("x1", [128, 4, 1024], F32)
        xn = sb("xn", [128, 1024], BF16)
        mixT = xn[:].rearrange("p (a b) -> p a b", b=128)
        h2T = [sb(f"h2TB{i}", [128, 8, 256], BF16) for i in range(2)]
        uT = sb("uT", [128, 8, 256], BF16)
        rt = [sb(f"rtB{i}", [128, 256], F32) for i in range(2)]
        bst = [sb(f"bstB{i}", [128, 2, 6], F32) for i in range(2)]
        mv = [sb(f"mvB{i}", [128, 2], F32) for i in range(2)]
        rs = [sb(f"rsB{i}", [128, 2], F32) for i in range(2)]

        def ln_stats(src, sname, k):
            def f(e):
                e.bn_stats(out=bst[k][:, 0, :], in_=src[:, 0:512])
                return e.bn_stats(out=bst[k][:, 1, :], in_=src[:, 512:1024])
            P.op('dve', f, reads=[sname], writes=[f'bst{k}'])
            yield
            P.op('dve', lambda e: e.bn_aggr(out=mv[k][:], in_=bst[k][:].rearrange("p a b -> p (a b)")),
                 reads=[f'bst{k}'], writes=[f'mv{k}'])
            yield
            P.op('act', lambda e: e.activation(out=rs[k][:, 0:1], in_=mv[k][:, 1:2], func=AF.Sqrt, bias=epsT[:, 0:1]),
                 reads=[f'mv{k}', 'epsT'], writes=[f'rsa{k}'])
            yield
            P.op('dve', lambda e: e.reciprocal(out=rs[k][:, 0:1], in_=rs[k][:, 0:1]), reads=[f'rsa{k}'],
                 writes=[f'rsa{k}'])
            yield
            P.op('dve', lambda e: e.tensor_scalar(out=rs[k][:, 1:2], in0=mv[k][:, 0:1], scalar1=rs[k][:, 0:1],
                                                  scalar2=-1.0, op0=ALU.mult, op1=ALU.mult),
                 reads=[f'rsa{k}', f'mv{k}'], writes=[f'rsb{k}'])
            yield

        def prologue(g):
            hp = g % 2
            for t in range(2):
                r0 = (2 * g + t) * 128
                mx = mxs[t]
                if not fusedB:
                    ld('sp', mx[:], mixq[r0:r0 + 128, :], f'mx{t}', f'D_m{t}')
                else:
                    T_ = 2 * g + t
                    for s_ in range(8):
                        gsrc = gath_m if s_ < 4 else gath_f
                        sr = s_ % 4
                        P.op('pool', lambda e, s_=s_, sr=sr, T_=T_, gsrc=gsrc, mx=mx: e.indirect_dma_start(
                            out=mx[:, s_ * 128:(s_ + 1) * 128], out_offset=None, in_=gsrc.ap()[:, :],
                            in_offset=bass.IndirectOffsetOnAxis(ap=gix[:, sr * NT + T_:sr * NT + T_ + 1], axis=0)),
                            reads=['gath_m', 'gath_f', 'gix'], writes=[f'mx{t}'], dsem=f'D_m{t}')
                yield
            for t in range(2):
                r0 = (2 * g + t) * 128
                xi = (2 * g + t) % 4
                x1t = x1[:, xi, :]
                xname = f'x1_{xi}'
                mx = mxs[t]
                ld('sp', xt[:], xq[r0:r0 + 128, :], 'xt', 'D_x')
                yield

                def ftr(e, mx=mx):
                    inst = None
                    for j in range(8):
                        inst = e.transpose(out=pTr[:, j, :], in_=mx[:, j * 128:(j + 1) * 128], identity=ident[:])
                    return inst
                P.op('pe', ftr, reads=[f'mx{t}', 'ident'], writes=['pTr'])
                yield
                P.op('act', lambda e: e.activation(out=xn[:], in_=pTr[:].rearrange("p a b -> p (a b)"), func=AF.Copy),
                     reads=['pTr'], writes=['xn'])
                yield
                for hf in range(2):
                    def fo(e, hf=hf):
                        inst = None
                        for j in range(8):
                            inst = e.matmul(pb_[0][:, :], lhsT=mixT[:, j, :], rhs=wout[:, j, hf * 512:(hf + 1) * 512],
                                            start=(j == 0), stop=(j == 7))
                        return inst
                    P.op('pe', fo, reads=['xn', 'wout'], writes=['pb0'])
                    yield
                    P.op('dve', lambda e, hf=hf: e.scalar_tensor_tensor(
                        out=xt[:, hf * 512:(hf + 1) * 512], in0=xt[:, hf * 512:(hf + 1) * 512], scalar=ALPHA,
                        in1=pb_[0][:, :], op0=ALU.mult, op1=ALU.add), reads=['pb0', 'xt'], writes=['xt'])
                    yield
                yield from ln_stats(xt, 'xt', 0)
                P.op('act', lambda e, x1t=x1t: e.activation(out=x1t, in_=xt[:], func=AF.Identity, scale=rs[0][:, 0:1],
                                                            bias=rs[0][:, 1:2]),
                     reads=['xt', 'rsa0', 'rsb0'], writes=[xname])
                yield
                P.op('dve', lambda e, x1t=x1t: e.tensor_tensor(out=x1t, in0=x1t, in1=l1g[:], op=ALU.mult),
                     reads=[xname, 'l1g'], writes=[xname])
                yield
                P.op('dve', lambda e, x1t=x1t: e.tensor_tensor(out=x1t, in0=x1t, in1=l1b[:], op=ALU.add),
                     reads=[xname, 'l1b'], writes=[xname])
                yield
                yield from ln_stats(x1t, xname, 0)
                P.op('act', lambda e, x1t=x1t: e.activation(out=xn[:], in_=x1t, func=AF.Identity, scale=rs[0][:, 0:1],
                                                            bias=rs[0][:, 1:2]),
                     reads=[xname, 'rsa0', 'rsb0'], writes=['xn'])
                yield

                def ftr2(e):
                    inst = None
                    for j in range(8):
                        inst = e.transpose(out=pTr[:, j, :], in_=xn[:, j * 128:(j + 1) * 128], identity=ident[:])
                    return inst
                P.op('pe', ftr2, reads=['xn', 'ident'], writes=['pTr'])
                yield
                for j in range(8):
                    P.op('act', lambda e, j=j, t=t, hp=hp: e.activation(out=h2T[hp][:, j, t * 128:(t + 1) * 128],
                                                                       in_=pTr[:, j, :], func=AF.Identity,
                                                                       scale=sc2p[:, j:j + 1],
                                                                       bias=modT[:, 24 + j:25 + j]),
                         reads=['pTr', 'sc2p', 'modT3'], writes=[f'h2T{hp}_{j}'])
                    if j % 2 == 1:
                        yield

        def ffn(g):
            hp = g % 2
            hn = [f'h2T{hp}_{j}' for j in range(8)]
            for fh in range(4):
                for fc in range(8):
                    f_ = fh * 8 + fc
                    k = 1 + fc % 2

                    def f1(e, f_=f_, k=k):
                        inst = None
                        for j in range(8):
                            inst = e.matmul(pb_[k][:, 0:256], lhsT=wff1[:, j, f_ * 128:(f_ + 1) * 128],
                                            rhs=h2T[hp][:, j, :], start=(j == 0), stop=(j == 7))
                        return inst
                    P.op('pe', f1, reads=hn + ['wff1'], writes=[f'pb{k}'])
                    P.op('act', lambda e, f_=f_, k=k: e.activation(out=rt[k - 1][:], in_=pb_[k][:, 0:256], func=AF.Relu,
                                                                   bias=b1T[:, f_:f_ + 1]),
                         reads=[f'pb{k}', 'b1T'], writes=[f'rt{k}'])
                    sq_eng = 'dve'
                    P.op(sq_eng, lambda e, fc=fc, k=k: e.tensor_tensor(out=uT[:, fc, :], in0=rt[k - 1][:],
                                                                      in1=rt[k - 1][:], op=ALU.mult),
                         reads=[f'rt{k}'], writes=[f'uT{fc}'])
                    yield
                un = [f'uT{fc}' for fc in range(8)]
                for t in range(2):
                    for ch in range(2):
                        bk = 3 + t * 2 + ch

                        def f2(e, t=t, ch=ch, bk=bk, fh=fh):
                            inst = None
                            for fc in range(8):
                                f_ = fh * 8 + fc
                                inst = e.matmul(pb_[bk][:, :], lhsT=uT[:, fc, t * 128:(t + 1) * 128],
                                                rhs=wff2[:, f_, ch * 512:(ch + 1) * 512],
                                                start=(f_ == 0), stop=False, skip_group_check=True)
                            if fh == 3:
                                inst = e.matmul(pb_[bk][:, :], lhsT=ones_b[0:1, :],
                                                rhs=bff2_b[0:1, ch * 512:(ch + 1) * 512],
                                                start=False, stop=True, skip_group_check=True)
                            return inst
                        P.op('pe', f2, reads=un + w2n + ['ones_b', 'bff2_b'], writes=[f'pb{bk}'])
                        yield

        def epilogue(g):
            for t in range(2):
                r0 = (2 * g + t) * 128
                xi = (2 * g + t) % 4
                x1t = x1[:, xi, :]
                xname = f'x1_{xi}'
                for ch in range(2):
                    bk = 3 + t * 2 + ch
                    P.op('dve', lambda e, ch=ch, bk=bk, xi=xi: e.scalar_tensor_tensor(
                        out=x1[:, xi, ch * 512:(ch + 1) * 512], in0=x1[:, xi, ch * 512:(ch + 1) * 512], scalar=ALPHA,
                        in1=pb_[bk][:, :], op0=ALU.mult, op1=ALU.add), reads=[f'pb{bk}', xname], writes=[xname])
                    yield
                yield from ln_stats(x1t, xname, 1)
                P.op('act', lambda e, x1t=x1t: e.activation(out=x1t, in_=x1t, func=AF.Identity, scale=rs[1][:, 0:1],
                                                            bias=rs[1][:, 1:2]),
                     reads=[xname, 'rsa1', 'rsb1'], writes=[xname])
                yield
                P.op('pool', lambda e, x1t=x1t: e.tensor_tensor(out=x1t, in0=x1t, in1=l2g[:], op=ALU.mult),
                     reads=[xname, 'l2g'], writes=[xname])
                yield
                P.op('pool', lambda e, x1t=x1t: e.tensor_tensor(out=x1t, in0=x1t, in1=l2b[:], op=ALU.add),
                     reads=[xname, 'l2b'], writes=[xname])
                yield
                P.op('sp', lambda e, r0=r0, x1t=x1t: e.dma_start(out=out[r0:r0 + 128, :], in_=x1t),
                     reads=[xname], writes=['out'], dsem=f'D_out{xi}')
                yield

        def interleave(*gens):
            gens = list(gens)
            while gens:
                for gen in list(gens):
                    try:
                        next(gen)
                    except StopIteration:
                        gens.remove(gen)

        interleave(prologue(0))
        for g in range(NGB):
            if g + 1 < NGB:
                interleave(ffn(g), prologue(g + 1))
            else:
                interleave(ffn(g))
            interleave(epilogue(g))
        flush()
    psB.__exit__(None, None, None)
    if not fusedB:
        es.__exit__(None, None, None)
    return nc


def _consts(N1):
    S = 128 * N1
    i = np.arange(128)
    ident = np.eye(128, dtype=np.float32)
    le = (i[:, None] <= i[None, :]).astype(np.float32)
    ge = (i[:, None] >= i[None, :]).astype(np.float32)
    ang = 2 * np.pi * np.outer(i, i) / 128.0
    c128, s128 = np.cos(ang), np.sin(ang)
    a = np.arange(N1)
    angA = 2 * np.pi * np.outer(a, a) / N1
    cA, sA = np.cos(angA), np.sin(angA)
    angT = 2 * np.pi * np.outer(i, a) / S
    tc, ts = np.cos(angT), np.sin(angT)
    nrm = 1.0 / np.sqrt(S * 128.0)
    return {
        "c_ident": ident, "c_maskf": 0.125 * le, "c_maskb": 0.125 * ge, "c_ule": le, "c_uge": ge,
        "c_cs": np.concatenate([c128, s128], 1).astype(np.float32),
        "c_a1": np.concatenate([cA, sA], 1).astype(np.float32),
        "c_a2": np.concatenate([-sA, cA], 1).astype(np.float32),
        "c_tw": np.concatenate([tc, tc, ts, ts], 1).astype(np.float32),
        "c_cn": (np.concatenate([c128, -s128], 1) * nrm).astype(np.float32),
    }


def make_in_maps(inp, N1, fused=False):
    S = 128 * N1
    TQ = S // 4
    cst = _consts(N1)
    f = lambda a: np.ascontiguousarray(np.asarray(a, dtype=np.float32))
    maps = []
    for core in range(NCORES):
        b, g = core // 4, core % 4
        cols = np.concatenate([
            np.arange(64) + 64 * g,
            256 + np.arange(64) + 64 * g,
            512 + np.arange(128) + 128 * g,
            1024 + np.arange(128) + 128 * g,
            2048 + np.array([g, 4 + g, 8 + g, 12 + g]),
            1536 + np.arange(128) + 128 * g,
        ])
        m = {
            "xb": f(inp["x"][b]),
            "xq": f(inp["x"][b, g * TQ:(g + 1) * TQ]),
            "cT": f(inp["c"][b].reshape(8, 128).T),
            "w_ada": f(inp["w_ada"][0]),
            "b_adaT": f(inp["b_ada"][0].reshape(48, 128).T),
            "b_ada_row": f(inp["b_ada"][0].reshape(1, -1)),
            "w_in": f(inp["w_in"][0][:, cols]),
            "b_gate": f(inp["b_gate"][0][[g, 4 + g, 8 + g, 12 + g]].reshape(1, 4)),
            "nw": f(inp["mlstm_norm_w"][0][128 * g:128 * (g + 1)].reshape(1, 128)),
            "w_out": f(inp["w_out"][0]),
            "ln1_g": f(inp["ln1_g"][0].reshape(1, -1)), "ln1_b": f(inp["ln1_b"][0].reshape(1, -1)),
            "w_ff1": f(inp["w_ff1"][0]), "b_ff1T": f(inp["b_ff1"][0].reshape(32, 128).T),
            "w_ff2": f(inp["w_ff2"][0]), "b_ff2": f(inp["b_ff2"][0].reshape(1, -1)),
            "ln2_g": f(inp["ln2_g"][0].reshape(1, -1)), "ln2_b": f(inp["ln2_b"][0].reshape(1, -1)),
        }
        if fused:
            NT = TQ // 128
            NP_ = max(1, S // 2048)
            R_ = S // NP_
            gi = np.empty((128, 4 * NT), np.int32)
            for s_ in range(4):
                for T_ in range(NT):
                    n_ = g * TQ + T_ * 128 + np.arange(128)
                    gi[:, s_ * NT + T_] = (n_ // R_) * 4 * R_ + s_ * R_ + n_ % R_
            m["gidx"] = gi
        m.update(cst)
        maps.append(m)
    return maps


def _bf16_to_mixq(results, N1):
    S = 128 * N1
    TQ = S // 4
    mixqs = []
    for core in range(NCORES):
        b, r = core // 4, core % 4
        parts_m = [results[4 * b + g]["dbg"][r * TQ:(r + 1) * TQ, 0:128] for g in range(4)]
        parts_f = [results[4 * b + g]["dbg"][r * TQ:(r + 1) * TQ, 128:256] for g in range(4)]
        mixqs.append(np.ascontiguousarray(np.concatenate(parts_m + parts_f, axis=1)))
    return mixqs


def kernel(**inputs):
    N1 = 128
    S = 128 * N1
    TQ = S // 4
    maps = make_in_maps(inputs, N1, fused=True)
    nc = build_nc(N1, fused=True)
    res = run_bass_kernel_spmd(nc, maps, core_ids=list(range(NCORES)))
    outp = np.empty((2, S, D), np.float32)
    for core in range(NCORES):
        b, g = core // 4, core % 4
        outp[b, g * TQ:(g + 1) * TQ] = np.asarray(res.results[core]["out"], dtype=np.float32)
    return outp
```

```python
import contextlib
import numpy as np
import concourse.bass as bass
import concourse.mybir as mybir
from concourse.bass_utils import run_bass_kernel_spmd

F32 = mybir.dt.float32
BF16 = mybir.dt.bfloat16
AF = mybir.ActivationFunctionType
ALU = mybir.AluOpType

D = 1024
DFF = 4096
LN_EPS = 1e-5
ALPHA = 2.0 ** 0.25
NCORES = 8


class Prog:
    ENG = ('pe', 'act', 'dve', 'pool', 'sp')

    def __init__(self):
        self.streams = {e: [] for e in self.ENG}
        self.cnt = {}
        self.lastw = {}
        self.rd = {}
        self.waited = {e: {} for e in self.ENG}
        self.semnames = ['S_' + e for e in self.ENG]
        self.psum_names = set()
        self.pacc = {}
        self._rec = None

    def record(self, body):
        assert self._rec is None
        self._rec = []
        body()
        r = self._rec
        self._rec = None
        return r

    def replay(self, *lists):
        lists = [l for l in lists if l]
        pos = [0] * len(lists)
        total = sum(len(l) for l in lists)
        for _ in range(total):
            best = None
            for i, l in enumerate(lists):
                if pos[i] < len(l):
                    frac = pos[i] / len(l)
                    if best is None or frac < best[0]:
                        best = (frac, i)
            i = best[1]
            self.op(*lists[i][pos[i]])
            pos[i] += 1

    def op(self, eng, fn, reads=(), writes=(), dsem=None, inc=None):
        if self._rec is not None:
            self._rec.append((eng, fn, tuple(reads), tuple(writes), dsem, inc))
            return None
        d = {}

        def add(t):
            if t is None:
                return
            s, v = t
            if d.get(s, 0) < v:
                d[s] = v
        for r in reads:
            add(self.lastw.get(r))
        for w in writes:
            add(self.lastw.get(w))
            for s, v in self.rd.get(w, {}).items():
                add((s, v))
        for n in list(reads) + list(writes):
            if n in self.psum_names:
                for e2, t2 in self.pacc.get(n, {}).items():
                    if e2 != eng:
                        add(t2)
        waits = []
        for s, v in d.items():
            if eng == 'pe' and s == 'S_pe':
                continue
            if self.waited[eng].get(s, 0) >= v:
                continue
            self.waited[eng][s] = v
            waits.append((s, v))
        if dsem is None:
            s = 'S_' + eng
            inc = 1
        else:
            s = dsem
            inc = 16 if inc is None else inc
            if s not in self.semnames:
                self.semnames.append(s)
        self.cnt[s] = self.cnt.get(s, 0) + inc
        t = (s, self.cnt[s])
        for r in reads:
            m = self.rd.setdefault(r, {})
            if m.get(s, 0) < t[1]:
                m[s] = t[1]
        for w in writes:
            self.lastw[w] = t
            self.rd[w] = {}
        for n in list(reads) + list(writes):
            if n in self.psum_names:
                self.pacc.setdefault(n, {})[eng] = t
        self.streams[eng].append((waits, fn, s, inc))
        return t

    def barrier_all(self, eng):
        waits = []
        for s, v in self.cnt.items():
            if s == 'CC':
                continue
            if self.waited[eng].get(s, 0) >= v:
                continue
            self.waited[eng][s] = v
            waits.append((s, v))
        self.streams[eng].append((waits, None, None, 0))

    def emit(self, nc, sems, block):
        def run(engname):
            def body(eng):
                for waits, fn, s, inc in self.streams[engname]:
                    for ws, wv in waits:
                        eng.wait_ge(sems[ws], wv)
                    if fn is None:
                        continue
                    inst = fn(eng)
                    inst.then_inc(sems[s], inc)
            return body
        block.tensor(run('pe'))
        block.scalar(run('act'))
        block.vector(run('dve'))
        block.gpsimd(run('pool'))
        block.sync(run('sp'))


def build_nc(N1, stop_after=None, fused=False):
    S = 128 * N1
    TQ = S // 4
    nc = bass.Bass("TRN2", target_bir_lowering=False)
    P = Prog()

    def din(name, shape, dt=F32):
        return nc.dram_tensor(name, shape, dt, kind="ExternalInput").ap()

    xb = din("xb", [S, D])
    cT = din("cT", [128, 8])
    w_ada = din("w_ada", [D, 6 * D])
    b_adaT = din("b_adaT", [128, 48])
    w_in = din("w_in", [D, 516])
    b_gate = din("b_gate", [1, 4])
    nw = din("nw", [1, 128])
    c_ident = din("c_ident", [128, 128])
    c_maskf = din("c_maskf", [128, 128])
    c_maskb = din("c_maskb", [128, 128])
    c_ule = din("c_ule", [128, 128])
    c_uge = din("c_uge", [128, 128])
    c_cs = din("c_cs", [128, 256])
    c_a1 = din("c_a1", [N1, 2 * N1])
    c_a2 = din("c_a2", [N1, 2 * N1])
    c_tw = din("c_tw", [128, 4 * N1])
    c_cn = din("c_cn", [128, 256])

    dbg = None
    if not fused:
        dbg = nc.dram_tensor("dbg", [S, 256], BF16, kind="ExternalOutput").ap()
    else:
        xq = din("xq", [TQ, D])
        b_ada_row = din("b_ada_row", [1, 6 * D])
        w_out = din("w_out", [D, D])
        ln1_g = din("ln1_g", [1, D]); ln1_b = din("ln1_b", [1, D])
        w_ff1 = din("w_ff1", [D, DFF]); b_ff1T = din("b_ff1T", [128, 32])
        w_ff2 = din("w_ff2", [DFF, D]); b_ff2 = din("b_ff2", [1, D])
        ln2_g = din("ln2_g", [1, D]); ln2_b = din("ln2_b", [1, D])
        gidx = din("gidx", [128, 4 * (TQ // 128)], mybir.dt.int32)
        out = nc.dram_tensor("out", [TQ, D], F32, kind="ExternalOutput").ap()

    so_d = nc.dram_tensor("so_d", [S, 128], F32).ap()
    hf_d = nc.dram_tensor("hf_d", [S, 128], F32).ap()
    mixg_m = nc.dram_tensor("mixg_m", [S, 128], BF16)
    mixg_f = nc.dram_tensor("mixg_f", [S, 128], BF16)
    gath_m = nc.dram_tensor("gath_m", [4 * S, 128], BF16)
    gath_f = nc.dram_tensor("gath_f", [4 * S, 128], BF16)
    NP_ = max(1, S // 2048)
    R_ = S // NP_
    CPP = R_ // 128

    import contextlib
    es = contextlib.ExitStack()
    with es:
        cur = [es]
        sems = {}

        def sb(name, shape, dt=F32):
            return cur[-1].enter_context(nc.sbuf_tensor(name, shape, dt))

        def flush():
            P.barrier_all('sp')
            for sname in P.semnames:
                if sname not in sems:
                    sems[sname] = es.enter_context(nc.semaphore(sname))
            with nc.Block() as block:
                P.emit(nc, sems, block)
            for e_ in P.ENG:
                P.streams[e_] = []
            for e_ in P.ENG:
                P.barrier_all(e_)

        psA = contextlib.ExitStack()
        psA.__enter__()

        def ps(name, shape, dt=F32):
            return psA.enter_context(nc.psum_tensor(name, shape, dt))

        ident = sb("ident", [128, 128], BF16)
        ones = sb("ones", [128, 128], F32)
        modT = sb("modT", [128, 48], F32)
        sc2p = sb("sc2p", [128, 8], F32)
        epsT = sb("epsT", [128, 1], F32)
        sGA = contextlib.ExitStack(); sGA.__enter__(); cur.append(sGA)
        maskf = sb("maskf", [128, 128], F32)
        maskb = sb("maskb", [128, 128], F32)
        ule = sb("ule", [128, 128], F32)
        uge = sb("uge", [128, 128], F32)
        bg = sb("bg", [128, 4], F32)
        nwb = sb("nwb", [128, 128], F32)
        cact = sb("cact", [128, 8], F32)
        cact_b = sb("cact_b", [128, 8], BF16)
        badaT = sb("badaT", [128, 48], F32)
        sc1p = sb("sc1p", [128, 8], F32)
        w_sb = sb("w_sb", [128, 8, 516], BF16)

        def ld(eng, dst, src, name, sem, reads=()):
            P.op(eng, lambda e: e.dma_start(out=dst, in_=src), reads=reads, writes=[name], dsem=sem)

        ld('pool', ident[:], c_ident, 'ident', 'D_c0')
        ld('sp', maskf[:], c_maskf, 'maskf', 'D_c1')
        ld('sp', maskb[:], c_maskb, 'maskb', 'D_c2')
        ld('sp', ule[:], c_ule, 'ule', 'D_c3')
        ld('sp', uge[:], c_uge, 'uge', 'D_c4')
        ld('sp', bg[:], b_gate.broadcast_to([128, 4]), 'bg', 'D_c5')
        ld('sp', nwb[:], nw.broadcast_to([128, 128]), 'nwb', 'D_c6')
        ld('sp', cact[:], cT, 'cact', 'D_c7')
        ld('sp', badaT[:], b_adaT, 'badaT', 'D_c8')
        ld('pool', w_sb[:], w_in.rearrange("(j p) n -> p j n", p=128), 'w_sb', 'D_c9')
        P.op('pool', lambda e: e.memset(ones[:], 1.0), writes=['ones'])
        P.op('pool', lambda e: e.memset(epsT[:], LN_EPS), writes=['epsT'])

        def finish():
            flush()
            return nc

        if stop_after == 'C':
            return finish()
        P.op('act', lambda e: e.activation(out=cact[:], in_=cact[:], func=AF.Silu), reads=['cact'], writes=['cact'])
        s0 = contextlib.ExitStack()
        s0.__enter__()
        cur.append(s0)
        wa = [sb(f"wa{i}", [128, 8, 1024], F32) for i in range(2)]
        if fused:
            g1b = sb("g1b", [128, 1024], F32)
            g2b = sb("g2b", [128, 1024], F32)
            g1row = nc.dram_tensor("g1row", [1, 1024], F32).ap()
            g2row = nc.dram_tensor("g2row", [1, 1024], F32).ap()
        pT = ps("pT", [128, 2, 4, 2, 128], BF16)
        pFZ = ps("pFZ", [128, 512], F32)
        pTMb = ps("pTM", [128, 512], F32)
        pTM = [pTMb, pTMb]
        pQ = ps("pQ", [128, 1024], BF16)
        pS = ps("pS", [128, 512], F32)
        pKV = ps("pKV", [128, 512], F32)
        pND = ps("pND", [128, 512], F32)
        P.psum_names |= {'pTa', 'pTb', 'pFZ', 'pTM', 'pQ', 'pS', 'pKV', 'pND'}
        pmod = pFZ
        pmod2 = pTM[0]
        crep = sb("crep", [128, 8, 128], F32)
        def mk_crep(e):
            i = None
            for j in range(8):
                i = e.tensor_scalar(out=crep[:, j, :], in0=ones[:], scalar1=cact[:, j:j + 1], scalar2=None,
                                    op0=ALU.mult)
            return i
        P.op('dve', mk_crep, reads=['cact', 'ones'], writes=['crep'])

        def load_wa(i, part):
            ld('sp', wa[i][:], w_ada[:, part * 1024:(part + 1) * 1024].rearrange("(j p) n -> p j n", p=128),
               f'wa{i}', f'D_wa{i}')

        def mod_featmajor(i, part):
            def f(e):
                inst = None
                for m in range(8):
                    for j in range(8):
                        inst = e.matmul(pmod[:, m:m + 1], lhsT=wa[i][:, j, m * 128:(m + 1) * 128],
                                        rhs=cact[:, j:j + 1], start=(j == 0), stop=(j == 7))
                return inst
            P.op('pe', f, reads=[f'wa{i}', 'cact'], writes=['pFZ'])
            P.op('dve', lambda e: e.tensor_tensor(out=modT[:, part * 8:(part + 1) * 8], in0=pmod[:, 0:8],
                                                  in1=badaT[:, part * 8:(part + 1) * 8], op=ALU.add),
                 reads=['pFZ', 'badaT'], writes=[f'modT{part}'])

        def mod_rowbcast(i, part, dst, dname):
            ld('sp', dst[:], b_ada_row[:, part * 1024:(part + 1) * 1024].broadcast_to([128, 1024]), dname,
               f'D_{dname}')
            for h in range(2):
                def f(e, h=h):
                    inst = None
                    for j in range(8):
                        inst = e.matmul(pmod2[:, :], lhsT=crep[:, j, :], rhs=wa[i][:, j, h * 512:(h + 1) * 512],
                                        start=(j == 0), stop=(j == 7))
                    return inst
                P.op('pe', f, reads=[f'wa{i}', 'crep'], writes=['pTM'])
                P.op('dve', lambda e, h=h: e.scalar_tensor_tensor(
                    out=dst[:, h * 512:(h + 1) * 512], in0=dst[:, h * 512:(h + 1) * 512], scalar=1.0,
                    in1=pmod2[:, :], op0=ALU.add, op1=ALU.add), reads=['pTM', dname], writes=[dname])

        load_wa(0, 0); load_wa(1, 1)
        mod_featmajor(0, 0)
        mod_featmajor(1, 1)
        P.op('dve', lambda e: e.tensor_scalar(out=sc1p[:], in0=modT[:, 8:16], scalar1=1.0, scalar2=None, op0=ALU.add),
             reads=['modT1'], writes=['sc1p'])
        if fused:
            load_wa(0, 2); load_wa(1, 3)
            mod_rowbcast(0, 2, g1b, 'g1b')
            mod_featmajor(1, 3)
            load_wa(0, 4); load_wa(1, 5)
            mod_featmajor(0, 4)
            P.op('dve', lambda e: e.tensor_scalar(out=sc2p[:], in0=modT[:, 32:40], scalar1=1.0, scalar2=None,
                                                  op0=ALU.add), reads=['modT4'], writes=['sc2p'])
            mod_rowbcast(1, 5, g2b, 'g2b')
            P.op('sp', lambda e: e.dma_start(out=g1row, in_=g1b[0:1, :]), reads=['g1b'], writes=['g1row'], dsem='D_g1r')
            P.op('sp', lambda e: e.dma_start(out=g2row, in_=g2b[0:1, :]), reads=['g2b'], writes=['g2row'], dsem='D_g2r')
        flush()
        cur.pop()
        s0.__exit__(None, None, None)
        sAF = contextlib.ExitStack(); sAF.__enter__(); cur.append(sAF)
        fzT = sb("fzT", [128, S], BF16)
        sA = contextlib.ExitStack(); sA.__enter__(); cur.append(sA)
        if stop_after == 'P0':
            return finish()
        NG = N1 // 2
        xt = [sb(f"xt{i}", [128, 1024], F32) for i in range(4)]
        xn = [sb(f"xn{i}", [128, 1024], BF16) for i in range(2)]
        bst = [sb(f"bst{i}", [128, 2, 6], F32) for i in range(2)]
        mv = [sb(f"mv{i}", [128, 2], F32) for i in range(2)]
        rs = [sb(f"rs{i}", [128, 2], F32) for i in range(2)]
        hT = [sb(f"hT{i}", [128, 8, 256], BF16) for i in range(2)]
        qk_tm = sb("qk_tm", [128, N1, 128], BF16)
        vp = sb("vp", [128, N1, 130], BF16)
        gt = sb("gt", [128, N1, 4], F32)
        scal = sb("scal", [128, NG, 16], F32)
        so_t = [sb(f"so{i}", [128, 128], F32) for i in range(2)]
        hf_t = [sb(f"hf{i}", [128, 128], F32) for i in range(2)]
        qkT = [sb(f"qkT{i}", [64, 2, 128], BF16) for i in range(2)]
        PTs = [sb(f"PTs{i}", [128, 128], BF16) for i in range(2)]
        ku = [sb(f"ku{i}", [128, 64], BF16) for i in range(2)]
        Dst = sb("Dst", [64, 129], F32)
        Cb = [sb(f"Cb{i}", [64, 129], BF16) for i in range(2)]
        ee = [sb(f"ee{i}", [128, 2, 2], F32) for i in range(2)]
        sp_ = [sb(f"sp{i}", [128, 2, 2], F32) for i in range(2)]
        ein = [sb(f"ein{i}", [128, 4, 4], F32) for i in range(2)]
        dd = [sb(f"dd{i}", [128, 2], F32) for i in range(2)]

        P.op('pool', lambda e: e.memset(vp[:, :, 128:130], 1.0), writes=['vp_ones'])

        def ln_stats(xtile, xname, k, par):
            def f(e):
                e.bn_stats(out=bst[par][:, 0, :], in_=xtile[:, 0:512])
                return e.bn_stats(out=bst[par][:, 1, :], in_=xtile[:, 512:1024])
            P.op('dve', f, reads=[xname], writes=[f'bst{par}'])
            P.op('dve', lambda e: e.bn_aggr(out=mv[par][:], in_=bst[par][:].rearrange("p a b -> p (a b)")),
                 reads=[f'bst{par}'], writes=[f'mv{par}'])
            P.op('act', lambda e: e.activation(out=rs[par][:, 0:1], in_=mv[par][:, 1:2], func=AF.Sqrt,
                                               bias=epsT[:, 0:1]),
                 reads=[f'mv{par}', 'epsT'], writes=[f'rs{par}a'])
            P.op('dve', lambda e: e.reciprocal(out=rs[par][:, 0:1], in_=rs[par][:, 0:1]),
                 reads=[f'rs{par}a'], writes=[f'rs{par}a'])
            P.op('dve', lambda e: e.tensor_scalar(out=rs[par][:, 1:2], in0=mv[par][:, 0:1], scalar1=rs[par][:, 0:1],
                                                  scalar2=-1.0, op0=ALU.mult, op1=ALU.mult),
                 reads=[f'rs{par}a', f'mv{par}'], writes=[f'rs{par}b'])

        def gate_scalars(g):
            gp = g % 2
            c0 = 2 * g
            def f1(e):
                e.activation(out=ee[gp][:, 0, :], in_=gt[:, c0:c0 + 2, 1], func=AF.Exp, scale=-1.0)
                return e.activation(out=ee[gp][:, 1, :], in_=gt[:, c0:c0 + 2, 3], func=AF.Exp, scale=-1.0)
            P.op('act', f1, reads=[f'gt{c0}', f'gt{c0 + 1}'], writes=[f'ee{gp}'])
            P.op('act', lambda e: e.activation(out=sp_[gp][:].rearrange("p a b -> p (a b)"),
                                               in_=ee[gp][:].rearrange("p a b -> p (a b)"), func=AF.Ln, bias=1.0),
                 reads=[f'ee{gp}'], writes=[f'sp{gp}'])
            def f2(e):
                e.matmul(pKV[:, 384:386], lhsT=ule[:], rhs=sp_[gp][:, 0, :], start=True, stop=True)
                e.matmul(pKV[:, 386:388], lhsT=uge[:], rhs=sp_[gp][:, 1, :], start=True, stop=True)
                return e.matmul(pKV[:, 388:392], lhsT=ones[:], rhs=sp_[gp][:].rearrange("p a b -> p (a b)"),
                                start=True, stop=True)
            P.op('pe', f2, reads=[f'sp{gp}', 'ule', 'uge', 'ones'], writes=['pKV'])
            def f3(e):
                e.tensor_tensor(out=ein[gp][:, 0, 0:2], in0=gt[:, c0:c0 + 2, 0], in1=pKV[:, 384:386], op=ALU.add)
                e.tensor_tensor(out=ein[gp][:, 0, 2:4], in0=gt[:, c0:c0 + 2, 2], in1=pKV[:, 386:388], op=ALU.add)
                e.tensor_copy(out=ein[gp][:, 1, :], in_=pKV[:, 384:388])
                return e.tensor_scalar(out=ein[gp][:, 3, :], in0=pKV[:, 388:392], scalar1=-1.0, scalar2=None,
                                       op0=ALU.mult)
            P.op('dve', f3, reads=['pKV', f'gt{c0}', f'gt{c0 + 1}'], writes=[f'ein{gp}a'])
            P.op('dve', lambda e: e.tensor_tensor(out=ein[gp][:, 2, :], in0=ein[gp][:, 0, :], in1=pKV[:, 388:392],
                                                  op=ALU.subtract),
                 reads=['pKV', f'ein{gp}a'], writes=[f'ein{gp}'])
            P.op('act', lambda e: e.activation(out=scal[:, g, :], in_=ein[gp][:].rearrange("p a b -> p (a b)"),
                                               func=AF.Exp),
                 reads=[f'ein{gp}', f'ein{gp}a'], writes=[f'scal{g}'])

        def sc_ap(g, k, d, ci, rows=128):
            i = k * 4 + d * 2 + ci
            return scal[0:rows, g, i:i + 1]

        def mlstm_chunk(c, d, part='both'):
            g = c // 2
            ci = c % 2
            par = c % 2
            mask, mname = (maskf, 'maskf') if d == 0 else (maskb, 'maskb')
            kv = pKV[0:64, par * 129:(par + 1) * 129]
            kvn = 'pKV'
            nd = pND[:, 0:129]
            cbi = c % 2
            if part in ('front', 'both'):
                mlstm_front(c, d, g, ci, par, mask, mname, kv, kvn)
            if part in ('tail', 'both'):
                mlstm_tail(c, d, g, ci, par, kv, kvn, nd, cbi)
            return nd, par

        def mlstm_front(c, d, g, ci, par, mask, mname, kv, kvn):
            def ftr(e):
                e.transpose(out=pQ[0:64, 0:128], in_=qk_tm[:, c, 0:64], identity=ident[:])
                return e.transpose(out=pQ[0:64, 128:256], in_=qk_tm[:, c, 64:128],
                                   identity=ident[:])
            P.op('pe', ftr, reads=[f'qk{c}', 'ident'], writes=['pQ'])
            P.op('act', lambda e: e.activation(out=qkT[par][:].rearrange("p a b -> p (a b)"),
                                               in_=pQ[0:64, 0:256], func=AF.Copy),
                 reads=['pQ'], writes=[f'qkT{par}'])
            P.op('pe', lambda e: e.matmul(pS[:, 0:128], lhsT=qkT[par][:, 1, :],
                                          rhs=qkT[par][:, 0, :], start=True, stop=True),
                 reads=[f'qkT{par}'], writes=['pS'])
            P.op('dve', lambda e: e.scalar_tensor_tensor(out=PTs[par][:], in0=pS[:, 0:128],
                                                         scalar=sc_ap(g, 0, d, ci), in1=mask[:], op0=ALU.mult,
                                                         op1=ALU.mult),
                 reads=['pS', f'scal{g}', mname], writes=[f'PTs{par}'])
            P.op('pool', lambda e: e.tensor_scalar(out=ku[par][:], in0=qk_tm[:, c, 64:128], scalar1=sc_ap(g, 2, d, ci),
                                                   scalar2=None, op0=ALU.mult),
                 reads=[f'qk{c}', f'scal{g}'], writes=[f'ku{par}'])
            P.op('pe', lambda e: e.matmul(kv, lhsT=ku[par][:], rhs=vp[:, c, 0:129], start=True, stop=True),
                 reads=[f'ku{par}', f'vp{c}', 'vp_ones'], writes=[kvn])

        def mlstm_tail(c, d, g, ci, par, kv, kvn, nd, cbi):
            def fnd(e):
                e.matmul(nd, lhsT=PTs[par][:], rhs=vp[:, c, 0:129], start=True, stop=False)
                return e.matmul(nd, lhsT=qkT[par][:, 0, :], rhs=Cb[cbi][:], start=False, stop=True)
            P.op('pe', fnd, reads=[f'PTs{par}', f'vp{c}', 'vp_ones', f'qkT{par}', f'Cb{cbi}'], writes=['pND'])
            P.op('dve', lambda e: e.tensor_scalar(out=dd[par][:, 0:1], in0=nd[:, 128:129], scalar1=-1.0, scalar2=None,
                                                  op0=ALU.mult),
                 reads=['pND'], writes=[f'dd{par}'])
            P.op('dve', lambda e: e.tensor_tensor(out=dd[par][:, 0:1], in0=nd[:, 128:129], in1=dd[par][:, 0:1],
                                                  op=ALU.max),
                 reads=['pND', f'dd{par}'], writes=[f'dd{par}'])
            P.op('dve', lambda e: e.tensor_scalar(out=dd[par][:, 0:1], in0=dd[par][:, 0:1], scalar1=sc_ap(g, 1, d, ci),
                                                  scalar2=None, op0=ALU.max),
                 reads=[f'dd{par}', f'scal{g}'], writes=[f'dd{par}'])
            P.op('dve', lambda e: e.reciprocal(out=dd[par][:, 1:2], in_=dd[par][:, 0:1]),
                 reads=[f'dd{par}'], writes=[f'ddr{par}'])
            P.op('dve', lambda e: e.scalar_tensor_tensor(out=Dst[:], in0=Dst[:], scalar=sc_ap(g, 3, d, ci, 64),
                                                         in1=kv, op0=ALU.mult, op1=ALU.add),
                 reads=[kvn, 'Dst', f'scal{g}'], writes=['Dst'])
            P.op('act', lambda e: e.activation(out=Cb[1 - cbi][:], in_=Dst[:], func=AF.Copy, scale=0.125),
                 reads=['Dst'], writes=[f'Cb{1 - cbi}'])

        def stageA(g):
            gp = g % 2
            for t in range(2):
                c = 2 * g + t
                xi = c % 4
                ld('sp', xt[xi][:], xb[c * 128:(c + 1) * 128, :], f'xt{xi}', f'D_x{xi}')
                ln_stats(xt[xi], f'xt{xi}', c, t)
                P.op('act', lambda e, xi=xi, t=t: e.activation(out=xn[t][:], in_=xt[xi][:], func=AF.Identity,
                                                               scale=rs[t][:, 0:1], bias=rs[t][:, 1:2]),
                     reads=[f'xt{xi}', f'rs{t}a', f'rs{t}b'], writes=[f'xn{t}'])
                def ftr(e, t=t):
                    inst = None
                    for j in range(8):
                        inst = e.transpose(out=pT[:, j // 4, j % 4, t, :], in_=xn[t][:, j * 128:(j + 1) * 128],
                                           identity=ident[:])
                    return inst
                P.op('pe', ftr, reads=[f'xn{t}', 'ident'], writes=['pTa', 'pTb'])
            for j in range(8):
                src = pT[:, j // 4, j % 4, :, :].rearrange("p a b -> p (a b)")
                dst = hT[gp][:, j, :]
                if j < 4:
                    P.op('act', lambda e, src=src, dst=dst, j=j: e.activation(out=dst, in_=src, func=AF.Identity,
                                                                              scale=sc1p[:, j:j + 1],
                                                                              bias=modT[:, j:j + 1]),
                         reads=['pTa', 'sc1p', 'modT0'], writes=[f'hT{gp}_{j}'])
                else:
                    P.op('dve', lambda e, src=src, dst=dst, j=j: e.tensor_scalar(out=dst, in0=src,
                                                                                 scalar1=sc1p[:, j:j + 1],
                                                                                 scalar2=modT[:, j:j + 1],
                                                                                 op0=ALU.mult, op1=ALU.add),
                         reads=['pTb', 'sc1p', 'modT0'], writes=[f'hT{gp}_{j}'])
            hnames = [f'hT{gp}_{j}' for j in range(8)]
            for t in range(2):
                c = 2 * g + t
                def ftm(e, t=t, gp=gp):
                    inst = None
                    for j in range(8):
                        inst = e.matmul(pTM[t][:, 0:388], lhsT=hT[gp][:, j, t * 128:(t + 1) * 128],
                                        rhs=w_sb[:, j, 0:388], start=(j == 0), stop=(j == 7))
                    return inst
                P.op('pe', ftm, reads=hnames + ['w_sb'], writes=['pTM'])
                P.op('act', lambda e, t=t, c=c: e.activation(out=vp[:, c, 0:128], in_=pTM[t][:, 128:256], func=AF.Copy),
                     reads=['pTM'], writes=[f'vp{c}'])
                P.op('act', lambda e, t=t: e.activation(out=so_t[t][:], in_=pTM[t][:, 256:384], func=AF.Sigmoid),
                     reads=['pTM'], writes=[f'so{t}'])
                P.op('dve', lambda e, t=t, c=c: e.tensor_copy(out=qk_tm[:, c, :], in_=pTM[t][:, 0:128]),
                     reads=['pTM'], writes=[f'qk{c}'])
                P.op('dve', lambda e, t=t, c=c: e.tensor_tensor(out=gt[:, c, :], in0=pTM[t][:, 384:388], in1=bg[:],
                                                                op=ALU.add),
                     reads=['pTM', 'bg'], writes=[f'gt{c}'])
                P.op('pool', lambda e, t=t, c=c: e.dma_start(out=so_d[c * 128:(c + 1) * 128, :], in_=so_t[t][:]),
                     reads=[f'so{t}'], writes=[f'so_d{c}'], dsem=f'D_so{t}')
                if t == 0:
                    def ffz(e, gp=gp):
                        inst = None
                        for j in range(8):
                            inst = e.matmul(pFZ[:, 0:256], lhsT=w_sb[:, j, 388:516], rhs=hT[gp][:, j, :],
                                            start=(j == 0), stop=(j == 7))
                        return inst
                    P.op('pe', ffz, reads=hnames + ['w_sb'], writes=['pFZ'])
                    P.op('act', lambda e, g=g: e.activation(out=fzT[:, g * 256:(g + 1) * 256], in_=pFZ[:, 0:256],
                                                            func=AF.Copy),
                         reads=['pFZ'], writes=[f'fzT{g}'])
            gate_scalars(g)
        def hf_out(c, nd, par):
            P.op('act', lambda e: e.activation(out=hf_t[par][:], in_=nd[:, 0:128], func=AF.Copy, scale=dd[par][:, 1:2]),
                 reads=['pND', f'ddr{par}'], writes=[f'hf{par}'])
            P.op('pool', lambda e: e.dma_start(out=hf_d[c * 128:(c + 1) * 128, :], in_=hf_t[par][:]),
                 reads=[f'hf{par}'], writes=[f'hf_d{c}'], dsem=f'D_hf{par}')

        P.op('pool', lambda e: e.memset(Dst[:], 0.0), writes=['Dst'])
        P.op('pool', lambda e: e.memset(Cb[0][:], 0.0), writes=['Cb0'])
        P.replay(P.record(lambda: stageA(0)))
        for g in range(NG):
            c0 = 2 * g
            f0 = P.record(lambda: mlstm_chunk(c0, 0, 'front'))
            t0_ = P.record(lambda: (mlstm_chunk(c0, 0, 'tail'), hf_out(c0, pND[:, 0:129], c0 % 2)))
            f1 = P.record(lambda: mlstm_chunk(c0 + 1, 0, 'front'))
            t1_ = P.record(lambda: (mlstm_chunk(c0 + 1, 0, 'tail'), hf_out(c0 + 1, pND[:, 0:129], (c0 + 1) % 2)))
            nxt = P.record(lambda: stageA(g + 1)) if g + 1 < NG else []
            n3 = len(nxt) // 3
            P.replay(f0, nxt[:n3])
            P.replay(t0_, f1, nxt[n3:2 * n3])
            P.replay(t1_, nxt[2 * n3:])
        if stop_after == 'P1':
            return finish()
        hs = [sb(f"hs{i}", [128, 128], F32) for i in range(2)]
        hn = [sb(f"hn{i}", [128, 128], F32) for i in range(2)]
        ym = [sb(f"ym{i}", [128, 128], BF16) for i in range(2)]
        sol = [sb(f"sol{i}", [128, 128], F32) for i in range(2)]
        hfl = [sb(f"hfl{i}", [128, 128], F32) for i in range(2)]
        bs2 = [sb(f"bs2{i}", [128, 6], F32) for i in range(2)]
        mv2 = [sb(f"mv2{i}", [128, 2], F32) for i in range(2)]
        rs2 = [sb(f"rs2{i}", [128, 2], F32) for i in range(2)]

        P.op('pool', lambda e: e.memset(Dst[:], 0.0), reads=[], writes=['Dst'])
        cb_first = (N1 - 1) % 2
        P.op('pool', lambda e: e.memset(Cb[cb_first][:], 0.0), writes=[f'Cb{cb_first}'])
        mt_m = dbg[:, 0:128] if not fused else mixg_m.ap()
        mt_f = dbg[:, 128:256] if not fused else mixg_f.ap()
        def s2_front(c):
            par = c % 2
            ld('sp', sol[par][:], so_d[c * 128:(c + 1) * 128, :], f'sol{par}', f'D_sol{par}', reads=[f'so_d{c}'])
            ld('sp', hfl[par][:], hf_d[c * 128:(c + 1) * 128, :], f'hfl{par}', f'D_hfl{par}', reads=[f'hf_d{c}'])
            P.op('pool', lambda e, par=par: e.tensor_tensor(out=sol[par][:], in0=sol[par][:], in1=nwb[:], op=ALU.mult),
                 reads=[f'sol{par}', 'nwb'], writes=[f'sol{par}'])
            mlstm_chunk(c, 1, 'front')

        def s2_tail(c):
            par = c % 2
            nd, _ = mlstm_chunk(c, 1, 'tail')
            P.op('dve', lambda e, nd=nd, par=par: e.scalar_tensor_tensor(out=hs[par][:], in0=nd[:, 0:128],
                                                                         scalar=dd[par][:, 1:2], in1=hfl[par][:],
                                                                         op0=ALU.mult, op1=ALU.add),
                 reads=['pND', f'ddr{par}', f'hfl{par}'], writes=[f'hs{par}'])
            P.op('dve', lambda e, par=par: e.bn_stats(out=bs2[par][:], in_=hs[par][:]),
                 reads=[f'hs{par}'], writes=[f'bs2{par}'])
            P.op('dve', lambda e, par=par: e.bn_aggr(out=mv2[par][:], in_=bs2[par][:]),
                 reads=[f'bs2{par}'], writes=[f'mv2{par}'])
            P.op('act', lambda e, par=par: e.activation(out=rs2[par][:, 0:1], in_=mv2[par][:, 1:2], func=AF.Sqrt,
                                                        bias=epsT[:, 0:1]),
                 reads=[f'mv2{par}', 'epsT'], writes=[f'rs2{par}a'])
            P.op('dve', lambda e, par=par: e.reciprocal(out=rs2[par][:, 0:1], in_=rs2[par][:, 0:1]),
                 reads=[f'rs2{par}a'], writes=[f'rs2{par}a'])
            P.op('dve', lambda e, par=par: e.tensor_scalar(out=rs2[par][:, 1:2], in0=mv2[par][:, 0:1],
                                                           scalar1=rs2[par][:, 0:1], scalar2=-1.0, op0=ALU.mult,
                                                           op1=ALU.mult),
                 reads=[f'rs2{par}a', f'mv2{par}'], writes=[f'rs2{par}b'])
            P.op('act', lambda e, par=par: e.activation(out=hn[par][:], in_=hs[par][:], func=AF.Identity,
                                                        scale=rs2[par][:, 0:1], bias=rs2[par][:, 1:2]),
                 reads=[f'hs{par}', f'rs2{par}a', f'rs2{par}b'], writes=[f'hn{par}'])
            P.op('pool', lambda e, par=par: e.tensor_tensor(out=ym[par][:], in0=hn[par][:], in1=sol[par][:],
                                                            op=ALU.mult),
                 reads=[f'hn{par}', f'sol{par}'], writes=[f'ym{par}'])
            P.op('pool', lambda e, par=par, c=c: e.dma_start(out=mt_m[c * 128:(c + 1) * 128, :],
                                                             in_=ym[par][:]),
                 reads=[f'ym{par}'], writes=[f'mixg_m_c{c}'], dsem=f'D_ym{par}')

        P.replay(P.record(lambda: s2_front(N1 - 1)))
        for c in range(N1 - 1, -1, -1):
            tl = P.record(lambda: s2_tail(c))
            fr = P.record(lambda: s2_front(c - 1)) if c > 0 else []
            P.replay(tl, fr)

        if stop_after == 'P2':
            return finish()
        flush()
        cur.pop(); sA.__exit__(None, None, None)
        sF = contextlib.ExitStack(); sF.__enter__(); cur.append(sF)
        piecesA = [i_ for i_ in range(NP_) if i_ % 2 == 0]
        piecesB = [i_ for i_ in range(NP_) if i_ % 2 == 1]

        def issue_ag(src, dst, i_, reads, wname):
            P.op('pool', lambda e: e.collective_compute(
                "AllGather", ALU.bypass, replica_groups=[[0, 1, 2, 3], [4, 5, 6, 7]],
                ins=[src.ap()[i_ * R_:(i_ + 1) * R_, :].opt()],
                outs=[dst.ap()[i_ * 4 * R_:(i_ + 1) * 4 * R_, :].opt()]),
                reads=reads, writes=[wname], dsem='CC', inc=1)
        if fused:
            for i_ in piecesA:
                issue_ag(mixg_m, gath_m, i_, [f'mixg_m_c{cc}' for cc in range(i_ * CPP, (i_ + 1) * CPP)], 'gath_m_A')
        cs_b = sb("cs_b", [128, 256], BF16)
        a1_b = sb("a1_b", [N1, 2 * N1], BF16)
        a2_b = sb("a2_b", [N1, 2 * N1], BF16)
        cn_b = sb("cn_b", [128, 256], BF16)
        tw = sb("tw", [128, 4 * N1], F32)
        ld('pool', cs_b[:], c_cs, 'cs_b', 'D_f0')
        ld('pool', a1_b[:], c_a1, 'a1_b', 'D_f1')
        ld('pool', a2_b[:], c_a2, 'a2_b', 'D_f2')
        ld('pool', cn_b[:], c_cn, 'cn_b', 'D_f3')
        ld('sp', tw[:], c_tw, 'tw', 'D_f4')
        G = sb("G", [128, 128, 256], BF16)
        Yr = sb("Yr", [128, N1, 128], BF16)
        Qp = sb("Qp", [128, N1, 128], BF16)
        yf = G[:].rearrange("p a b -> p (a b)")[:, 0:N1 * 128].rearrange("p (a b) -> p a b", b=128)
        t1 = [sb(f"t1_{i}", [128, 2 * N1], F32) for i in range(2)]
        t2 = [sb(f"t2_{i}", [128, 2 * N1], F32) for i in range(2)]
        pG = [pFZ, pTMb, pS, pKV]
        pGn = ['pFZ', 'pTM', 'pS', 'pKV']
        fz_all = [f'fzT{g}' for g in range(NG)]
        for i in range(64):
            k = i % 4
            eng = 'act' if k < 2 else 'dve'
            def f0(e, i=i, k=k):
                inst = None
                for q in range(2):
                    n2 = 2 * i + q
                    inst = e.matmul(pG[k][0:N1, q * 256:(q + 1) * 256], lhsT=fzT[:, n2:S:128], rhs=cs_b[:],
                                    start=True, stop=True)
                return inst
            P.op('pe', f0, reads=fz_all + ['cs_b'], writes=[pGn[k]])
            dst = G[0:N1, 2 * i:2 * i + 2, :].rearrange("p a b -> p (a b)")
            if eng == 'act':
                P.op('act', lambda e, k=k, dst=dst: e.activation(out=dst, in_=pG[k][0:N1, :], func=AF.Copy),
                     reads=[pGn[k]], writes=[f'G{i}'])
            else:
                P.op('dve', lambda e, k=k, dst=dst: e.tensor_copy(out=dst, in_=pG[k][0:N1, :]),
                     reads=[pGn[k]], writes=[f'G{i}'])
        g_all = [f'G{i}' for i in range(64)]
        pY = [pFZ, pTMb]
        pYn = ['pFZ', 'pTM']
        for j in range(128):
            k = j % 2
            def fa(e, j=j, k=k):
                e.matmul(pY[k][:, 0:2 * N1], lhsT=G[0:N1, :, j], rhs=a1_b[:], start=True, stop=False)
                return e.matmul(pY[k][:, 0:2 * N1], lhsT=G[0:N1, :, 128 + j], rhs=a2_b[:], start=False, stop=True)
            P.op('pe', fa, reads=g_all + ['a1_b', 'a2_b'], writes=[pYn[k]])
            P.op('dve', lambda e, k=k: e.tensor_tensor(out=t1[k][:], in0=pY[k][:, 0:2 * N1], in1=tw[:, 0:2 * N1],
                                                       op=ALU.mult),
                 reads=[pYn[k], 'tw'], writes=[f't1_{k}'])
            P.op('dve', lambda e, k=k: e.tensor_tensor(out=t2[k][:], in0=pY[k][:, 0:2 * N1],
                                                       in1=tw[:, 2 * N1:4 * N1], op=ALU.mult),
                 reads=[pYn[k], 'tw'], writes=[f't2_{k}'])
            P.op('pool', lambda e, k=k, j=j: e.tensor_tensor(out=Yr[:, :, j], in0=t1[k][:, 0:N1],
                                                             in1=t2[k][:, N1:2 * N1], op=ALU.subtract),
                 reads=[f't1_{k}', f't2_{k}'], writes=[f'Yr{j}'])
            P.op('pool', lambda e, k=k, j=j: e.tensor_tensor(out=Qp[:, :, j], in0=t1[k][:, N1:2 * N1],
                                                             in1=t2[k][:, 0:N1], op=ALU.add),
                 reads=[f't1_{k}', f't2_{k}'], writes=[f'Qp{j}'])
        y_all = [f'Yr{j}' for j in range(128)] + [f'Qp{j}' for j in range(128)]
        NB = N1 // 4
        pX = [pS, pKV]
        pXn = ['pS', 'pKV']
        for bi in range(NB):
            k = bi % 2
            def fc(e, bi=bi, k=k):
                e.matmul(pX[k][:, :], lhsT=cn_b[:, 0:128], rhs=Yr[:, 4 * bi:4 * bi + 4, :].rearrange("p a b -> p (a b)"),
                         start=True, stop=False)
                return e.matmul(pX[k][:, :], lhsT=cn_b[:, 128:256],
                                rhs=Qp[:, 4 * bi:4 * bi + 4, :].rearrange("p a b -> p (a b)"), start=False, stop=True)
            P.op('pe', fc, reads=y_all + ['cn_b'], writes=[pXn[k]])
            P.op('act', lambda e, bi=bi, k=k: e.activation(
                out=yf[:, 4 * bi:4 * bi + 4, :].rearrange("p a b -> p (a b)"), in_=pX[k][:, :], func=AF.Copy),
                reads=[pXn[k]], writes=[f'yf{bi}'])
        mt3 = mt_f.rearrange("(a b) j -> a b j", b=N1)
        npc = 4 if N1 >= 4 else 1
        for pc in range(npc):
            lo, hi = pc * N1 // npc, (pc + 1) * N1 // npc
            P.op('pool', lambda e, lo=lo, hi=hi: e.dma_start(out=mt3[:, lo:hi, :], in_=yf[:, lo:hi, :]),
                 reads=[f'yf{bi}' for bi in range(NB)], writes=['mixg_f'], dsem='D_yf')
        flush()
        cur.pop(); sF.__exit__(None, None, None)
        cur.pop(); sAF.__exit__(None, None, None)
        cur.pop(); sGA.__exit__(None, None, None)
        psA.__exit__(None, None, None)
        if not fused:
            return nc
        ctx = dict(nc=nc, P=P, es=es, cur=cur, flush=flush, ld=ld, sb=sb, xq=xq, w_out=w_out, ln1_g=ln1_g, ln1_b=ln1_b,
                   w_ff1=w_ff1, b_ff1T=b_ff1T, w_ff2=w_ff2, b_ff2=b_ff2, ln2_g=ln2_g, ln2_b=ln2_b, gidx=gidx, out=out,
                   gath_m=gath_m, gath_f=gath_f, mixg_f=mixg_f, mixg_m=mixg_m, issue_ag=issue_ag, piecesA=piecesA, piecesB=piecesB, CPP=CPP, ident=ident, ones=ones, epsT=epsT, modT=modT, sc2p=sc2p, g1row=g1row, g2row=g2row)
        build_B(N1, ctx)
        return nc


def build_B(N1, ctx=None):
    S = 128 * N1
    TQ = S // 4
    NT = TQ // 128
    NGB = NT // 2
    fusedB = ctx is not None
    if not fusedB:
        nc = bass.Bass("TRN2", target_bir_lowering=False)
        P = Prog()

        def din(name, shape, dt=F32):
            return nc.dram_tensor(name, shape, dt, kind="ExternalInput").ap()
        xq = din("xq", [TQ, D])
        mixq = din("mixq", [TQ, D], BF16)
        cT = din("cT", [128, 8])
        w_ada = din("w_ada", [D, 6 * D])
        b_adaT = din("b_adaT", [128, 48])
        b_ada_row = din("b_ada_row", [1, 6 * D])
        w_out = din("w_out", [D, D])
        ln1_g = din("ln1_g", [1, D]); ln1_b = din("ln1_b", [1, D])
        w_ff1 = din("w_ff1", [D, DFF]); b_ff1T = din("b_ff1T", [128, 32])
        w_ff2 = din("w_ff2", [DFF, D]); b_ff2 = din("b_ff2", [1, D])
        ln2_g = din("ln2_g", [1, D]); ln2_b = din("ln2_b", [1, D])
        c_ident = din("c_ident", [128, 128])
        out = nc.dram_tensor("out", [TQ, D], F32, kind="ExternalOutput").ap()
        es = contextlib.ExitStack()
        es.__enter__()
        cur = [es]
        sems = {}

        def sb(name, shape, dt=F32):
            return cur[-1].enter_context(nc.sbuf_tensor(name, shape, dt))

        def flush():
            P.barrier_all('sp')
            for sname in P.semnames:
                if sname not in sems:
                    sems[sname] = es.enter_context(nc.semaphore(sname))
            with nc.Block() as block:
                P.emit(nc, sems, block)
            for e_ in P.ENG:
                P.streams[e_] = []
            for e_ in P.ENG:
                P.barrier_all(e_)

        def ld(eng, dst, src, name, sem, reads=()):
            P.op(eng, lambda e: e.dma_start(out=dst, in_=src), reads=reads, writes=[name], dsem=sem)
    else:
        nc = ctx['nc']; P = ctx['P']; es = ctx['es']; cur = ctx['cur']; flush = ctx['flush']; ld = ctx['ld']; sb = ctx['sb']
        xq = ctx['xq']; w_out = ctx['w_out']; ln1_g = ctx['ln1_g']; ln1_b = ctx['ln1_b']; w_ff1 = ctx['w_ff1']
        b_ff1T = ctx['b_ff1T']; w_ff2 = ctx['w_ff2']; b_ff2 = ctx['b_ff2']; ln2_g = ctx['ln2_g']; ln2_b = ctx['ln2_b']
        gidx = ctx['gidx']; out = ctx['out']; gath_m = ctx['gath_m']; gath_f = ctx['gath_f']; mixg_f = ctx['mixg_f']
    psB = contextlib.ExitStack()
    psB.__enter__()

    def ps(name, shape, dt=F32):
        return psB.enter_context(nc.psum_tensor(name, shape, dt))
    if True:
        if fusedB:
            ident = ctx['ident']; ones = ctx['ones']; epsT = ctx['epsT']; modT = ctx['modT']; sc2p = ctx['sc2p']
            g1row = ctx['g1row']; g2row = ctx['g2row']
        else:
            ident = sb("ident", [128, 128], BF16)
            ones = sb("ones", [128, 128], F32)
            cact = sb("cact", [128, 8], F32)
            modT = sb("modT", [128, 48], F32)
            badaT = sb("badaT", [128, 48], F32)
            sc2p = sb("sc2p", [128, 8], F32)
            epsT = sb("epsT", [128, 1], F32)
            g1row = nc.dram_tensor("g1row", [1, 1024], F32).ap()
            g2row = nc.dram_tensor("g2row", [1, 1024], F32).ap()
        ones_b = sb("ones_b", [1, 128], BF16)
        b1T = sb("b1T", [128, 32], F32)
        bff2_b = sb("bff2_b", [1, 1024], BF16)
        l1g = sb("l1g", [128, 1024], F32); l1b = sb("l1b", [128, 1024], F32)
        l2g = sb("l2g", [128, 1024], F32); l2b = sb("l2b", [128, 1024], F32)
        pb_ = [ps(f"pb{i}", [128, 512], F32) for i in range(7)]
        pTr = ps("pTr", [128, 8, 128], BF16)
        P.psum_names |= {f'pb{i}' for i in range(7)} | {'pTr'}
        ld('sp', b1T[:], b_ff1T, 'b1T', 'D_b1')
        ld('pool', bff2_b[:], b_ff2, 'bff2_b', 'D_b2')
        ld('sp', l1g[:], ln1_g.broadcast_to([128, 1024]), 'l1g', 'D_l1g')
        ld('sp', l1b[:], ln1_b.broadcast_to([128, 1024]), 'l1b', 'D_l1b')
        ld('sp', l2g[:], ln2_g.broadcast_to([128, 1024]), 'l2g', 'D_l2g')
        ld('sp', l2b[:], ln2_b.broadcast_to([128, 1024]), 'l2b', 'D_l2b')
        P.op('pool', lambda e: e.memset(ones_b[:], 1.0), writes=['ones_b'])
        if fusedB:
            gix = sb("gix", [128, 4 * NT], mybir.dt.int32)
            ld('sp', gix[:], gidx, 'gix', 'D_gix')
        if not fusedB:
            ld('pool', ident[:], c_ident, 'ident', 'D_c0')
            ld('sp', cact[:], cT, 'cact', 'D_c7')
            ld('sp', badaT[:], b_adaT, 'badaT', 'D_c8')
            P.op('pool', lambda e: e.memset(ones[:], 1.0), writes=['ones'])
            P.op('pool', lambda e: e.memset(epsT[:], LN_EPS), writes=['epsT'])
            P.op('act', lambda e: e.activation(out=cact[:], in_=cact[:], func=AF.Silu), reads=['cact'], writes=['cact'])
            s0 = contextlib.ExitStack(); s0.__enter__(); cur.append(s0)
            wa = [sb(f"wa{i}", [128, 8, 1024], F32) for i in range(2)]
            crep = sb("crep", [128, 8, 128], F32)
            g1b = sb("g1b", [128, 1024], F32)
            g2b = sb("g2b", [128, 1024], F32)

            def mk_crep(e):
                i = None
                for j in range(8):
                    i = e.tensor_scalar(out=crep[:, j, :], in0=ones[:], scalar1=cact[:, j:j + 1], scalar2=None,
                                        op0=ALU.mult)
                return i
            P.op('dve', mk_crep, reads=['cact', 'ones'], writes=['crep'])

            def load_wa(i, part):
                ld('sp', wa[i][:], w_ada[:, part * 1024:(part + 1) * 1024].rearrange("(j p) n -> p j n", p=128),
                   f'wa{i}', f'D_wa{i}')

            def mod_featmajor(i, part):
                def f(e):
                    inst = None
                    for m in range(8):
                        for j in range(8):
                            inst = e.matmul(pb_[0][:, m:m + 1], lhsT=wa[i][:, j, m * 128:(m + 1) * 128],
                                            rhs=cact[:, j:j + 1], start=(j == 0), stop=(j == 7))
                    return inst
                P.op('pe', f, reads=[f'wa{i}', 'cact'], writes=['pb0'])
                P.op('dve', lambda e: e.tensor_tensor(out=modT[:, part * 8:(part + 1) * 8], in0=pb_[0][:, 0:8],
                                                      in1=badaT[:, part * 8:(part + 1) * 8], op=ALU.add),
                     reads=['pb0', 'badaT'], writes=[f'modT{part}'])

            def mod_rowbcast(i, part, dst, dname):
                ld('sp', dst[:], b_ada_row[:, part * 1024:(part + 1) * 1024].broadcast_to([128, 1024]), dname,
                   f'D_{dname}')
                for h in range(2):
                    def f(e, h=h):
                        inst = None
                        for j in range(8):
                            inst = e.matmul(pb_[1][:, :], lhsT=crep[:, j, :], rhs=wa[i][:, j, h * 512:(h + 1) * 512],
                                            start=(j == 0), stop=(j == 7))
                        return inst
                    P.op('pe', f, reads=[f'wa{i}', 'crep'], writes=['pb1'])
                    P.op('dve', lambda e, h=h: e.scalar_tensor_tensor(
                        out=dst[:, h * 512:(h + 1) * 512], in0=dst[:, h * 512:(h + 1) * 512], scalar=1.0,
                        in1=pb_[1][:, :], op0=ALU.add, op1=ALU.add), reads=['pb1', dname], writes=[dname])

            load_wa(0, 2); load_wa(1, 3)
            mod_rowbcast(0, 2, g1b, 'g1b')
            mod_featmajor(1, 3)
            load_wa(0, 4); load_wa(1, 5)
            mod_featmajor(0, 4)
            P.op('dve', lambda e: e.tensor_scalar(out=sc2p[:], in0=modT[:, 32:40], scalar1=1.0, scalar2=None,
                                                  op0=ALU.add), reads=['modT4'], writes=['sc2p'])
            mod_rowbcast(1, 5, g2b, 'g2b')
            P.op('sp', lambda e: e.dma_start(out=g1row, in_=g1b[0:1, :]), reads=['g1b'], writes=['g1row'], dsem='D_g1r')
            P.op('sp', lambda e: e.dma_start(out=g2row, in_=g2b[0:1, :]), reads=['g2b'], writes=['g2row'], dsem='D_g2r')
            flush()
            cur.pop(); s0.__exit__(None, None, None)

        wout = sb("wout", [128, 8, 1024], BF16)
        wff1 = sb("wff1", [128, 8, 4096], BF16)
        wff2 = sb("wff2", [128, 32, 1024], BF16)
        if fusedB:
            issue_ag = ctx['issue_ag']; mixg_m = ctx['mixg_m']; CPP = ctx['CPP']
            for i_ in ctx['piecesA']:
                issue_ag(mixg_f, gath_f, i_, ['mixg_f'], 'gath_f_A')
            for i_ in ctx['piecesB']:
                issue_ag(mixg_m, gath_m, i_, [f'mixg_m_c{cc}' for cc in range(i_ * CPP, (i_ + 1) * CPP)], 'gath_m_B')
            for i_ in ctx['piecesB']:
                issue_ag(mixg_f, gath_f, i_, ['mixg_f'], 'gath_f_B')
        sT = contextlib.ExitStack(); sT.__enter__(); cur.append(sT)
        tg1 = sb("tg1", [128, 1024], F32)
        tg2 = sb("tg2", [128, 1024], F32)
        stg = [sb(f"stg{i}", [128, 2048], F32) for i in range(3)]
        ld('sp', tg1[:], g1row.broadcast_to([128, 1024]), 'tg1', 'D_tg1', reads=['g1row'])
        ld('sp', tg2[:], g2row.broadcast_to([128, 1024]), 'tg2', 'D_tg2', reads=['g2row'])
        si = [0]

        def stage(src_ap, shape3=None):
            k = si[0] % 3
            si[0] += 1
            dst = stg[k][:] if shape3 is None else stg[k][:].rearrange("p (a b) -> p a b", a=shape3)
            ld('sp', dst, src_ap, f'stg{k}', f'D_stg{k}')
            return k
        for jj in range(4):
            k = stage(w_out[jj * 256:(jj + 1) * 256, :].rearrange("(a p) n -> p a n", p=128), 2)
            for a in range(2):
                P.op('dve', lambda e, k=k, a=a, jj=jj: e.tensor_tensor(out=wout[:, 2 * jj + a, :],
                                                                     in0=stg[k][:, a * 1024:(a + 1) * 1024],
                                                                     in1=tg1[:], op=ALU.mult),
                     reads=[f'stg{k}', 'tg1'], writes=['wout'])
        for j in range(8):
            for hh in range(2):
                k = stage(w_ff1[j * 128:(j + 1) * 128, hh * 2048:(hh + 1) * 2048])
                P.op('act', lambda e, k=k, j=j, hh=hh: e.activation(out=wff1[:, j, hh * 2048:(hh + 1) * 2048],
                                                                  in_=stg[k][:], func=AF.Copy),
                     reads=[f'stg{k}'], writes=['wff1'])
        for ff in range(16):
            k = stage(w_ff2[ff * 256:(ff + 1) * 256, :].rearrange("(a p) n -> p a n", p=128), 2)
            for a in range(2):
                f_ = 2 * ff + a
                eng_ = 'dve' if a == 0 else 'pool'
                P.op(eng_, lambda e, k=k, a=a, f_=f_: e.tensor_tensor(out=wff2[:, f_, :],
                                                                     in0=stg[k][:, a * 1024:(a + 1) * 1024],
                                                                     in1=tg2[:], op=ALU.mult),
                     reads=[f'stg{k}', 'tg2'], writes=[f'wff2_{f_}'])
        P.op('dve', lambda e: e.tensor_tensor(out=bff2_b[:], in0=bff2_b[:], in1=tg2[0:1, :], op=ALU.mult),
             reads=['bff2_b', 'tg2'], writes=['bff2_b'])
        w2n = [f'wff2_{f_}' for f_ in range(32)]
        flush()
        cur.pop(); sT.__exit__(None, None, None)

        xt = sb("xt", [128, 1024], F32)
        mxs = [sb(f"mx{i}", [128, 1024], BF16) for i in range(2)]
        x1 = sb("x1", [128, 4, 1024], F32)
        xn = sb("xn", [128, 1024], BF16)
        mixT = xn[:].rearrange("p (a b) -> p a b", b=128)
        h2T = [sb(f"h2TB{i}", [128, 8, 256], BF16) for i in range(2)]
        uT = sb("uT", [128, 8, 256], BF16)
        rt = [sb(f"rtB{i}", [128, 256], F32) for i in range(2)]
        bst = [sb(f"bstB{i}", [128, 2, 6], F32) for i in range(2)]
        mv = [sb(f"mvB{i}", [128, 2], F32) for i in range(2)]
        rs = [sb(f"rsB{i}", [128, 2], F32) for i in range(2)]

        def ln_stats(src, sname, k):
            def f(e):
                e.bn_stats(out=bst[k][:, 0, :], in_=src[:, 0:512])
                return e.bn_stats(out=bst[k][:, 1, :], in_=src[:, 512:1024])
            P.op('dve', f, reads=[sname], writes=[f'bst{k}'])
            yield
            P.op('dve', lambda e: e.bn_aggr(out=mv[k][:], in_=bst[k][:].rearrange("p a b -> p (a b)")),
                 reads=[f'bst{k}'], writes=[f'mv{k}'])
            yield
            P.op('act', lambda e: e.activation(out=rs[k][:, 0:1], in_=mv[k][:, 1:2], func=AF.Sqrt, bias=epsT[:, 0:1]),
                 reads=[f'mv{k}', 'epsT'], writes=[f'rsa{k}'])
            yield
            P.op('dve', lambda e: e.reciprocal(out=rs[k][:, 0:1], in_=rs[k][:, 0:1]), reads=[f'rsa{k}'],
                 writes=[f'rsa{k}'])
            yield
            P.op('dve', lambda e: e.tensor_scalar(out=rs[k][:, 1:2], in0=mv[k][:, 0:1], scalar1=rs[k][:, 0:1],
                                                  scalar2=-1.0, op0=ALU.mult, op1=ALU.mult),
                 reads=[f'rsa{k}', f'mv{k}'], writes=[f'rsb{k}'])
            yield

        def prologue(g):
            hp = g % 2
            for t in range(2):
                r0 = (2 * g + t) * 128
                mx = mxs[t]
                if not fusedB:
                    ld('sp', mx[:], mixq[r0:r0 + 128, :], f'mx{t}', f'D_m{t}')
                else:
                    T_ = 2 * g + t
                    for s_ in range(8):
                        gsrc = gath_m if s_ < 4 else gath_f
                        sr = s_ % 4
                        P.op('pool', lambda e, s_=s_, sr=sr, T_=T_, gsrc=gsrc, mx=mx: e.indirect_dma_start(
                            out=mx[:, s_ * 128:(s_ + 1) * 128], out_offset=None, in_=gsrc.ap()[:, :],
                            in_offset=bass.IndirectOffsetOnAxis(ap=gix[:, sr * NT + T_:sr * NT + T_ + 1], axis=0)),
                            reads=(['gath_m_A', 'gath_f_A', 'gix'] if (TQ == 2 * (S // max(1, S // 2048)) and T_ < NT // 2) else
                                   ['gath_m_A', 'gath_f_A', 'gath_m_B', 'gath_f_B', 'gix']),
                            writes=[f'mx{t}'], dsem=f'D_m{t}')
                yield
            for t in range(2):
                r0 = (2 * g + t) * 128
                xi = (2 * g + t) % 4
                x1t = x1[:, xi, :]
                xname = f'x1_{xi}'
                mx = mxs[t]
                ld('sp', xt[:], xq[r0:r0 + 128, :], 'xt', 'D_x')
                yield

                def ftr(e, mx=mx):
                    inst = None
                    for j in range(8):
                        inst = e.transpose(out=pTr[:, j, :], in_=mx[:, j * 128:(j + 1) * 128], identity=ident[:])
                    return inst
                P.op('pe', ftr, reads=[f'mx{t}', 'ident'], writes=['pTr'])
                yield
                P.op('act', lambda e: e.activation(out=xn[:], in_=pTr[:].rearrange("p a b -> p (a b)"), func=AF.Copy),
                     reads=['pTr'], writes=['xn'])
                yield
                for hf in range(2):
                    def fo(e, hf=hf):
                        inst = None
                        for j in range(8):
                            inst = e.matmul(pb_[0][:, :], lhsT=mixT[:, j, :], rhs=wout[:, j, hf * 512:(hf + 1) * 512],
                                            start=(j == 0), stop=(j == 7))
                        return inst
                    P.op('pe', fo, reads=['xn', 'wout'], writes=['pb0'])
                    yield
                    P.op('dve', lambda e, hf=hf: e.scalar_tensor_tensor(
                        out=xt[:, hf * 512:(hf + 1) * 512], in0=xt[:, hf * 512:(hf + 1) * 512], scalar=ALPHA,
                        in1=pb_[0][:, :], op0=ALU.mult, op1=ALU.add), reads=['pb0', 'xt'], writes=['xt'])
                    yield
                yield from ln_stats(xt, 'xt', 0)
                P.op('act', lambda e, x1t=x1t: e.activation(out=x1t, in_=xt[:], func=AF.Identity, scale=rs[0][:, 0:1],
                                                            bias=rs[0][:, 1:2]),
                     reads=['xt', 'rsa0', 'rsb0'], writes=[xname])
                yield
                P.op('dve', lambda e, x1t=x1t: e.tensor_tensor(out=x1t, in0=x1t, in1=l1g[:], op=ALU.mult),
                     reads=[xname, 'l1g'], writes=[xname])
                yield
                P.op('dve', lambda e, x1t=x1t: e.tensor_tensor(out=x1t, in0=x1t, in1=l1b[:], op=ALU.add),
                     reads=[xname, 'l1b'], writes=[xname])
                yield
                yield from ln_stats(x1t, xname, 0)
                P.op('act', lambda e, x1t=x1t: e.activation(out=xn[:], in_=x1t, func=AF.Identity, scale=rs[0][:, 0:1],
                                                            bias=rs[0][:, 1:2]),
                     reads=[xname, 'rsa0', 'rsb0'], writes=['xn'])
                yield

                def ftr2(e):
                    inst = None
                    for j in range(8):
                        inst = e.transpose(out=pTr[:, j, :], in_=xn[:, j * 128:(j + 1) * 128], identity=ident[:])
                    return inst
                P.op('pe', ftr2, reads=['xn', 'ident'], writes=['pTr'])
                yield
                for j in range(8):
                    P.op('act', lambda e, j=j, t=t, hp=hp: e.activation(out=h2T[hp][:, j, t * 128:(t + 1) * 128],
                                                                       in_=pTr[:, j, :], func=AF.Identity,
                                                                       scale=sc2p[:, j:j + 1],
                                                                       bias=modT[:, 24 + j:25 + j]),
                         reads=['pTr', 'sc2p', 'modT3'], writes=[f'h2T{hp}_{j}'])
                    if j % 2 == 1:
                        yield

        def ffn(g):
            hp = g % 2
            hn = [f'h2T{hp}_{j}' for j in range(8)]
            for fh in range(4):
                for fc in range(8):
                    f_ = fh * 8 + fc
                    k = 1 + fc % 2

                    def f1(e, f_=f_, k=k):
                        inst = None
                        for j in range(8):
                            inst = e.matmul(pb_[k][:, 0:256], lhsT=wff1[:, j, f_ * 128:(f_ + 1) * 128],
                                            rhs=h2T[hp][:, j, :], start=(j == 0), stop=(j == 7))
                        return inst
                    P.op('pe', f1, reads=hn + ['wff1'], writes=[f'pb{k}'])
                    P.op('act', lambda e, f_=f_, k=k: e.activation(out=rt[k - 1][:], in_=pb_[k][:, 0:256], func=AF.Relu,
                                                                   bias=b1T[:, f_:f_ + 1]),
                         reads=[f'pb{k}', 'b1T'], writes=[f'rt{k}'])
                    sq_eng = 'dve'
                    P.op(sq_eng, lambda e, fc=fc, k=k: e.tensor_tensor(out=uT[:, fc, :], in0=rt[k - 1][:],
                                                                      in1=rt[k - 1][:], op=ALU.mult),
                         reads=[f'rt{k}'], writes=[f'uT{fc}'])
                    yield
                un = [f'uT{fc}' for fc in range(8)]
                for t in range(2):
                    for ch in range(2):
                        bk = 3 + t * 2 + ch

                        def f2(e, t=t, ch=ch, bk=bk, fh=fh):
                            inst = None
                            for fc in range(8):
                                f_ = fh * 8 + fc
                                inst = e.matmul(pb_[bk][:, :], lhsT=uT[:, fc, t * 128:(t + 1) * 128],
                                                rhs=wff2[:, f_, ch * 512:(ch + 1) * 512],
                                                start=(f_ == 0), stop=False, skip_group_check=True)
                            if fh == 3:
                                inst = e.matmul(pb_[bk][:, :], lhsT=ones_b[0:1, :],
                                                rhs=bff2_b[0:1, ch * 512:(ch + 1) * 512],
                                                start=False, stop=True, skip_group_check=True)
                            return inst
                        P.op('pe', f2, reads=un + w2n + ['ones_b', 'bff2_b'], writes=[f'pb{bk}'])
                        yield

        def epilogue(g):
            for t in range(2):
                r0 = (2 * g + t) * 128
                xi = (2 * g + t) % 4
                x1t = x1[:, xi, :]
                xname = f'x1_{xi}'
                for ch in range(2):
                    bk = 3 + t * 2 + ch
                    P.op('dve', lambda e, ch=ch, bk=bk, xi=xi: e.scalar_tensor_tensor(
                        out=x1[:, xi, ch * 512:(ch + 1) * 512], in0=x1[:, xi, ch * 512:(ch + 1) * 512], scalar=ALPHA,
                        in1=pb_[bk][:, :], op0=ALU.mult, op1=ALU.add), reads=[f'pb{bk}', xname], writes=[xname])
                    yield
                yield from ln_stats(x1t, xname, 1)
                P.op('act', lambda e, x1t=x1t: e.activation(out=x1t, in_=x1t, func=AF.Identity, scale=rs[1][:, 0:1],
                                                            bias=rs[1][:, 1:2]),
                     reads=[xname, 'rsa1', 'rsb1'], writes=[xname])
                yield
                P.op('pool', lambda e, x1t=x1t: e.tensor_tensor(out=x1t, in0=x1t, in1=l2g[:], op=ALU.mult),
                     reads=[xname, 'l2g'], writes=[xname])
                yield
                P.op('pool', lambda e, x1t=x1t: e.tensor_tensor(out=x1t, in0=x1t, in1=l2b[:], op=ALU.add),
                     reads=[xname, 'l2b'], writes=[xname])
                yield
                P.op('sp', lambda e, r0=r0, x1t=x1t: e.dma_start(out=out[r0:r0 + 128, :], in_=x1t),
                     reads=[xname], writes=['out'], dsem=f'D_out{xi}')
                yield

        def interleave(*gens):
            gens = list(gens)
            while gens:
                for gen in list(gens):
                    try:
                        next(gen)
                    except StopIteration:
                        gens.remove(gen)

        interleave(prologue(0))
        for g in range(NGB):
            if g + 1 < NGB:
                interleave(ffn(g), prologue(g + 1))
            else:
                interleave(ffn(g))
            interleave(epilogue(g))
        flush()
    psB.__exit__(None, None, None)
    if not fusedB:
        es.__exit__(None, None, None)
    return nc


def _consts(N1):
    S = 128 * N1
    i = np.arange(128)
    ident = np.eye(128, dtype=np.float32)
    le = (i[:, None] <= i[None, :]).astype(np.float32)
    ge = (i[:, None] >= i[None, :]).astype(np.float32)
    ang = 2 * np.pi * np.outer(i, i) / 128.0
    c128, s128 = np.cos(ang), np.sin(ang)
    a = np.arange(N1)
    angA = 2 * np.pi * np.outer(a, a) / N1
    cA, sA = np.cos(angA), np.sin(angA)
    angT = 2 * np.pi * np.outer(i, a) / S
    tc, ts = np.cos(angT), np.sin(angT)
    nrm = 1.0 / np.sqrt(S * 128.0)
    return {
        "c_ident": ident, "c_maskf": 0.125 * le, "c_maskb": 0.125 * ge, "c_ule": le, "c_uge": ge,
        "c_cs": np.concatenate([c128, s128], 1).astype(np.float32),
        "c_a1": np.concatenate([cA, sA], 1).astype(np.float32),
        "c_a2": np.concatenate([-sA, cA], 1).astype(np.float32),
        "c_tw": np.concatenate([tc, tc, ts, ts], 1).astype(np.float32),
        "c_cn": (np.concatenate([c128, -s128], 1) * nrm).astype(np.float32),
    }


def make_in_maps(inp, N1, fused=False):
    S = 128 * N1
    TQ = S // 4
    cst = _consts(N1)
    f = lambda a: np.ascontiguousarray(np.asarray(a, dtype=np.float32))
    maps = []
    for core in range(NCORES):
        b, g = core // 4, core % 4
        cols = np.concatenate([
            np.arange(64) + 64 * g,
            256 + np.arange(64) + 64 * g,
            512 + np.arange(128) + 128 * g,
            1024 + np.arange(128) + 128 * g,
            2048 + np.array([g, 4 + g, 8 + g, 12 + g]),
            1536 + np.arange(128) + 128 * g,
        ])
        m = {
            "xb": f(inp["x"][b]),
            "xq": f(inp["x"][b, g * TQ:(g + 1) * TQ]),
            "cT": f(inp["c"][b].reshape(8, 128).T),
            "w_ada": f(inp["w_ada"][0]),
            "b_adaT": f(inp["b_ada"][0].reshape(48, 128).T),
            "b_ada_row": f(inp["b_ada"][0].reshape(1, -1)),
            "w_in": f(inp["w_in"][0][:, cols]),
            "b_gate": f(inp["b_gate"][0][[g, 4 + g, 8 + g, 12 + g]].reshape(1, 4)),
            "nw": f(inp["mlstm_norm_w"][0][128 * g:128 * (g + 1)].reshape(1, 128)),
            "w_out": f(inp["w_out"][0]),
            "ln1_g": f(inp["ln1_g"][0].reshape(1, -1)), "ln1_b": f(inp["ln1_b"][0].reshape(1, -1)),
            "w_ff1": f(inp["w_ff1"][0]), "b_ff1T": f(inp["b_ff1"][0].reshape(32, 128).T),
            "w_ff2": f(inp["w_ff2"][0]), "b_ff2": f(inp["b_ff2"][0].reshape(1, -1)),
            "ln2_g": f(inp["ln2_g"][0].reshape(1, -1)), "ln2_b": f(inp["ln2_b"][0].reshape(1, -1)),
        }
        if fused:
            NT = TQ // 128
            NP_ = max(1, S // 2048)
            R_ = S // NP_
            gi = np.empty((128, 4 * NT), np.int32)
            for s_ in range(4):
                for T_ in range(NT):
                    n_ = g * TQ + T_ * 128 + np.arange(128)
                    gi[:, s_ * NT + T_] = (n_ // R_) * 4 * R_ + s_ * R_ + n_ % R_
            m["gidx"] = gi
        m.update(cst)
        maps.append(m)
    return maps


def _bf16_to_mixq(results, N1):
    S = 128 * N1
    TQ = S // 4
    mixqs = []
    for core in range(NCORES):
        b, r = core // 4, core % 4
        parts_m = [results[4 * b + g]["dbg"][r * TQ:(r + 1) * TQ, 0:128] for g in range(4)]
        parts_f = [results[4 * b + g]["dbg"][r * TQ:(r + 1) * TQ, 128:256] for g in range(4)]
        mixqs.append(np.ascontiguousarray(np.concatenate(parts_m + parts_f, axis=1)))
    return mixqs


def kernel(**inputs):
    N1 = 128
    S = 128 * N1
    TQ = S // 4
    maps = make_in_maps(inputs, N1, fused=True)
    nc = build_nc(N1, fused=True)
    res = run_bass_kernel_spmd(nc, maps, core_ids=list(range(NCORES)))
    outp = np.empty((2, S, D), np.float32)
    for core in range(NCORES):
        b, g = core // 4, core % 4
        outp[b, g * TQ:(g + 1) * TQ] = np.asarray(res.results[core]["out"], dtype=np.float32)
    return outp
```

```python
import contextlib
import numpy as np
import concourse.bass as bass
import concourse.mybir as mybir
from concourse.bass_utils import run_bass_kernel_spmd

F32 = mybir.dt.float32
BF16 = mybir.dt.bfloat16
AF = mybir.ActivationFunctionType
ALU = mybir.AluOpType

D = 1024
DFF = 4096
LN_EPS = 1e-5
ALPHA = 2.0 ** 0.25
NCORES = 8


class Prog:
    ENG = ('pe', 'act', 'dve', 'pool', 'sp')

    def __init__(self):
        self.streams = {e: [] for e in self.ENG}
        self.cnt = {}
        self.lastw = {}
        self.rd = {}
        self.waited = {e: {} for e in self.ENG}
        self.semnames = ['S_' + e for e in self.ENG]
        self.psum_names = set()
        self.pacc = {}
        self._rec = None

    def record(self, body):
        assert self._rec is None
        self._rec = []
        body()
        r = self._rec
        self._rec = None
        return r

    def replay(self, *lists):
        lists = [l for l in lists if l]
        pos = [0] * len(lists)
        total = sum(len(l) for l in lists)
        for _ in range(total):
            best = None
            for i, l in enumerate(lists):
                if pos[i] < len(l):
                    frac = pos[i] / len(l)
                    if best is None or frac < best[0]:
                        best = (frac, i)
            i = best[1]
            self.op(*lists[i][pos[i]])
            pos[i] += 1

    def op(self, eng, fn, reads=(), writes=(), dsem=None, inc=None):
        if self._rec is not None:
            self._rec.append((eng, fn, tuple(reads), tuple(writes), dsem, inc))
            return None
        d = {}

        def add(t):
            if t is None:
                return
            s, v = t
            if d.get(s, 0) < v:
                d[s] = v
        for r in reads:
            add(self.lastw.get(r))
        for w in writes:
            add(self.lastw.get(w))
            for s, v in self.rd.get(w, {}).items():
                add((s, v))
        for n in list(reads) + list(writes):
            if n in self.psum_names:
                for e2, t2 in self.pacc.get(n, {}).items():
                    if e2 != eng:
                        add(t2)
        waits = []
        for s, v in d.items():
            if eng == 'pe' and s == 'S_pe':
                continue
            if self.waited[eng].get(s, 0) >= v:
                continue
            self.waited[eng][s] = v
            waits.append((s, v))
        if dsem is None:
            s = 'S_' + eng
            inc = 1
        else:
            s = dsem
            inc = 16 if inc is None else inc
            if s not in self.semnames:
                self.semnames.append(s)
        self.cnt[s] = self.cnt.get(s, 0) + inc
        t = (s, self.cnt[s])
        for r in reads:
            m = self.rd.setdefault(r, {})
            if m.get(s, 0) < t[1]:
                m[s] = t[1]
        for w in writes:
            self.lastw[w] = t
            self.rd[w] = {}
        for n in list(reads) + list(writes):
            if n in self.psum_names:
                self.pacc.setdefault(n, {})[eng] = t
        self.streams[eng].append((waits, fn, s, inc))
        return t

    def barrier_all(self, eng):
        waits = []
        for s, v in self.cnt.items():
            if s == 'CC':
                continue
            if self.waited[eng].get(s, 0) >= v:
                continue
            self.waited[eng][s] = v
            waits.append((s, v))
        self.streams[eng].append((waits, None, None, 0))

    def emit(self, nc, sems, block):
        def run(engname):
            def body(eng):
                for waits, fn, s, inc in self.streams[engname]:
                    for ws, wv in waits:
                        eng.wait_ge(sems[ws], wv)
                    if fn is None:
                        continue
                    inst = fn(eng)
                    inst.then_inc(sems[s], inc)
            return body
        block.tensor(run('pe'))
        block.scalar(run('act'))
        block.vector(run('dve'))
        block.gpsimd(run('pool'))
        block.sync(run('sp'))


def build_nc(N1, stop_after=None, fused=False):
    S = 128 * N1
    TQ = S // 4
    nc = bass.Bass("TRN2", target_bir_lowering=False)
    P = Prog()

    def din(name, shape, dt=F32):
        return nc.dram_tensor(name, shape, dt, kind="ExternalInput").ap()

    xb = din("xb", [S, D])
    cT = din("cT", [128, 8])
    w_ada = din("w_ada", [D, 6 * D])
    b_adaT = din("b_adaT", [128, 48])
    w_in = din("w_in", [D, 516])
    b_gate = din("b_gate", [1, 4])
    nw = din("nw", [1, 128])
    c_ident = din("c_ident", [128, 128])
    c_maskf = din("c_maskf", [128, 128])
    c_maskb = din("c_maskb", [128, 128])
    c_ule = din("c_ule", [128, 128])
    c_uge = din("c_uge", [128, 128])
    c_cs = din("c_cs", [128, 256])
    c_a1 = din("c_a1", [N1, 2 * N1])
    c_a2 = din("c_a2", [N1, 2 * N1])
    c_tw = din("c_tw", [128, 4 * N1])
    c_cn = din("c_cn", [128, 256])

    dbg = None
    if not fused:
        dbg = nc.dram_tensor("dbg", [S, 256], BF16, kind="ExternalOutput").ap()
    else:
        xq = din("xq", [TQ, D])
        b_ada_row = din("b_ada_row", [1, 6 * D])
        w_out = din("w_out", [D, D])
        ln1_g = din("ln1_g", [1, D]); ln1_b = din("ln1_b", [1, D])
        w_ff1 = din("w_ff1", [D, DFF]); b_ff1T = din("b_ff1T", [128, 32])
        w_ff2 = din("w_ff2", [DFF, D]); b_ff2 = din("b_ff2", [1, D])
        ln2_g = din("ln2_g", [1, D]); ln2_b = din("ln2_b", [1, D])
        gidx = din("gidx", [128, 4 * (TQ // 128)], mybir.dt.int32)
        out = nc.dram_tensor("out", [TQ, D], F32, kind="ExternalOutput").ap()

    so_d = nc.dram_tensor("so_d", [S, 128], F32).ap()
    hf_d = nc.dram_tensor("hf_d", [S, 128], F32).ap()
    mixg_m = nc.dram_tensor("mixg_m", [S, 128], BF16)
    mixg_f = nc.dram_tensor("mixg_f", [S, 128], BF16)
    gath_m = nc.dram_tensor("gath_m", [4 * S, 128], BF16)
    gath_f = nc.dram_tensor("gath_f", [4 * S, 128], BF16)
    NP_ = max(1, S // 2048)
    R_ = S // NP_
    CPP = R_ // 128

    import contextlib
    es = contextlib.ExitStack()
    with es:
        cur = [es]
        sems = {}

        def sb(name, shape, dt=F32):
            return cur[-1].enter_context(nc.sbuf_tensor(name, shape, dt))

        def flush():
            P.barrier_all('sp')
            for sname in P.semnames:
                if sname not in sems:
                    sems[sname] = es.enter_context(nc.semaphore(sname))
            with nc.Block() as block:
                P.emit(nc, sems, block)
            for e_ in P.ENG:
                P.streams[e_] = []
            for e_ in P.ENG:
                P.barrier_all(e_)

        psA = contextlib.ExitStack()
        psA.__enter__()

        def ps(name, shape, dt=F32):
            return psA.enter_context(nc.psum_tensor(name, shape, dt))

        ident = sb("ident", [128, 128], BF16)
        ones = sb("ones", [128, 128], F32)
        modT = sb("modT", [128, 48], F32)
        sc2p = sb("sc2p", [128, 8], F32)
        epsT = sb("epsT", [128, 1], F32)
        cact = sb("cact", [128, 8], F32)
        badaT = sb("badaT", [128, 48], F32)
        sGA = contextlib.ExitStack(); sGA.__enter__(); cur.append(sGA)
        maskf = sb("maskf", [128, 128], F32)
        maskb = sb("maskb", [128, 128], F32)
        ule = sb("ule", [128, 128], F32)
        uge = sb("uge", [128, 128], F32)
        bg = sb("bg", [128, 4], F32)
        nwb = sb("nwb", [128, 128], F32)
        cact_b = sb("cact_b", [128, 8], BF16)
        sc1p = sb("sc1p", [128, 8], F32)
        w_sb = sb("w_sb", [128, 8, 516], BF16)

        def ld(eng, dst, src, name, sem, reads=()):
            P.op(eng, lambda e: e.dma_start(out=dst, in_=src), reads=reads, writes=[name], dsem=sem)

        ld('pool', ident[:], c_ident, 'ident', 'D_c0')
        ld('sp', maskf[:], c_maskf, 'maskf', 'D_c1')
        ld('sp', maskb[:], c_maskb, 'maskb', 'D_c2')
        ld('sp', ule[:], c_ule, 'ule', 'D_c3')
        ld('sp', uge[:], c_uge, 'uge', 'D_c4')
        ld('sp', bg[:], b_gate.broadcast_to([128, 4]), 'bg', 'D_c5')
        ld('sp', nwb[:], nw.broadcast_to([128, 128]), 'nwb', 'D_c6')
        ld('sp', cact[:], cT, 'cact', 'D_c7')
        ld('sp', badaT[:], b_adaT, 'badaT', 'D_c8')
        ld('pool', w_sb[:], w_in.rearrange("(j p) n -> p j n", p=128), 'w_sb', 'D_c9')
        P.op('pool', lambda e: e.memset(ones[:], 1.0), writes=['ones'])
        P.op('pool', lambda e: e.memset(epsT[:], LN_EPS), writes=['epsT'])

        def finish():
            flush()
            return nc

        if stop_after == 'C':
            return finish()
        P.op('act', lambda e: e.activation(out=cact[:], in_=cact[:], func=AF.Silu), reads=['cact'], writes=['cact'])
        s0 = contextlib.ExitStack()
        s0.__enter__()
        cur.append(s0)
        wa = [sb(f"wa{i}", [128, 8, 1024], F32) for i in range(2)]
        if fused:
            g1row = nc.dram_tensor("g1row", [1, 1024], F32).ap()
            g2row = nc.dram_tensor("g2row", [1, 1024], F32).ap()
        pT = ps("pT", [128, 2, 4, 2, 128], BF16)
        pFZ = ps("pFZ", [128, 512], F32)
        pTMb = ps("pTM", [128, 512], F32)
        pTM = [pTMb, pTMb]
        pQ = ps("pQ", [128, 1024], BF16)
        pS = ps("pS", [128, 512], F32)
        pKV = ps("pKV", [128, 512], F32)
        pND = ps("pND", [128, 512], F32)
        P.psum_names |= {'pTa', 'pTb', 'pFZ', 'pTM', 'pQ', 'pS', 'pKV', 'pND'}
        pmod = pFZ
        pmod2 = pTM[0]
        crep = sb("crep", [128, 8, 128], F32)
        def mk_crep(e):
            i = None
            for j in range(8):
                i = e.tensor_scalar(out=crep[:, j, :], in0=ones[:], scalar1=cact[:, j:j + 1], scalar2=None,
                                    op0=ALU.mult)
            return i
        P.op('dve', mk_crep, reads=['cact', 'ones'], writes=['crep'])

        def load_wa(i, part):
            ld('sp', wa[i][:], w_ada[:, part * 1024:(part + 1) * 1024].rearrange("(j p) n -> p j n", p=128),
               f'wa{i}', f'D_wa{i}')

        def mod_featmajor(i, part):
            def f(e):
                inst = None
                for m in range(8):
                    for j in range(8):
                        inst = e.matmul(pmod[:, m:m + 1], lhsT=wa[i][:, j, m * 128:(m + 1) * 128],
                                        rhs=cact[:, j:j + 1], start=(j == 0), stop=(j == 7))
                return inst
            P.op('pe', f, reads=[f'wa{i}', 'cact'], writes=['pFZ'])
            P.op('dve', lambda e: e.tensor_tensor(out=modT[:, part * 8:(part + 1) * 8], in0=pmod[:, 0:8],
                                                  in1=badaT[:, part * 8:(part + 1) * 8], op=ALU.add),
                 reads=['pFZ', 'badaT'], writes=[f'modT{part}'])

        def mod_rowbcast(i, part, dst, dname):
            ld('sp', dst[:], b_ada_row[:, part * 1024:(part + 1) * 1024].broadcast_to([128, 1024]), dname,
               f'D_{dname}')
            for h in range(2):
                def f(e, h=h):
                    inst = None
                    for j in range(8):
                        inst = e.matmul(pmod2[:, :], lhsT=crep[:, j, :], rhs=wa[i][:, j, h * 512:(h + 1) * 512],
                                        start=(j == 0), stop=(j == 7))
                    return inst
                P.op('pe', f, reads=[f'wa{i}', 'crep'], writes=['pTM'])
                P.op('dve', lambda e, h=h: e.scalar_tensor_tensor(
                    out=dst[:, h * 512:(h + 1) * 512], in0=dst[:, h * 512:(h + 1) * 512], scalar=1.0,
                    in1=pmod2[:, :], op0=ALU.add, op1=ALU.add), reads=['pTM', dname], writes=[dname])

        load_wa(0, 0); load_wa(1, 1)
        mod_featmajor(0, 0)
        mod_featmajor(1, 1)
        P.op('dve', lambda e: e.tensor_scalar(out=sc1p[:], in0=modT[:, 8:16], scalar1=1.0, scalar2=None, op0=ALU.add),
             reads=['modT1'], writes=['sc1p'])
        flush()
        cur.pop()
        s0.__exit__(None, None, None)
        sAF = contextlib.ExitStack(); sAF.__enter__(); cur.append(sAF)
        fzT = sb("fzT", [128, S], BF16)
        sA = contextlib.ExitStack(); sA.__enter__(); cur.append(sA)
        if stop_after == 'P0':
            return finish()
        NG = N1 // 2
        xt = [sb(f"xt{i}", [128, 1024], F32) for i in range(4)]
        xn = [sb(f"xn{i}", [128, 1024], BF16) for i in range(2)]
        bst = [sb(f"bst{i}", [128, 2, 6], F32) for i in range(2)]
        mv = [sb(f"mv{i}", [128, 2], F32) for i in range(2)]
        rs = [sb(f"rs{i}", [128, 2], F32) for i in range(2)]
        hT = [sb(f"hT{i}", [128, 8, 256], BF16) for i in range(2)]
        qk_tm = sb("qk_tm", [128, N1, 128], BF16)
        vp = sb("vp", [128, N1, 130], BF16)
        gt = sb("gt", [128, N1, 4], F32)
        scal = sb("scal", [128, NG, 16], F32)
        so_t = [sb(f"so{i}", [128, 128], F32) for i in range(2)]
        hf_t = [sb(f"hf{i}", [128, 128], F32) for i in range(2)]
        qkT = [sb(f"qkT{i}", [64, 2, 128], BF16) for i in range(2)]
        PTs = [sb(f"PTs{i}", [128, 128], BF16) for i in range(2)]
        ku = [sb(f"ku{i}", [128, 64], BF16) for i in range(2)]
        Dst = sb("Dst", [64, 129], F32)
        Cb = [sb(f"Cb{i}", [64, 129], BF16) for i in range(2)]
        ee = [sb(f"ee{i}", [128, 2, 2], F32) for i in range(2)]
        sp_ = [sb(f"sp{i}", [128, 2, 2], F32) for i in range(2)]
        ein = [sb(f"ein{i}", [128, 4, 4], F32) for i in range(2)]
        dd = [sb(f"dd{i}", [128, 2], F32) for i in range(2)]

        P.op('pool', lambda e: e.memset(vp[:, :, 128:130], 1.0), writes=['vp_ones'])

        def ln_stats(xtile, xname, k, par):
            def f(e):
                e.bn_stats(out=bst[par][:, 0, :], in_=xtile[:, 0:512])
                return e.bn_stats(out=bst[par][:, 1, :], in_=xtile[:, 512:1024])
            P.op('dve', f, reads=[xname], writes=[f'bst{par}'])
            P.op('dve', lambda e: e.bn_aggr(out=mv[par][:], in_=bst[par][:].rearrange("p a b -> p (a b)")),
                 reads=[f'bst{par}'], writes=[f'mv{par}'])
            P.op('act', lambda e: e.activation(out=rs[par][:, 0:1], in_=mv[par][:, 1:2], func=AF.Sqrt,
                                               bias=epsT[:, 0:1]),
                 reads=[f'mv{par}', 'epsT'], writes=[f'rs{par}a'])
            P.op('dve', lambda e: e.reciprocal(out=rs[par][:, 0:1], in_=rs[par][:, 0:1]),
                 reads=[f'rs{par}a'], writes=[f'rs{par}a'])
            P.op('dve', lambda e: e.tensor_scalar(out=rs[par][:, 1:2], in0=mv[par][:, 0:1], scalar1=rs[par][:, 0:1],
                                                  scalar2=-1.0, op0=ALU.mult, op1=ALU.mult),
                 reads=[f'rs{par}a', f'mv{par}'], writes=[f'rs{par}b'])

        def gate_scalars(g):
            gp = g % 2
            c0 = 2 * g
            def f1(e):
                e.activation(out=ee[gp][:, 0, :], in_=gt[:, c0:c0 + 2, 1], func=AF.Exp, scale=-1.0)
                return e.activation(out=ee[gp][:, 1, :], in_=gt[:, c0:c0 + 2, 3], func=AF.Exp, scale=-1.0)
            P.op('act', f1, reads=[f'gt{c0}', f'gt{c0 + 1}'], writes=[f'ee{gp}'])
            P.op('act', lambda e: e.activation(out=sp_[gp][:].rearrange("p a b -> p (a b)"),
                                               in_=ee[gp][:].rearrange("p a b -> p (a b)"), func=AF.Ln, bias=1.0),
                 reads=[f'ee{gp}'], writes=[f'sp{gp}'])
            def f2(e):
                e.matmul(pKV[:, 384:386], lhsT=ule[:], rhs=sp_[gp][:, 0, :], start=True, stop=True)
                e.matmul(pKV[:, 386:388], lhsT=uge[:], rhs=sp_[gp][:, 1, :], start=True, stop=True)
                return e.matmul(pKV[:, 388:392], lhsT=ones[:], rhs=sp_[gp][:].rearrange("p a b -> p (a b)"),
                                start=True, stop=True)
            P.op('pe', f2, reads=[f'sp{gp}', 'ule', 'uge', 'ones'], writes=['pKV'])
            def f3(e):
                e.tensor_tensor(out=ein[gp][:, 0, 0:2], in0=gt[:, c0:c0 + 2, 0], in1=pKV[:, 384:386], op=ALU.add)
                e.tensor_tensor(out=ein[gp][:, 0, 2:4], in0=gt[:, c0:c0 + 2, 2], in1=pKV[:, 386:388], op=ALU.add)
                e.tensor_copy(out=ein[gp][:, 1, :], in_=pKV[:, 384:388])
                return e.tensor_scalar(out=ein[gp][:, 3, :], in0=pKV[:, 388:392], scalar1=-1.0, scalar2=None,
                                       op0=ALU.mult)
            P.op('dve', f3, reads=['pKV', f'gt{c0}', f'gt{c0 + 1}'], writes=[f'ein{gp}a'])
            P.op('dve', lambda e: e.tensor_tensor(out=ein[gp][:, 2, :], in0=ein[gp][:, 0, :], in1=pKV[:, 388:392],
                                                  op=ALU.subtract),
                 reads=['pKV', f'ein{gp}a'], writes=[f'ein{gp}'])
            P.op('act', lambda e: e.activation(out=scal[:, g, :], in_=ein[gp][:].rearrange("p a b -> p (a b)"),
                                               func=AF.Exp),
                 reads=[f'ein{gp}', f'ein{gp}a'], writes=[f'scal{g}'])

        def sc_ap(g, k, d, ci, rows=128):
            i = k * 4 + d * 2 + ci
            return scal[0:rows, g, i:i + 1]

        def mlstm_chunk(c, d, part='both'):
            g = c // 2
            ci = c % 2
            par = c % 2
            mask, mname = (maskf, 'maskf') if d == 0 else (maskb, 'maskb')
            kv = pKV[0:64, par * 129:(par + 1) * 129]
            kvn = 'pKV'
            nd = pND[:, 0:129]
            cbi = c % 2
            if part in ('front', 'both'):
                mlstm_front(c, d, g, ci, par, mask, mname, kv, kvn)
            if part in ('tail', 'both'):
                mlstm_tail(c, d, g, ci, par, kv, kvn, nd, cbi)
            return nd, par

        def mlstm_front(c, d, g, ci, par, mask, mname, kv, kvn):
            def ftr(e):
                e.transpose(out=pQ[0:64, 0:128], in_=qk_tm[:, c, 0:64], identity=ident[:])
                return e.transpose(out=pQ[0:64, 128:256], in_=qk_tm[:, c, 64:128],
                                   identity=ident[:])
            P.op('pe', ftr, reads=[f'qk{c}', 'ident'], writes=['pQ'])
            P.op('act', lambda e: e.activation(out=qkT[par][:].rearrange("p a b -> p (a b)"),
                                               in_=pQ[0:64, 0:256], func=AF.Copy),
                 reads=['pQ'], writes=[f'qkT{par}'])
            P.op('pe', lambda e: e.matmul(pS[:, 0:128], lhsT=qkT[par][:, 1, :],
                                          rhs=qkT[par][:, 0, :], start=True, stop=True),
                 reads=[f'qkT{par}'], writes=['pS'])
            P.op('dve', lambda e: e.scalar_tensor_tensor(out=PTs[par][:], in0=pS[:, 0:128],
                                                         scalar=sc_ap(g, 0, d, ci), in1=mask[:], op0=ALU.mult,
                                                         op1=ALU.mult),
                 reads=['pS', f'scal{g}', mname], writes=[f'PTs{par}'])
            P.op('pool', lambda e: e.tensor_scalar(out=ku[par][:], in0=qk_tm[:, c, 64:128], scalar1=sc_ap(g, 2, d, ci),
                                                   scalar2=None, op0=ALU.mult),
                 reads=[f'qk{c}', f'scal{g}'], writes=[f'ku{par}'])
            P.op('pe', lambda e: e.matmul(kv, lhsT=ku[par][:], rhs=vp[:, c, 0:129], start=True, stop=True),
                 reads=[f'ku{par}', f'vp{c}', 'vp_ones'], writes=[kvn])

        def mlstm_tail(c, d, g, ci, par, kv, kvn, nd, cbi):
            def fnd(e):
                e.matmul(nd, lhsT=PTs[par][:], rhs=vp[:, c, 0:129], start=True, stop=False)
                return e.matmul(nd, lhsT=qkT[par][:, 0, :], rhs=Cb[cbi][:], start=False, stop=True)
            P.op('pe', fnd, reads=[f'PTs{par}', f'vp{c}', 'vp_ones', f'qkT{par}', f'Cb{cbi}'], writes=['pND'])
            P.op('dve', lambda e: e.tensor_scalar(out=dd[par][:, 0:1], in0=nd[:, 128:129], scalar1=-1.0, scalar2=None,
                                                  op0=ALU.mult),
                 reads=['pND'], writes=[f'dd{par}'])
            P.op('dve', lambda e: e.tensor_tensor(out=dd[par][:, 0:1], in0=nd[:, 128:129], in1=dd[par][:, 0:1],
                                                  op=ALU.max),
                 reads=['pND', f'dd{par}'], writes=[f'dd{par}'])
            P.op('dve', lambda e: e.tensor_scalar(out=dd[par][:, 0:1], in0=dd[par][:, 0:1], scalar1=sc_ap(g, 1, d, ci),
                                                  scalar2=None, op0=ALU.max),
                 reads=[f'dd{par}', f'scal{g}'], writes=[f'dd{par}'])
            P.op('dve', lambda e: e.reciprocal(out=dd[par][:, 1:2], in_=dd[par][:, 0:1]),
                 reads=[f'dd{par}'], writes=[f'ddr{par}'])
            P.op('dve', lambda e: e.scalar_tensor_tensor(out=Dst[:], in0=Dst[:], scalar=sc_ap(g, 3, d, ci, 64),
                                                         in1=kv, op0=ALU.mult, op1=ALU.add),
                 reads=[kvn, 'Dst', f'scal{g}'], writes=['Dst'])
            P.op('act', lambda e: e.activation(out=Cb[1 - cbi][:], in_=Dst[:], func=AF.Copy, scale=0.125),
                 reads=['Dst'], writes=[f'Cb{1 - cbi}'])

        def stageA(g):
            gp = g % 2
            for t in range(2):
                c = 2 * g + t
                xi = c % 4
                ld('sp', xt[xi][:], xb[c * 128:(c + 1) * 128, :], f'xt{xi}', f'D_x{xi}')
                ln_stats(xt[xi], f'xt{xi}', c, t)
                P.op('act', lambda e, xi=xi, t=t: e.activation(out=xn[t][:], in_=xt[xi][:], func=AF.Identity,
                                                               scale=rs[t][:, 0:1], bias=rs[t][:, 1:2]),
                     reads=[f'xt{xi}', f'rs{t}a', f'rs{t}b'], writes=[f'xn{t}'])
                def ftr(e, t=t):
                    inst = None
                    for j in range(8):
                        inst = e.transpose(out=pT[:, j // 4, j % 4, t, :], in_=xn[t][:, j * 128:(j + 1) * 128],
                                           identity=ident[:])
                    return inst
                P.op('pe', ftr, reads=[f'xn{t}', 'ident'], writes=['pTa', 'pTb'])
            for j in range(8):
                src = pT[:, j // 4, j % 4, :, :].rearrange("p a b -> p (a b)")
                dst = hT[gp][:, j, :]
                if j < 4:
                    P.op('act', lambda e, src=src, dst=dst, j=j: e.activation(out=dst, in_=src, func=AF.Identity,
                                                                              scale=sc1p[:, j:j + 1],
                                                                              bias=modT[:, j:j + 1]),
                         reads=['pTa', 'sc1p', 'modT0'], writes=[f'hT{gp}_{j}'])
                else:
                    P.op('dve', lambda e, src=src, dst=dst, j=j: e.tensor_scalar(out=dst, in0=src,
                                                                                 scalar1=sc1p[:, j:j + 1],
                                                                                 scalar2=modT[:, j:j + 1],
                                                                                 op0=ALU.mult, op1=ALU.add),
                         reads=['pTb', 'sc1p', 'modT0'], writes=[f'hT{gp}_{j}'])
            hnames = [f'hT{gp}_{j}' for j in range(8)]
            for t in range(2):
                c = 2 * g + t
                def ftm(e, t=t, gp=gp):
                    inst = None
                    for j in range(8):
                        inst = e.matmul(pTM[t][:, 0:388], lhsT=hT[gp][:, j, t * 128:(t + 1) * 128],
                                        rhs=w_sb[:, j, 0:388], start=(j == 0), stop=(j == 7))
                    return inst
                P.op('pe', ftm, reads=hnames + ['w_sb'], writes=['pTM'])
                P.op('act', lambda e, t=t, c=c: e.activation(out=vp[:, c, 0:128], in_=pTM[t][:, 128:256], func=AF.Copy),
                     reads=['pTM'], writes=[f'vp{c}'])
                P.op('act', lambda e, t=t: e.activation(out=so_t[t][:], in_=pTM[t][:, 256:384], func=AF.Sigmoid),
                     reads=['pTM'], writes=[f'so{t}'])
                P.op('dve', lambda e, t=t, c=c: e.tensor_copy(out=qk_tm[:, c, :], in_=pTM[t][:, 0:128]),
                     reads=['pTM'], writes=[f'qk{c}'])
                P.op('dve', lambda e, t=t, c=c: e.tensor_tensor(out=gt[:, c, :], in0=pTM[t][:, 384:388], in1=bg[:],
                                                                op=ALU.add),
                     reads=['pTM', 'bg'], writes=[f'gt{c}'])
                P.op('pool', lambda e, t=t, c=c: e.dma_start(out=so_d[c * 128:(c + 1) * 128, :], in_=so_t[t][:]),
                     reads=[f'so{t}'], writes=[f'so_d{c}'], dsem=f'D_so{t}')
                if t == 0:
                    def ffz(e, gp=gp):
                        inst = None
                        for j in range(8):
                            inst = e.matmul(pFZ[:, 0:256], lhsT=w_sb[:, j, 388:516], rhs=hT[gp][:, j, :],
                                            start=(j == 0), stop=(j == 7))
                        return inst
                    P.op('pe', ffz, reads=hnames + ['w_sb'], writes=['pFZ'])
                    P.op('act', lambda e, g=g: e.activation(out=fzT[:, g * 256:(g + 1) * 256], in_=pFZ[:, 0:256],
                                                            func=AF.Copy),
                         reads=['pFZ'], writes=[f'fzT{g}'])
            gate_scalars(g)
        def hf_out(c, nd, par):
            P.op('act', lambda e: e.activation(out=hf_t[par][:], in_=nd[:, 0:128], func=AF.Copy, scale=dd[par][:, 1:2]),
                 reads=['pND', f'ddr{par}'], writes=[f'hf{par}'])
            P.op('pool', lambda e: e.dma_start(out=hf_d[c * 128:(c + 1) * 128, :], in_=hf_t[par][:]),
                 reads=[f'hf{par}'], writes=[f'hf_d{c}'], dsem=f'D_hf{par}')

        P.op('pool', lambda e: e.memset(Dst[:], 0.0), writes=['Dst'])
        P.op('pool', lambda e: e.memset(Cb[0][:], 0.0), writes=['Cb0'])
        P.replay(P.record(lambda: stageA(0)))
        for g in range(NG):
            c0 = 2 * g
            f0 = P.record(lambda: mlstm_chunk(c0, 0, 'front'))
            t0_ = P.record(lambda: (mlstm_chunk(c0, 0, 'tail'), hf_out(c0, pND[:, 0:129], c0 % 2)))
            f1 = P.record(lambda: mlstm_chunk(c0 + 1, 0, 'front'))
            t1_ = P.record(lambda: (mlstm_chunk(c0 + 1, 0, 'tail'), hf_out(c0 + 1, pND[:, 0:129], (c0 + 1) % 2)))
            nxt = P.record(lambda: stageA(g + 1)) if g + 1 < NG else []
            n3 = len(nxt) // 3
            P.replay(f0, nxt[:n3])
            P.replay(t0_, f1, nxt[n3:2 * n3])
            P.replay(t1_, nxt[2 * n3:])
        if stop_after == 'P1':
            return finish()
        hs = [sb(f"hs{i}", [128, 128], F32) for i in range(2)]
        hn = [sb(f"hn{i}", [128, 128], F32) for i in range(2)]
        ym = [sb(f"ym{i}", [128, 128], BF16) for i in range(2)]
        sol = [sb(f"sol{i}", [128, 128], F32) for i in range(2)]
        hfl = [sb(f"hfl{i}", [128, 128], F32) for i in range(2)]
        bs2 = [sb(f"bs2{i}", [128, 6], F32) for i in range(2)]
        mv2 = [sb(f"mv2{i}", [128, 2], F32) for i in range(2)]
        rs2 = [sb(f"rs2{i}", [128, 2], F32) for i in range(2)]

        P.op('pool', lambda e: e.memset(Dst[:], 0.0), reads=[], writes=['Dst'])
        cb_first = (N1 - 1) % 2
        P.op('pool', lambda e: e.memset(Cb[cb_first][:], 0.0), writes=[f'Cb{cb_first}'])
        mt_m = dbg[:, 0:128] if not fused else mixg_m.ap()
        mt_f = dbg[:, 128:256] if not fused else mixg_f.ap()
        def s2_front(c):
            par = c % 2
            ld('sp', sol[par][:], so_d[c * 128:(c + 1) * 128, :], f'sol{par}', f'D_sol{par}', reads=[f'so_d{c}'])
            ld('sp', hfl[par][:], hf_d[c * 128:(c + 1) * 128, :], f'hfl{par}', f'D_hfl{par}', reads=[f'hf_d{c}'])
            P.op('pool', lambda e, par=par: e.tensor_tensor(out=sol[par][:], in0=sol[par][:], in1=nwb[:], op=ALU.mult),
                 reads=[f'sol{par}', 'nwb'], writes=[f'sol{par}'])
            mlstm_chunk(c, 1, 'front')

        def s2_tail(c):
            par = c % 2
            nd, _ = mlstm_chunk(c, 1, 'tail')
            P.op('dve', lambda e, nd=nd, par=par: e.scalar_tensor_tensor(out=hs[par][:], in0=nd[:, 0:128],
                                                                         scalar=dd[par][:, 1:2], in1=hfl[par][:],
                                                                         op0=ALU.mult, op1=ALU.add),
                 reads=['pND', f'ddr{par}', f'hfl{par}'], writes=[f'hs{par}'])
            P.op('dve', lambda e, par=par: e.bn_stats(out=bs2[par][:], in_=hs[par][:]),
                 reads=[f'hs{par}'], writes=[f'bs2{par}'])
            P.op('dve', lambda e, par=par: e.bn_aggr(out=mv2[par][:], in_=bs2[par][:]),
                 reads=[f'bs2{par}'], writes=[f'mv2{par}'])
            P.op('act', lambda e, par=par: e.activation(out=rs2[par][:, 0:1], in_=mv2[par][:, 1:2], func=AF.Sqrt,
                                                        bias=epsT[:, 0:1]),
                 reads=[f'mv2{par}', 'epsT'], writes=[f'rs2{par}a'])
            P.op('dve', lambda e, par=par: e.reciprocal(out=rs2[par][:, 0:1], in_=rs2[par][:, 0:1]),
                 reads=[f'rs2{par}a'], writes=[f'rs2{par}a'])
            P.op('dve', lambda e, par=par: e.tensor_scalar(out=rs2[par][:, 1:2], in0=mv2[par][:, 0:1],
                                                           scalar1=rs2[par][:, 0:1], scalar2=-1.0, op0=ALU.mult,
                                                           op1=ALU.mult),
                 reads=[f'rs2{par}a', f'mv2{par}'], writes=[f'rs2{par}b'])
            P.op('act', lambda e, par=par: e.activation(out=hn[par][:], in_=hs[par][:], func=AF.Identity,
                                                        scale=rs2[par][:, 0:1], bias=rs2[par][:, 1:2]),
                 reads=[f'hs{par}', f'rs2{par}a', f'rs2{par}b'], writes=[f'hn{par}'])
            P.op('pool', lambda e, par=par: e.tensor_tensor(out=ym[par][:], in0=hn[par][:], in1=sol[par][:],
                                                            op=ALU.mult),
                 reads=[f'hn{par}', f'sol{par}'], writes=[f'ym{par}'])
            P.op('pool', lambda e, par=par, c=c: e.dma_start(out=mt_m[c * 128:(c + 1) * 128, :],
                                                             in_=ym[par][:]),
                 reads=[f'ym{par}'], writes=[f'mixg_m_c{c}'], dsem=f'D_ym{par}')

        P.replay(P.record(lambda: s2_front(N1 - 1)))
        for c in range(N1 - 1, -1, -1):
            tl = P.record(lambda: s2_tail(c))
            fr = P.record(lambda: s2_front(c - 1)) if c > 0 else []
            P.replay(tl, fr)

        if stop_after == 'P2':
            return finish()
        flush()
        cur.pop(); sA.__exit__(None, None, None)
        sF = contextlib.ExitStack(); sF.__enter__(); cur.append(sF)
        piecesA = [i_ for i_ in range(NP_) if i_ % 2 == 0]
        piecesB = [i_ for i_ in range(NP_) if i_ % 2 == 1]

        def issue_ag(src, dst, i_, reads, wname):
            P.op('pool', lambda e: e.collective_compute(
                "AllGather", ALU.bypass, replica_groups=[[0, 1, 2, 3], [4, 5, 6, 7]],
                ins=[src.ap()[i_ * R_:(i_ + 1) * R_, :].opt()],
                outs=[dst.ap()[i_ * 4 * R_:(i_ + 1) * 4 * R_, :].opt()]),
                reads=reads, writes=[wname], dsem='CC', inc=1)
        cs_b = sb("cs_b", [128, 256], BF16)
        a1_b = sb("a1_b", [N1, 2 * N1], BF16)
        a2_b = sb("a2_b", [N1, 2 * N1], BF16)
        cn_b = sb("cn_b", [128, 256], BF16)
        tw = sb("tw", [128, 4 * N1], F32)
        ld('pool', cs_b[:], c_cs, 'cs_b', 'D_f0')
        ld('pool', a1_b[:], c_a1, 'a1_b', 'D_f1')
        ld('pool', a2_b[:], c_a2, 'a2_b', 'D_f2')
        ld('pool', cn_b[:], c_cn, 'cn_b', 'D_f3')
        ld('sp', tw[:], c_tw, 'tw', 'D_f4')
        G = sb("G", [128, 128, 256], BF16)
        Yr = sb("Yr", [128, N1, 128], BF16)
        Qp = sb("Qp", [128, N1, 128], BF16)
        yf = G[:].rearrange("p a b -> p (a b)")[:, 0:N1 * 128].rearrange("p (a b) -> p a b", b=128)
        t1 = [sb(f"t1_{i}", [128, 2 * N1], F32) for i in range(2)]
        t2 = [sb(f"t2_{i}", [128, 2 * N1], F32) for i in range(2)]
        pG = [pFZ, pTMb, pS, pKV]
        pGn = ['pFZ', 'pTM', 'pS', 'pKV']
        fz_all = [f'fzT{g}' for g in range(NG)]
        for i in range(64):
            k = i % 4
            eng = 'act' if k < 2 else 'dve'
            def f0(e, i=i, k=k):
                inst = None
                for q in range(2):
                    n2 = 2 * i + q
                    inst = e.matmul(pG[k][0:N1, q * 256:(q + 1) * 256], lhsT=fzT[:, n2:S:128], rhs=cs_b[:],
                                    start=True, stop=True)
                return inst
            P.op('pe', f0, reads=fz_all + ['cs_b'], writes=[pGn[k]])
            dst = G[0:N1, 2 * i:2 * i + 2, :].rearrange("p a b -> p (a b)")
            if eng == 'act':
                P.op('act', lambda e, k=k, dst=dst: e.activation(out=dst, in_=pG[k][0:N1, :], func=AF.Copy),
                     reads=[pGn[k]], writes=[f'G{i}'])
            else:
                P.op('dve', lambda e, k=k, dst=dst: e.tensor_copy(out=dst, in_=pG[k][0:N1, :]),
                     reads=[pGn[k]], writes=[f'G{i}'])
        g_all = [f'G{i}' for i in range(64)]
        pY = [pFZ, pTMb]
        pYn = ['pFZ', 'pTM']
        for j in range(128):
            k = j % 2
            def fa(e, j=j, k=k):
                e.matmul(pY[k][:, 0:2 * N1], lhsT=G[0:N1, :, j], rhs=a1_b[:], start=True, stop=False)
                return e.matmul(pY[k][:, 0:2 * N1], lhsT=G[0:N1, :, 128 + j], rhs=a2_b[:], start=False, stop=True)
            P.op('pe', fa, reads=g_all + ['a1_b', 'a2_b'], writes=[pYn[k]])
            P.op('dve', lambda e, k=k: e.tensor_tensor(out=t1[k][:], in0=pY[k][:, 0:2 * N1], in1=tw[:, 0:2 * N1],
                                                       op=ALU.mult),
                 reads=[pYn[k], 'tw'], writes=[f't1_{k}'])
            P.op('dve', lambda e, k=k: e.tensor_tensor(out=t2[k][:], in0=pY[k][:, 0:2 * N1],
                                                       in1=tw[:, 2 * N1:4 * N1], op=ALU.mult),
                 reads=[pYn[k], 'tw'], writes=[f't2_{k}'])
            P.op('pool', lambda e, k=k, j=j: e.tensor_tensor(out=Yr[:, :, j], in0=t1[k][:, 0:N1],
                                                             in1=t2[k][:, N1:2 * N1], op=ALU.subtract),
                 reads=[f't1_{k}', f't2_{k}'], writes=[f'Yr{j}'])
            P.op('pool', lambda e, k=k, j=j: e.tensor_tensor(out=Qp[:, :, j], in0=t1[k][:, N1:2 * N1],
                                                             in1=t2[k][:, 0:N1], op=ALU.add),
                 reads=[f't1_{k}', f't2_{k}'], writes=[f'Qp{j}'])
        y_all = [f'Yr{j}' for j in range(128)] + [f'Qp{j}' for j in range(128)]
        NB = N1 // 4
        pX = [pS, pKV]
        pXn = ['pS', 'pKV']
        for bi in range(NB):
            k = bi % 2
            def fc(e, bi=bi, k=k):
                e.matmul(pX[k][:, :], lhsT=cn_b[:, 0:128], rhs=Yr[:, 4 * bi:4 * bi + 4, :].rearrange("p a b -> p (a b)"),
                         start=True, stop=False)
                return e.matmul(pX[k][:, :], lhsT=cn_b[:, 128:256],
                                rhs=Qp[:, 4 * bi:4 * bi + 4, :].rearrange("p a b -> p (a b)"), start=False, stop=True)
            P.op('pe', fc, reads=y_all + ['cn_b'], writes=[pXn[k]])
            P.op('act', lambda e, bi=bi, k=k: e.activation(
                out=yf[:, 4 * bi:4 * bi + 4, :].rearrange("p a b -> p (a b)"), in_=pX[k][:, :], func=AF.Copy),
                reads=[pXn[k]], writes=[f'yf{bi}'])
        mt3 = mt_f.rearrange("(a b) j -> a b j", b=N1)
        npc = 4 if N1 >= 4 else 1
        for pc in range(npc):
            lo, hi = pc * N1 // npc, (pc + 1) * N1 // npc
            P.op('pool', lambda e, lo=lo, hi=hi: e.dma_start(out=mt3[:, lo:hi, :], in_=yf[:, lo:hi, :]),
                 reads=[f'yf{bi}' for bi in range(NB)], writes=['mixg_f'], dsem='D_yf')
        if fused:
            for i_ in piecesA:
                issue_ag(mixg_m, gath_m, i_, [f'mixg_m_c{cc}' for cc in range(i_ * CPP, (i_ + 1) * CPP)], 'gath_m_A')
        flush()
        cur.pop(); sF.__exit__(None, None, None)
        cur.pop(); sAF.__exit__(None, None, None)
        cur.pop(); sGA.__exit__(None, None, None)
        psA.__exit__(None, None, None)
        if not fused:
            return nc
        ctx = dict(nc=nc, P=P, es=es, cur=cur, flush=flush, ld=ld, sb=sb, xq=xq, w_out=w_out, ln1_g=ln1_g, ln1_b=ln1_b,
                   w_ff1=w_ff1, b_ff1T=b_ff1T, w_ff2=w_ff2, b_ff2=b_ff2, ln2_g=ln2_g, ln2_b=ln2_b, gidx=gidx, out=out,
                   gath_m=gath_m, gath_f=gath_f, mixg_f=mixg_f, mixg_m=mixg_m, issue_ag=issue_ag, piecesA=piecesA, piecesB=piecesB, CPP=CPP, ident=ident, ones=ones, epsT=epsT, modT=modT, sc2p=sc2p, g1row=g1row, g2row=g2row, cact=cact, badaT=badaT,
                   w_ada=w_ada, b_ada_row=b_ada_row)
        build_B(N1, ctx)
        return nc


def build_B(N1, ctx=None):
    S = 128 * N1
    TQ = S // 4
    NT = TQ // 128
    NGB = NT // 2
    fusedB = ctx is not None
    if not fusedB:
        nc = bass.Bass("TRN2", target_bir_lowering=False)
        P = Prog()

        def din(name, shape, dt=F32):
            return nc.dram_tensor(name, shape, dt, kind="ExternalInput").ap()
        xq = din("xq", [TQ, D])
        mixq = din("mixq", [TQ, D], BF16)
        cT = din("cT", [128, 8])
        w_ada = din("w_ada", [D, 6 * D])
        b_adaT = din("b_adaT", [128, 48])
        b_ada_row = din("b_ada_row", [1, 6 * D])
        w_out = din("w_out", [D, D])
        ln1_g = din("ln1_g", [1, D]); ln1_b = din("ln1_b", [1, D])
        w_ff1 = din("w_ff1", [D, DFF]); b_ff1T = din("b_ff1T", [128, 32])
        w_ff2 = din("w_ff2", [DFF, D]); b_ff2 = din("b_ff2", [1, D])
        ln2_g = din("ln2_g", [1, D]); ln2_b = din("ln2_b", [1, D])
        c_ident = din("c_ident", [128, 128])
        out = nc.dram_tensor("out", [TQ, D], F32, kind="ExternalOutput").ap()
        es = contextlib.ExitStack()
        es.__enter__()
        cur = [es]
        sems = {}

        def sb(name, shape, dt=F32):
            return cur[-1].enter_context(nc.sbuf_tensor(name, shape, dt))

        def flush():
            P.barrier_all('sp')
            for sname in P.semnames:
                if sname not in sems:
                    sems[sname] = es.enter_context(nc.semaphore(sname))
            with nc.Block() as block:
                P.emit(nc, sems, block)
            for e_ in P.ENG:
                P.streams[e_] = []
            for e_ in P.ENG:
                P.barrier_all(e_)

        def ld(eng, dst, src, name, sem, reads=()):
            P.op(eng, lambda e: e.dma_start(out=dst, in_=src), reads=reads, writes=[name], dsem=sem)
    else:
        nc = ctx['nc']; P = ctx['P']; es = ctx['es']; cur = ctx['cur']; flush = ctx['flush']; ld = ctx['ld']; sb = ctx['sb']
        xq = ctx['xq']; w_out = ctx['w_out']; ln1_g = ctx['ln1_g']; ln1_b = ctx['ln1_b']; w_ff1 = ctx['w_ff1']
        b_ff1T = ctx['b_ff1T']; w_ff2 = ctx['w_ff2']; b_ff2 = ctx['b_ff2']; ln2_g = ctx['ln2_g']; ln2_b = ctx['ln2_b']
        gidx = ctx['gidx']; out = ctx['out']; gath_m = ctx['gath_m']; gath_f = ctx['gath_f']; mixg_f = ctx['mixg_f']
    psB = contextlib.ExitStack()
    psB.__enter__()

    def ps(name, shape, dt=F32):
        return psB.enter_context(nc.psum_tensor(name, shape, dt))
    if True:
        if fusedB:
            ident = ctx['ident']; ones = ctx['ones']; epsT = ctx['epsT']; modT = ctx['modT']; sc2p = ctx['sc2p']
            g1row = ctx['g1row']; g2row = ctx['g2row']; cact = ctx['cact']; badaT = ctx['badaT']
            w_ada = ctx['w_ada']; b_ada_row = ctx['b_ada_row']
        else:
            ident = sb("ident", [128, 128], BF16)
            ones = sb("ones", [128, 128], F32)
            cact = sb("cact", [128, 8], F32)
            modT = sb("modT", [128, 48], F32)
            badaT = sb("badaT", [128, 48], F32)
            sc2p = sb("sc2p", [128, 8], F32)
            epsT = sb("epsT", [128, 1], F32)
            g1row = nc.dram_tensor("g1row", [1, 1024], F32).ap()
            g2row = nc.dram_tensor("g2row", [1, 1024], F32).ap()
        ones_b = sb("ones_b", [1, 128], BF16)
        b1T = sb("b1T", [128, 32], F32)
        bff2_b = sb("bff2_b", [1, 1024], BF16)
        l1g = sb("l1g", [128, 1024], F32); l1b = sb("l1b", [128, 1024], F32)
        l2g = sb("l2g", [128, 1024], F32); l2b = sb("l2b", [128, 1024], F32)
        pb_ = [ps(f"pb{i}", [128, 512], F32) for i in range(7)]
        pTr = ps("pTr", [128, 8, 128], BF16)
        P.psum_names |= {f'pb{i}' for i in range(7)} | {'pTr'}
        ld('sp', b1T[:], b_ff1T, 'b1T', 'D_b1')
        ld('pool', bff2_b[:], b_ff2, 'bff2_b', 'D_b2')
        ld('sp', l1g[:], ln1_g.broadcast_to([128, 1024]), 'l1g', 'D_l1g')
        ld('sp', l1b[:], ln1_b.broadcast_to([128, 1024]), 'l1b', 'D_l1b')
        ld('sp', l2g[:], ln2_g.broadcast_to([128, 1024]), 'l2g', 'D_l2g')
        ld('sp', l2b[:], ln2_b.broadcast_to([128, 1024]), 'l2b', 'D_l2b')
        P.op('pool', lambda e: e.memset(ones_b[:], 1.0), writes=['ones_b'])
        if fusedB:
            gix = sb("gix", [128, 4 * NT], mybir.dt.int32)
            ld('sp', gix[:], gidx, 'gix', 'D_gix')
        if fusedB:
            issue_ag = ctx['issue_ag']; mixg_m = ctx['mixg_m']; CPP = ctx['CPP']
            for i_ in ctx['piecesA']:
                issue_ag(mixg_f, gath_f, i_, ['mixg_f'], 'gath_f_A')
            for i_ in ctx['piecesB']:
                issue_ag(mixg_m, gath_m, i_, [f'mixg_m_c{cc}' for cc in range(i_ * CPP, (i_ + 1) * CPP)], 'gath_m_B')
            for i_ in ctx['piecesB']:
                issue_ag(mixg_f, gath_f, i_, ['mixg_f'], 'gath_f_B')
        if not fusedB:
            ld('pool', ident[:], c_ident, 'ident', 'D_c0')
            ld('sp', cact[:], cT, 'cact', 'D_c7')
            ld('sp', badaT[:], b_adaT, 'badaT', 'D_c8')
            P.op('pool', lambda e: e.memset(ones[:], 1.0), writes=['ones'])
            P.op('pool', lambda e: e.memset(epsT[:], LN_EPS), writes=['epsT'])
            P.op('act', lambda e: e.activation(out=cact[:], in_=cact[:], func=AF.Silu), reads=['cact'], writes=['cact'])
        if True:
            s0 = contextlib.ExitStack(); s0.__enter__(); cur.append(s0)
            wa = [sb(f"waB{i}", [128, 8, 1024], F32) for i in range(2)]
            crep = sb("crepB", [128, 8, 128], F32)
            g1b = sb("g1bB", [128, 1024], F32)
            g2b = sb("g2bB", [128, 1024], F32)

            def mk_crep(e):
                i = None
                for j in range(8):
                    i = e.tensor_scalar(out=crep[:, j, :], in0=ones[:], scalar1=cact[:, j:j + 1], scalar2=None,
                                        op0=ALU.mult)
                return i
            P.op('dve', mk_crep, reads=['cact', 'ones'], writes=['crep'])

            def load_wa(i, part):
                ld('sp', wa[i][:], w_ada[:, part * 1024:(part + 1) * 1024].rearrange("(j p) n -> p j n", p=128),
                   f'wa{i}', f'D_wa{i}')

            def mod_featmajor(i, part):
                def f(e):
                    inst = None
                    for m in range(8):
                        for j in range(8):
                            inst = e.matmul(pb_[0][:, m:m + 1], lhsT=wa[i][:, j, m * 128:(m + 1) * 128],
                                            rhs=cact[:, j:j + 1], start=(j == 0), stop=(j == 7))
                    return inst
                P.op('pe', f, reads=[f'wa{i}', 'cact'], writes=['pb0'])
                P.op('dve', lambda e: e.tensor_tensor(out=modT[:, part * 8:(part + 1) * 8], in0=pb_[0][:, 0:8],
                                                      in1=badaT[:, part * 8:(part + 1) * 8], op=ALU.add),
                     reads=['pb0', 'badaT'], writes=[f'modT{part}'])

            def mod_rowbcast(i, part, dst, dname):
                ld('sp', dst[:], b_ada_row[:, part * 1024:(part + 1) * 1024].broadcast_to([128, 1024]), dname,
                   f'D_{dname}')
                for h in range(2):
                    def f(e, h=h):
                        inst = None
                        for j in range(8):
                            inst = e.matmul(pb_[1][:, :], lhsT=crep[:, j, :], rhs=wa[i][:, j, h * 512:(h + 1) * 512],
                                            start=(j == 0), stop=(j == 7))
                        return inst
                    P.op('pe', f, reads=[f'wa{i}', 'crep'], writes=['pb1'])
                    P.op('dve', lambda e, h=h: e.scalar_tensor_tensor(
                        out=dst[:, h * 512:(h + 1) * 512], in0=dst[:, h * 512:(h + 1) * 512], scalar=1.0,
                        in1=pb_[1][:, :], op0=ALU.add, op1=ALU.add), reads=['pb1', dname], writes=[dname])

            load_wa(0, 2); load_wa(1, 3)
            mod_rowbcast(0, 2, g1b, 'g1b')
            mod_featmajor(1, 3)
            load_wa(0, 4); load_wa(1, 5)
            mod_featmajor(0, 4)
            P.op('dve', lambda e: e.tensor_scalar(out=sc2p[:], in0=modT[:, 32:40], scalar1=1.0, scalar2=None,
                                                  op0=ALU.add), reads=['modT4'], writes=['sc2p'])
            mod_rowbcast(1, 5, g2b, 'g2b')
            P.op('sp', lambda e: e.dma_start(out=g1row, in_=g1b[0:1, :]), reads=['g1b'], writes=['g1row'], dsem='D_g1r')
            P.op('sp', lambda e: e.dma_start(out=g2row, in_=g2b[0:1, :]), reads=['g2b'], writes=['g2row'], dsem='D_g2r')
            flush()
            cur.pop(); s0.__exit__(None, None, None)

        wout = sb("wout", [128, 8, 1024], BF16)
        wff1 = sb("wff1", [128, 8, 4096], BF16)
        wff2 = sb("wff2", [128, 32, 1024], BF16)
        sT = contextlib.ExitStack(); sT.__enter__(); cur.append(sT)
        tg1 = sb("tg1", [128, 1024], F32)
        tg2 = sb("tg2", [128, 1024], F32)
        stg = [sb(f"stg{i}", [128, 2048], F32) for i in range(3)]
        ld('sp', tg1[:], g1row.broadcast_to([128, 1024]), 'tg1', 'D_tg1', reads=['g1row'])
        ld('sp', tg2[:], g2row.broadcast_to([128, 1024]), 'tg2', 'D_tg2', reads=['g2row'])
        si = [0]

        def stage(src_ap, shape3=None):
            k = si[0] % 3
            si[0] += 1
            dst = stg[k][:] if shape3 is None else stg[k][:].rearrange("p (a b) -> p a b", a=shape3)
            ld('sp', dst, src_ap, f'stg{k}', f'D_stg{k}')
            return k
        for jj in range(4):
            k = stage(w_out[jj * 256:(jj + 1) * 256, :].rearrange("(a p) n -> p a n", p=128), 2)
            for a in range(2):
                P.op('dve', lambda e, k=k, a=a, jj=jj: e.tensor_tensor(out=wout[:, 2 * jj + a, :],
                                                                     in0=stg[k][:, a * 1024:(a + 1) * 1024],
                                                                     in1=tg1[:], op=ALU.mult),
                     reads=[f'stg{k}', 'tg1'], writes=['wout'])
        for j in range(8):
            for hh in range(2):
                k = stage(w_ff1[j * 128:(j + 1) * 128, hh * 2048:(hh + 1) * 2048])
                P.op('act', lambda e, k=k, j=j, hh=hh: e.activation(out=wff1[:, j, hh * 2048:(hh + 1) * 2048],
                                                                  in_=stg[k][:], func=AF.Copy),
                     reads=[f'stg{k}'], writes=['wff1'])
        for ff in range(16):
            k = stage(w_ff2[ff * 256:(ff + 1) * 256, :].rearrange("(a p) n -> p a n", p=128), 2)
            for a in range(2):
                f_ = 2 * ff + a
                eng_ = 'dve' if a == 0 else 'pool'
                P.op(eng_, lambda e, k=k, a=a, f_=f_: e.tensor_tensor(out=wff2[:, f_, :],
                                                                     in0=stg[k][:, a * 1024:(a + 1) * 1024],
                                                                     in1=tg2[:], op=ALU.mult),
                     reads=[f'stg{k}', 'tg2'], writes=[f'wff2_{f_}'])
        P.op('dve', lambda e: e.tensor_tensor(out=bff2_b[:], in0=bff2_b[:], in1=tg2[0:1, :], op=ALU.mult),
             reads=['bff2_b', 'tg2'], writes=['bff2_b'])
        w2n = [f'wff2_{f_}' for f_ in range(32)]
        flush()
        cur.pop(); sT.__exit__(None, None, None)

        xt = sb("xt", [128, 1024], F32)
        mxs = [sb(f"mx{i}", [128, 1024], BF16) for i in range(2)]
        x1 = sb("x1", [128, 4, 1024], F32)
        xn = sb("xn", [128, 1024], BF16)
        mixT = xn[:].rearrange("p (a b) -> p a b", b=128)
        h2T = [sb(f"h2TB{i}", [128, 8, 256], BF16) for i in range(2)]
        uT = sb("uT", [128, 8, 256], BF16)
        rt = [sb(f"rtB{i}", [128, 256], F32) for i in range(2)]
        bst = [sb(f"bstB{i}", [128, 2, 6], F32) for i in range(2)]
        mv = [sb(f"mvB{i}", [128, 2], F32) for i in range(2)]
        rs = [sb(f"rsB{i}", [128, 2], F32) for i in range(2)]

        def ln_stats(src, sname, k):
            def f(e):
                e.bn_stats(out=bst[k][:, 0, :], in_=src[:, 0:512])
                return e.bn_stats(out=bst[k][:, 1, :], in_=src[:, 512:1024])
            P.op('dve', f, reads=[sname], writes=[f'bst{k}'])
            yield
            P.op('dve', lambda e: e.bn_aggr(out=mv[k][:], in_=bst[k][:].rearrange("p a b -> p (a b)")),
                 reads=[f'bst{k}'], writes=[f'mv{k}'])
            yield
            P.op('act', lambda e: e.activation(out=rs[k][:, 0:1], in_=mv[k][:, 1:2], func=AF.Sqrt, bias=epsT[:, 0:1]),
                 reads=[f'mv{k}', 'epsT'], writes=[f'rsa{k}'])
            yield
            P.op('dve', lambda e: e.reciprocal(out=rs[k][:, 0:1], in_=rs[k][:, 0:1]), reads=[f'rsa{k}'],
                 writes=[f'rsa{k}'])
            yield
            P.op('dve', lambda e: e.tensor_scalar(out=rs[k][:, 1:2], in0=mv[k][:, 0:1], scalar1=rs[k][:, 0:1],
                                                  scalar2=-1.0, op0=ALU.mult, op1=ALU.mult),
                 reads=[f'rsa{k}', f'mv{k}'], writes=[f'rsb{k}'])
            yield

        def prologue(g):
            hp = g % 2
            for t in range(2):
                r0 = (2 * g + t) * 128
                mx = mxs[t]
                if not fusedB:
                    ld('sp', mx[:], mixq[r0:r0 + 128, :], f'mx{t}', f'D_m{t}')
                else:
                    T_ = 2 * g + t
                    for s_ in range(8):
                        gsrc = gath_m if s_ < 4 else gath_f
                        sr = s_ % 4
                        P.op('pool', lambda e, s_=s_, sr=sr, T_=T_, gsrc=gsrc, mx=mx: e.indirect_dma_start(
                            out=mx[:, s_ * 128:(s_ + 1) * 128], out_offset=None, in_=gsrc.ap()[:, :],
                            in_offset=bass.IndirectOffsetOnAxis(ap=gix[:, sr * NT + T_:sr * NT + T_ + 1], axis=0)),
                            reads=(['gath_m_A', 'gath_f_A', 'gix'] if (TQ == 2 * (S // max(1, S // 2048)) and T_ < NT // 2) else
                                   ['gath_m_A', 'gath_f_A', 'gath_m_B', 'gath_f_B', 'gix']),
                            writes=[f'mx{t}'], dsem=f'D_m{t}')
                yield
            for t in range(2):
                r0 = (2 * g + t) * 128
                xi = (2 * g + t) % 4
                x1t = x1[:, xi, :]
                xname = f'x1_{xi}'
                mx = mxs[t]
                ld('sp', xt[:], xq[r0:r0 + 128, :], 'xt', 'D_x')
                yield

                def ftr(e, mx=mx):
                    inst = None
                    for j in range(8):
                        inst = e.transpose(out=pTr[:, j, :], in_=mx[:, j * 128:(j + 1) * 128], identity=ident[:])
                    return inst
                P.op('pe', ftr, reads=[f'mx{t}', 'ident'], writes=['pTr'])
                yield
                P.op('act', lambda e: e.activation(out=xn[:], in_=pTr[:].rearrange("p a b -> p (a b)"), func=AF.Copy),
                     reads=['pTr'], writes=['xn'])
                yield
                for hf in range(2):
                    def fo(e, hf=hf):
                        inst = None
                        for j in range(8):
                            inst = e.matmul(pb_[0][:, :], lhsT=mixT[:, j, :], rhs=wout[:, j, hf * 512:(hf + 1) * 512],
                                            start=(j == 0), stop=(j == 7))
                        return inst
                    P.op('pe', fo, reads=['xn', 'wout'], writes=['pb0'])
                    yield
                    P.op('dve', lambda e, hf=hf: e.scalar_tensor_tensor(
                        out=xt[:, hf * 512:(hf + 1) * 512], in0=xt[:, hf * 512:(hf + 1) * 512], scalar=ALPHA,
                        in1=pb_[0][:, :], op0=ALU.mult, op1=ALU.add), reads=['pb0', 'xt'], writes=['xt'])
                    yield
                yield from ln_stats(xt, 'xt', 0)
                P.op('act', lambda e, x1t=x1t: e.activation(out=x1t, in_=xt[:], func=AF.Identity, scale=rs[0][:, 0:1],
                                                            bias=rs[0][:, 1:2]),
                     reads=['xt', 'rsa0', 'rsb0'], writes=[xname])
                yield
                P.op('dve', lambda e, x1t=x1t: e.tensor_tensor(out=x1t, in0=x1t, in1=l1g[:], op=ALU.mult),
                     reads=[xname, 'l1g'], writes=[xname])
                yield
                P.op('dve', lambda e, x1t=x1t: e.tensor_tensor(out=x1t, in0=x1t, in1=l1b[:], op=ALU.add),
                     reads=[xname, 'l1b'], writes=[xname])
                yield
                yield from ln_stats(x1t, xname, 0)
                P.op('act', lambda e, x1t=x1t: e.activation(out=xn[:], in_=x1t, func=AF.Identity, scale=rs[0][:, 0:1],
                                                            bias=rs[0][:, 1:2]),
                     reads=[xname, 'rsa0', 'rsb0'], writes=['xn'])
                yield

                def ftr2(e):
                    inst = None
                    for j in range(8):
                        inst = e.transpose(out=pTr[:, j, :], in_=xn[:, j * 128:(j + 1) * 128], identity=ident[:])
                    return inst
                P.op('pe', ftr2, reads=['xn', 'ident'], writes=['pTr'])
                yield
                for j in range(8):
                    P.op('act', lambda e, j=j, t=t, hp=hp: e.activation(out=h2T[hp][:, j, t * 128:(t + 1) * 128],
                                                                       in_=pTr[:, j, :], func=AF.Identity,
                                                                       scale=sc2p[:, j:j + 1],
                                                                       bias=modT[:, 24 + j:25 + j]),
                         reads=['pTr', 'sc2p', 'modT3'], writes=[f'h2T{hp}_{j}'])
                    if j % 2 == 1:
                        yield

        def ffn(g):
            hp = g % 2
            hn = [f'h2T{hp}_{j}' for j in range(8)]
            for fh in range(4):
                for fc in range(8):
                    f_ = fh * 8 + fc
                    k = 1 + fc % 2

                    def f1(e, f_=f_, k=k):
                        inst = None
                        for j in range(8):
                            inst = e.matmul(pb_[k][:, 0:256], lhsT=wff1[:, j, f_ * 128:(f_ + 1) * 128],
                                            rhs=h2T[hp][:, j, :], start=(j == 0), stop=(j == 7))
                        return inst
                    P.op('pe', f1, reads=hn + ['wff1'], writes=[f'pb{k}'])
                    P.op('act', lambda e, f_=f_, k=k: e.activation(out=rt[k - 1][:], in_=pb_[k][:, 0:256], func=AF.Relu,
                                                                   bias=b1T[:, f_:f_ + 1]),
                         reads=[f'pb{k}', 'b1T'], writes=[f'rt{k}'])
                    sq_eng = 'dve'
                    P.op(sq_eng, lambda e, fc=fc, k=k: e.tensor_tensor(out=uT[:, fc, :], in0=rt[k - 1][:],
                                                                      in1=rt[k - 1][:], op=ALU.mult),
                         reads=[f'rt{k}'], writes=[f'uT{fc}'])
                    yield
                un = [f'uT{fc}' for fc in range(8)]
                for t in range(2):
                    for ch in range(2):
                        bk = 3 + t * 2 + ch

                        def f2(e, t=t, ch=ch, bk=bk, fh=fh):
                            inst = None
                            for fc in range(8):
                                f_ = fh * 8 + fc
                                inst = e.matmul(pb_[bk][:, :], lhsT=uT[:, fc, t * 128:(t + 1) * 128],
                                                rhs=wff2[:, f_, ch * 512:(ch + 1) * 512],
                                                start=(f_ == 0), stop=False, skip_group_check=True)
                            if fh == 3:
                                inst = e.matmul(pb_[bk][:, :], lhsT=ones_b[0:1, :],
                                                rhs=bff2_b[0:1, ch * 512:(ch + 1) * 512],
                                                start=False, stop=True, skip_group_check=True)
                            return inst
                        P.op('pe', f2, reads=un + w2n + ['ones_b', 'bff2_b'], writes=[f'pb{bk}'])
                        yield

        def epilogue(g):
            for t in range(2):
                r0 = (2 * g + t) * 128
                xi = (2 * g + t) % 4
                x1t = x1[:, xi, :]
                xname = f'x1_{xi}'
                for ch in range(2):
                    bk = 3 + t * 2 + ch
                    P.op('dve', lambda e, ch=ch, bk=bk, xi=xi: e.scalar_tensor_tensor(
                        out=x1[:, xi, ch * 512:(ch + 1) * 512], in0=x1[:, xi, ch * 512:(ch + 1) * 512], scalar=ALPHA,
                        in1=pb_[bk][:, :], op0=ALU.mult, op1=ALU.add), reads=[f'pb{bk}', xname], writes=[xname])
                    yield
                yield from ln_stats(x1t, xname, 1)
                P.op('act', lambda e, x1t=x1t: e.activation(out=x1t, in_=x1t, func=AF.Identity, scale=rs[1][:, 0:1],
                                                            bias=rs[1][:, 1:2]),
                     reads=[xname, 'rsa1', 'rsb1'], writes=[xname])
                yield
                P.op('pool', lambda e, x1t=x1t: e.tensor_tensor(out=x1t, in0=x1t, in1=l2g[:], op=ALU.mult),
                     reads=[xname, 'l2g'], writes=[xname])
                yield
                P.op('pool', lambda e, x1t=x1t: e.tensor_tensor(out=x1t, in0=x1t, in1=l2b[:], op=ALU.add),
                     reads=[xname, 'l2b'], writes=[xname])
                yield
                P.op('sp', lambda e, r0=r0, x1t=x1t: e.dma_start(out=out[r0:r0 + 128, :], in_=x1t),
                     reads=[xname], writes=['out'], dsem=f'D_out{xi}')
                yield

        def interleave(*gens):
            gens = list(gens)
            while gens:
                for gen in list(gens):
                    try:
                        next(gen)
                    except StopIteration:
                        gens.remove(gen)

        interleave(prologue(0))
        for g in range(NGB):
            if g + 1 < NGB:
                interleave(ffn(g), prologue(g + 1))
            else:
                interleave(ffn(g))
            interleave(epilogue(g))
        flush()
    psB.__exit__(None, None, None)
    if not fusedB:
        es.__exit__(None, None, None)
    return nc


def _consts(N1):
    S = 128 * N1
    i = np.arange(128)
    ident = np.eye(128, dtype=np.float32)
    le = (i[:, None] <= i[None, :]).astype(np.float32)
    ge = (i[:, None] >= i[None, :]).astype(np.float32)
    ang = 2 * np.pi * np.outer(i, i) / 128.0
    c128, s128 = np.cos(ang), np.sin(ang)
    a = np.arange(N1)
    angA = 2 * np.pi * np.outer(a, a) / N1
    cA, sA = np.cos(angA), np.sin(angA)
    angT = 2 * np.pi * np.outer(i, a) / S
    tc, ts = np.cos(angT), np.sin(angT)
    nrm = 1.0 / np.sqrt(S * 128.0)
    return {
        "c_ident": ident, "c_maskf": 0.125 * le, "c_maskb": 0.125 * ge, "c_ule": le, "c_uge": ge,
        "c_cs": np.concatenate([c128, s128], 1).astype(np.float32),
        "c_a1": np.concatenate([cA, sA], 1).astype(np.float32),
        "c_a2": np.concatenate([-sA, cA], 1).astype(np.float32),
        "c_tw": np.concatenate([tc, tc, ts, ts], 1).astype(np.float32),
        "c_cn": (np.concatenate([c128, -s128], 1) * nrm).astype(np.float32),
    }


def make_in_maps(inp, N1, fused=False):
    S = 128 * N1
    TQ = S // 4
    cst = _consts(N1)
    f = lambda a: np.ascontiguousarray(np.asarray(a, dtype=np.float32))
    maps = []
    for core in range(NCORES):
        b, g = core // 4, core % 4
        cols = np.concatenate([
            np.arange(64) + 64 * g,
            256 + np.arange(64) + 64 * g,
            512 + np.arange(128) + 128 * g,
            1024 + np.arange(128) + 128 * g,
            2048 + np.array([g, 4 + g, 8 + g, 12 + g]),
            1536 + np.arange(128) + 128 * g,
        ])
        m = {
            "xb": f(inp["x"][b]),
            "xq": f(inp["x"][b, g * TQ:(g + 1) * TQ]),
            "cT": f(inp["c"][b].reshape(8, 128).T),
            "w_ada": f(inp["w_ada"][0]),
            "b_adaT": f(inp["b_ada"][0].reshape(48, 128).T),
            "b_ada_row": f(inp["b_ada"][0].reshape(1, -1)),
            "w_in": f(inp["w_in"][0][:, cols]),
            "b_gate": f(inp["b_gate"][0][[g, 4 + g, 8 + g, 12 + g]].reshape(1, 4)),
            "nw": f(inp["mlstm_norm_w"][0][128 * g:128 * (g + 1)].reshape(1, 128)),
            "w_out": f(inp["w_out"][0]),
            "ln1_g": f(inp["ln1_g"][0].reshape(1, -1)), "ln1_b": f(inp["ln1_b"][0].reshape(1, -1)),
            "w_ff1": f(inp["w_ff1"][0]), "b_ff1T": f(inp["b_ff1"][0].reshape(32, 128).T),
            "w_ff2": f(inp["w_ff2"][0]), "b_ff2": f(inp["b_ff2"][0].reshape(1, -1)),
            "ln2_g": f(inp["ln2_g"][0].reshape(1, -1)), "ln2_b": f(inp["ln2_b"][0].reshape(1, -1)),
        }
        if fused:
            NT = TQ // 128
            NP_ = max(1, S // 2048)
            R_ = S // NP_
            gi = np.empty((128, 4 * NT), np.int32)
            for s_ in range(4):
                for T_ in range(NT):
                    n_ = g * TQ + T_ * 128 + np.arange(128)
                    gi[:, s_ * NT + T_] = (n_ // R_) * 4 * R_ + s_ * R_ + n_ % R_
            m["gidx"] = gi
        m.update(cst)
        maps.append(m)
    return maps


def _bf16_to_mixq(results, N1):
    S = 128 * N1
    TQ = S // 4
    mixqs = []
    for core in range(NCORES):
        b, r = core // 4, core % 4
        parts_m = [results[4 * b + g]["dbg"][r * TQ:(r + 1) * TQ, 0:128] for g in range(4)]
        parts_f = [results[4 * b + g]["dbg"][r * TQ:(r + 1) * TQ, 128:256] for g in range(4)]
        mixqs.append(np.ascontiguousarray(np.concatenate(parts_m + parts_f, axis=1)))
    return mixqs


def kernel(**inputs):
    N1 = 128
    S = 128 * N1
    TQ = S // 4
    maps = make_in_maps(inputs, N1, fused=True)
    nc = build_nc(N1, fused=True)
    res = run_bass_kernel_spmd(nc, maps, core_ids=list(range(NCORES)))
    outp = np.empty((2, S, D), np.float32)
    for core in range(NCORES):
        b, g = core // 4, core % 4
        outp[b, g * TQ:(g + 1) * TQ] = np.asarray(res.results[core]["out"], dtype=np.float32)
    return outp
```

```python
import contextlib
import numpy as np
import concourse.bass as bass
import concourse.mybir as mybir
from concourse.bass_utils import run_bass_kernel_spmd

F32 = mybir.dt.float32
BF16 = mybir.dt.bfloat16
AF = mybir.ActivationFunctionType
ALU = mybir.AluOpType

D = 1024
DFF = 4096
LN_EPS = 1e-5
ALPHA = 2.0 ** 0.25
NCORES = 8


class Prog:
    ENG = ('pe', 'act', 'dve', 'pool', 'sp')

    def __init__(self):
        self.streams = {e: [] for e in self.ENG}
        self.cnt = {}
        self.lastw = {}
        self.rd = {}
        self.waited = {e: {} for e in self.ENG}
        self.semnames = ['S_' + e for e in self.ENG]
        self.psum_names = set()
        self.pacc = {}
        self._rec = None

    def record(self, body):
        assert self._rec is None
        self._rec = []
        body()
        r = self._rec
        self._rec = None
        return r

    def replay(self, *lists):
        lists = [l for l in lists if l]
        pos = [0] * len(lists)
        total = sum(len(l) for l in lists)
        for _ in range(total):
            best = None
            for i, l in enumerate(lists):
                if pos[i] < len(l):
                    frac = pos[i] / len(l)
                    if best is None or frac < best[0]:
                        best = (frac, i)
            i = best[1]
            self.op(*lists[i][pos[i]])
            pos[i] += 1

    def op(self, eng, fn, reads=(), writes=(), dsem=None, inc=None):
        if self._rec is not None:
            self._rec.append((eng, fn, tuple(reads), tuple(writes), dsem, inc))
            return None
        d = {}

        def add(t):
            if t is None:
                return
            s, v = t
            if d.get(s, 0) < v:
                d[s] = v
        for r in reads:
            add(self.lastw.get(r))
        for w in writes:
            add(self.lastw.get(w))
            for s, v in self.rd.get(w, {}).items():
                add((s, v))
        for n in list(reads) + list(writes):
            if n in self.psum_names:
                for e2, t2 in self.pacc.get(n, {}).items():
                    if e2 != eng:
                        add(t2)
        waits = []
        for s, v in d.items():
            if eng == 'pe' and s == 'S_pe':
                continue
            if self.waited[eng].get(s, 0) >= v:
                continue
            self.waited[eng][s] = v
            waits.append((s, v))
        if dsem is None:
            s = 'S_' + eng
            inc = 1
        else:
            s = dsem
            inc = 16 if inc is None else inc
            if s not in self.semnames:
                self.semnames.append(s)
        self.cnt[s] = self.cnt.get(s, 0) + inc
        t = (s, self.cnt[s])
        for r in reads:
            m = self.rd.setdefault(r, {})
            if m.get(s, 0) < t[1]:
                m[s] = t[1]
        for w in writes:
            self.lastw[w] = t
            self.rd[w] = {}
        for n in list(reads) + list(writes):
            if n in self.psum_names:
                self.pacc.setdefault(n, {})[eng] = t
        self.streams[eng].append((waits, fn, s, inc))
        return t

    def barrier_all(self, eng):
        waits = []
        for s, v in self.cnt.items():
            if s == 'CC':
                continue
            if self.waited[eng].get(s, 0) >= v:
                continue
            self.waited[eng][s] = v
            waits.append((s, v))
        self.streams[eng].append((waits, None, None, 0))

    def emit(self, nc, sems, block):
        def run(engname):
            def body(eng):
                for waits, fn, s, inc in self.streams[engname]:
                    for ws, wv in waits:
                        eng.wait_ge(sems[ws], wv)
                    if fn is None:
                        continue
                    inst = fn(eng)
                    inst.then_inc(sems[s], inc)
            return body
        block.tensor(run('pe'))
        block.scalar(run('act'))
        block.vector(run('dve'))
        block.gpsimd(run('pool'))
        block.sync(run('sp'))


def build_nc(N1, stop_after=None, fused=False):
    S = 128 * N1
    TQ = S // 4
    nc = bass.Bass("TRN2", target_bir_lowering=False)
    P = Prog()

    def din(name, shape, dt=F32):
        return nc.dram_tensor(name, shape, dt, kind="ExternalInput").ap()

    xb = din("xb", [S, D])
    cT = din("cT", [128, 8])
    w_ada = din("w_ada", [D, 6 * D])
    b_adaT = din("b_adaT", [128, 48])
    w_in = din("w_in", [D, 516])
    b_gate = din("b_gate", [1, 4])
    nw = din("nw", [1, 128])
    c_ident = din("c_ident", [128, 128])
    c_maskf = din("c_maskf", [128, 128])
    c_maskb = din("c_maskb", [128, 128])
    c_ule = din("c_ule", [128, 128])
    c_uge = din("c_uge", [128, 128])
    c_cs = din("c_cs", [128, 256])
    c_a1 = din("c_a1", [N1, 2 * N1])
    c_a2 = din("c_a2", [N1, 2 * N1])
    c_tw = din("c_tw", [128, 4 * N1])
    c_cn = din("c_cn", [128, 256])

    dbg = None
    if not fused:
        dbg = nc.dram_tensor("dbg", [S, 256], BF16, kind="ExternalOutput").ap()
    else:
        xq = din("xq", [TQ, D])
        b_ada_row = din("b_ada_row", [1, 6 * D])
        w_out = din("w_out", [D, D])
        ln1_g = din("ln1_g", [1, D]); ln1_b = din("ln1_b", [1, D])
        w_ff1 = din("w_ff1", [D, DFF]); b_ff1T = din("b_ff1T", [128, 32])
        w_ff2 = din("w_ff2", [DFF, D]); b_ff2 = din("b_ff2", [1, D])
        ln2_g = din("ln2_g", [1, D]); ln2_b = din("ln2_b", [1, D])
        gidx = din("gidx", [128, 4 * (TQ // 128)], mybir.dt.int32)
        out = nc.dram_tensor("out", [TQ, D], F32, kind="ExternalOutput").ap()

    so_d = nc.dram_tensor("so_d", [S, 128], F32).ap()
    hf_d = nc.dram_tensor("hf_d", [S, 128], F32).ap()
    mixg_m = nc.dram_tensor("mixg_m", [S, 128], BF16)
    mixg_f = nc.dram_tensor("mixg_f", [S, 128], BF16)
    gath_m = nc.dram_tensor("gath_m", [4 * S, 128], BF16)
    gath_f = nc.dram_tensor("gath_f", [4 * S, 128], BF16)
    NP_ = max(1, S // 2048)
    R_ = S // NP_
    CPP = R_ // 128

    import contextlib
    es = contextlib.ExitStack()
    with es:
        cur = [es]
        sems = {}

        def sb(name, shape, dt=F32):
            return cur[-1].enter_context(nc.sbuf_tensor(name, shape, dt))

        def flush():
            P.barrier_all('sp')
            for sname in P.semnames:
                if sname not in sems:
                    sems[sname] = es.enter_context(nc.semaphore(sname))
            with nc.Block() as block:
                P.emit(nc, sems, block)
            for e_ in P.ENG:
                P.streams[e_] = []
            for e_ in P.ENG:
                P.barrier_all(e_)

        psA = contextlib.ExitStack()
        psA.__enter__()

        def ps(name, shape, dt=F32):
            return psA.enter_context(nc.psum_tensor(name, shape, dt))

        ident = sb("ident", [128, 128], BF16)
        ones = sb("ones", [128, 128], F32)
        modT = sb("modT", [128, 48], F32)
        sc2p = sb("sc2p", [128, 8], F32)
        epsT = sb("epsT", [128, 1], F32)
        sGA = contextlib.ExitStack(); sGA.__enter__(); cur.append(sGA)
        maskf = sb("maskf", [128, 128], F32)
        maskb = sb("maskb", [128, 128], F32)
        ule = sb("ule", [128, 128], F32)
        uge = sb("uge", [128, 128], F32)
        bg = sb("bg", [128, 4], F32)
        nwb = sb("nwb", [128, 128], F32)
        cact = sb("cact", [128, 8], F32)
        cact_b = sb("cact_b", [128, 8], BF16)
        badaT = sb("badaT", [128, 48], F32)
        sc1p = sb("sc1p", [128, 8], F32)
        w_sb = sb("w_sb", [128, 8, 516], BF16)

        def ld(eng, dst, src, name, sem, reads=()):
            P.op(eng, lambda e: e.dma_start(out=dst, in_=src), reads=reads, writes=[name], dsem=sem)

        ld('pool', ident[:], c_ident, 'ident', 'D_c0')
        ld('sp', maskf[:], c_maskf, 'maskf', 'D_c1')
        ld('sp', maskb[:], c_maskb, 'maskb', 'D_c2')
        ld('sp', ule[:], c_ule, 'ule', 'D_c3')
        ld('sp', uge[:], c_uge, 'uge', 'D_c4')
        ld('sp', bg[:], b_gate.broadcast_to([128, 4]), 'bg', 'D_c5')
        ld('sp', nwb[:], nw.broadcast_to([128, 128]), 'nwb', 'D_c6')
        ld('sp', cact[:], cT, 'cact', 'D_c7')
        ld('sp', badaT[:], b_adaT, 'badaT', 'D_c8')
        ld('pool', w_sb[:], w_in.rearrange("(j p) n -> p j n", p=128), 'w_sb', 'D_c9')
        P.op('pool', lambda e: e.memset(ones[:], 1.0), writes=['ones'])
        P.op('pool', lambda e: e.memset(epsT[:], LN_EPS), writes=['epsT'])

        def finish():
            flush()
            return nc

        if stop_after == 'C':
            return finish()
        P.op('act', lambda e: e.activation(out=cact[:], in_=cact[:], func=AF.Silu), reads=['cact'], writes=['cact'])
        s0 = contextlib.ExitStack()
        s0.__enter__()
        cur.append(s0)
        wa = [sb(f"wa{i}", [128, 8, 1024], F32) for i in range(2)]
        if fused:
            g1b = sb("g1b", [128, 1024], F32)
            g2b = sb("g2b", [128, 1024], F32)
            g1row = nc.dram_tensor("g1row", [1, 1024], F32).ap()
            g2row = nc.dram_tensor("g2row", [1, 1024], F32).ap()
        pT = ps("pT", [128, 2, 4, 2, 128], BF16)
        pFZ = ps("pFZ", [128, 512], F32)
        pTMb = ps("pTM", [128, 512], F32)
        pTM = [pTMb, pTMb]
        pQ = ps("pQ", [128, 1024], BF16)
        pS = ps("pS", [128, 512], F32)
        pKV = ps("pKV", [128, 512], F32)
        pND = ps("pND", [128, 512], F32)
        P.psum_names |= {'pTa', 'pTb', 'pFZ', 'pTM', 'pQ', 'pS', 'pKV', 'pND'}
        pmod = pFZ
        pmod2 = pTM[0]
        crep = sb("crep", [128, 8, 128], F32)
        def mk_crep(e):
            i = None
            for j in range(8):
                i = e.tensor_scalar(out=crep[:, j, :], in0=ones[:], scalar1=cact[:, j:j + 1], scalar2=None,
                                    op0=ALU.mult)
            return i
        P.op('dve', mk_crep, reads=['cact', 'ones'], writes=['crep'])

        def load_wa(i, part):
            ld('sp', wa[i][:], w_ada[:, part * 1024:(part + 1) * 1024].rearrange("(j p) n -> p j n", p=128),
               f'wa{i}', f'D_wa{i}')

        def mod_featmajor(i, part):
            def f(e):
                inst = None
                for m in range(8):
                    for j in range(8):
                        inst = e.matmul(pmod[:, m:m + 1], lhsT=wa[i][:, j, m * 128:(m + 1) * 128],
                                        rhs=cact[:, j:j + 1], start=(j == 0), stop=(j == 7))
                return inst
            P.op('pe', f, reads=[f'wa{i}', 'cact'], writes=['pFZ'])
            P.op('dve', lambda e: e.tensor_tensor(out=modT[:, part * 8:(part + 1) * 8], in0=pmod[:, 0:8],
                                                  in1=badaT[:, part * 8:(part + 1) * 8], op=ALU.add),
                 reads=['pFZ', 'badaT'], writes=[f'modT{part}'])

        def mod_rowbcast(i, part, dst, dname):
            ld('sp', dst[:], b_ada_row[:, part * 1024:(part + 1) * 1024].broadcast_to([128, 1024]), dname,
               f'D_{dname}')
            for h in range(2):
                def f(e, h=h):
                    inst = None
                    for j in range(8):
                        inst = e.matmul(pmod2[:, :], lhsT=crep[:, j, :], rhs=wa[i][:, j, h * 512:(h + 1) * 512],
                                        start=(j == 0), stop=(j == 7))
                    return inst
                P.op('pe', f, reads=[f'wa{i}', 'crep'], writes=['pTM'])
                P.op('dve', lambda e, h=h: e.scalar_tensor_tensor(
                    out=dst[:, h * 512:(h + 1) * 512], in0=dst[:, h * 512:(h + 1) * 512], scalar=1.0,
                    in1=pmod2[:, :], op0=ALU.add, op1=ALU.add), reads=['pTM', dname], writes=[dname])

        load_wa(0, 0); load_wa(1, 1)
        mod_featmajor(0, 0)
        mod_featmajor(1, 1)
        P.op('dve', lambda e: e.tensor_scalar(out=sc1p[:], in0=modT[:, 8:16], scalar1=1.0, scalar2=None, op0=ALU.add),
             reads=['modT1'], writes=['sc1p'])
        if fused:
            load_wa(0, 2); load_wa(1, 3)
            mod_rowbcast(0, 2, g1b, 'g1b')
            mod_featmajor(1, 3)
            load_wa(0, 4); load_wa(1, 5)
            mod_featmajor(0, 4)
            P.op('dve', lambda e: e.tensor_scalar(out=sc2p[:], in0=modT[:, 32:40], scalar1=1.0, scalar2=None,
                                                  op0=ALU.add), reads=['modT4'], writes=['sc2p'])
            mod_rowbcast(1, 5, g2b, 'g2b')
            P.op('sp', lambda e: e.dma_start(out=g1row, in_=g1b[0:1, :]), reads=['g1b'], writes=['g1row'], dsem='D_g1r')
            P.op('sp', lambda e: e.dma_start(out=g2row, in_=g2b[0:1, :]), reads=['g2b'], writes=['g2row'], dsem='D_g2r')
        flush()
        cur.pop()
        s0.__exit__(None, None, None)
        sAF = contextlib.ExitStack(); sAF.__enter__(); cur.append(sAF)
        fzT = sb("fzT", [128, S], BF16)
        sA = contextlib.ExitStack(); sA.__enter__(); cur.append(sA)
        if stop_after == 'P0':
            return finish()
        NG = N1 // 2
        xt = [sb(f"xt{i}", [128, 1024], F32) for i in range(4)]
        xn = [sb(f"xn{i}", [128, 1024], BF16) for i in range(2)]
        bst = [sb(f"bst{i}", [128, 2, 6], F32) for i in range(2)]
        mv = [sb(f"mv{i}", [128, 2], F32) for i in range(2)]
        rs = [sb(f"rs{i}", [128, 2], F32) for i in range(2)]
        hT = [sb(f"hT{i}", [128, 8, 256], BF16) for i in range(2)]
        qk_tm = sb("qk_tm", [128, N1, 128], BF16)
        vp = sb("vp", [128, N1, 130], BF16)
        gt = sb("gt", [128, N1, 4], F32)
        scal = sb("scal", [128, NG, 16], F32)
        so_t = [sb(f"so{i}", [128, 128], F32) for i in range(2)]
        hf_t = [sb(f"hf{i}", [128, 128], F32) for i in range(2)]
        qkT = [sb(f"qkT{i}", [64, 2, 128], BF16) for i in range(2)]
        PTs = [sb(f"PTs{i}", [128, 128], BF16) for i in range(2)]
        ku = [sb(f"ku{i}", [128, 64], BF16) for i in range(2)]
        Dst = sb("Dst", [64, 129], F32)
        Cb = [sb(f"Cb{i}", [64, 129], BF16) for i in range(2)]
        ee = [sb(f"ee{i}", [128, 2, 2], F32) for i in range(2)]
        sp_ = [sb(f"sp{i}", [128, 2, 2], F32) for i in range(2)]
        ein = [sb(f"ein{i}", [128, 4, 4], F32) for i in range(2)]
        dd = [sb(f"dd{i}", [128, 2], F32) for i in range(2)]

        P.op('pool', lambda e: e.memset(vp[:, :, 128:130], 1.0), writes=['vp_ones'])

        def ln_stats(xtile, xname, k, par):
            def f(e):
                e.bn_stats(out=bst[par][:, 0, :], in_=xtile[:, 0:512])
                return e.bn_stats(out=bst[par][:, 1, :], in_=xtile[:, 512:1024])
            P.op('dve', f, reads=[xname], writes=[f'bst{par}'])
            P.op('dve', lambda e: e.bn_aggr(out=mv[par][:], in_=bst[par][:].rearrange("p a b -> p (a b)")),
                 reads=[f'bst{par}'], writes=[f'mv{par}'])
            P.op('act', lambda e: e.activation(out=rs[par][:, 0:1], in_=mv[par][:, 1:2], func=AF.Sqrt,
                                               bias=epsT[:, 0:1]),
                 reads=[f'mv{par}', 'epsT'], writes=[f'rs{par}a'])
            P.op('dve', lambda e: e.reciprocal(out=rs[par][:, 0:1], in_=rs[par][:, 0:1]),
                 reads=[f'rs{par}a'], writes=[f'rs{par}a'])
            P.op('dve', lambda e: e.tensor_scalar(out=rs[par][:, 1:2], in0=mv[par][:, 0:1], scalar1=rs[par][:, 0:1],
                                                  scalar2=-1.0, op0=ALU.mult, op1=ALU.mult),
                 reads=[f'rs{par}a', f'mv{par}'], writes=[f'rs{par}b'])

        def gate_scalars(g):
            gp = g % 2
            c0 = 2 * g
            def f1(e):
                e.activation(out=ee[gp][:, 0, :], in_=gt[:, c0:c0 + 2, 1], func=AF.Exp, scale=-1.0)
                return e.activation(out=ee[gp][:, 1, :], in_=gt[:, c0:c0 + 2, 3], func=AF.Exp, scale=-1.0)
            P.op('act', f1, reads=[f'gt{c0}', f'gt{c0 + 1}'], writes=[f'ee{gp}'])
            P.op('act', lambda e: e.activation(out=sp_[gp][:].rearrange("p a b -> p (a b)"),
                                               in_=ee[gp][:].rearrange("p a b -> p (a b)"), func=AF.Ln, bias=1.0),
                 reads=[f'ee{gp}'], writes=[f'sp{gp}'])
            def f2(e):
                e.matmul(pKV[:, 384:386], lhsT=ule[:], rhs=sp_[gp][:, 0, :], start=True, stop=True)
                e.matmul(pKV[:, 386:388], lhsT=uge[:], rhs=sp_[gp][:, 1, :], start=True, stop=True)
                return e.matmul(pKV[:, 388:392], lhsT=ones[:], rhs=sp_[gp][:].rearrange("p a b -> p (a b)"),
                                start=True, stop=True)
            P.op('pe', f2, reads=[f'sp{gp}', 'ule', 'uge', 'ones'], writes=['pKV'])
            def f3(e):
                e.tensor_tensor(out=ein[gp][:, 0, 0:2], in0=gt[:, c0:c0 + 2, 0], in1=pKV[:, 384:386], op=ALU.add)
                e.tensor_tensor(out=ein[gp][:, 0, 2:4], in0=gt[:, c0:c0 + 2, 2], in1=pKV[:, 386:388], op=ALU.add)
                e.tensor_copy(out=ein[gp][:, 1, :], in_=pKV[:, 384:388])
                return e.tensor_scalar(out=ein[gp][:, 3, :], in0=pKV[:, 388:392], scalar1=-1.0, scalar2=None,
                                       op0=ALU.mult)
            P.op('dve', f3, reads=['pKV', f'gt{c0}', f'gt{c0 + 1}'], writes=[f'ein{gp}a'])
            P.op('dve', lambda e: e.tensor_tensor(out=ein[gp][:, 2, :], in0=ein[gp][:, 0, :], in1=pKV[:, 388:392],
                                                  op=ALU.subtract),
                 reads=['pKV', f'ein{gp}a'], writes=[f'ein{gp}'])
            P.op('act', lambda e: e.activation(out=scal[:, g, :], in_=ein[gp][:].rearrange("p a b -> p (a b)"),
                                               func=AF.Exp),
                 reads=[f'ein{gp}', f'ein{gp}a'], writes=[f'scal{g}'])

        def sc_ap(g, k, d, ci, rows=128):
            i = k * 4 + d * 2 + ci
            return scal[0:rows, g, i:i + 1]

        def mlstm_chunk(c, d, part='both'):
            g = c // 2
            ci = c % 2
            par = c % 2
            mask, mname = (maskf, 'maskf') if d == 0 else (maskb, 'maskb')
            kv = pKV[0:64, par * 129:(par + 1) * 129]
            kvn = 'pKV'
            nd = pND[:, 0:129]
            cbi = c % 2
            if part in ('front', 'both'):
                mlstm_front(c, d, g, ci, par, mask, mname, kv, kvn)
            if part in ('tail', 'both'):
                mlstm_tail(c, d, g, ci, par, kv, kvn, nd, cbi)
            return nd, par

        def mlstm_front(c, d, g, ci, par, mask, mname, kv, kvn):
            def ftr(e):
                e.transpose(out=pQ[0:64, 0:128], in_=qk_tm[:, c, 0:64], identity=ident[:])
                return e.transpose(out=pQ[0:64, 128:256], in_=qk_tm[:, c, 64:128],
                                   identity=ident[:])
            P.op('pe', ftr, reads=[f'qk{c}', 'ident'], writes=['pQ'])
            P.op('act', lambda e: e.activation(out=qkT[par][:].rearrange("p a b -> p (a b)"),
                                               in_=pQ[0:64, 0:256], func=AF.Copy),
                 reads=['pQ'], writes=[f'qkT{par}'])
            P.op('pe', lambda e: e.matmul(pS[:, 0:128], lhsT=qkT[par][:, 1, :],
                                          rhs=qkT[par][:, 0, :], start=True, stop=True),
                 reads=[f'qkT{par}'], writes=['pS'])
            P.op('dve', lambda e: e.scalar_tensor_tensor(out=PTs[par][:], in0=pS[:, 0:128],
                                                         scalar=sc_ap(g, 0, d, ci), in1=mask[:], op0=ALU.mult,
                                                         op1=ALU.mult),
                 reads=['pS', f'scal{g}', mname], writes=[f'PTs{par}'])
            P.op('act', lambda e: e.activation(out=ku[par][:], in_=qk_tm[:, c, 64:128], func=AF.Copy,
                                               scale=sc_ap(g, 2, d, ci)),
                 reads=[f'qk{c}', f'scal{g}'], writes=[f'ku{par}'])
            P.op('pe', lambda e: e.matmul(kv, lhsT=ku[par][:], rhs=vp[:, c, 0:129], start=True, stop=True),
                 reads=[f'ku{par}', f'vp{c}', 'vp_ones'], writes=[kvn])

        def mlstm_tail(c, d, g, ci, par, kv, kvn, nd, cbi):
            def fnd(e):
                e.matmul(nd, lhsT=PTs[par][:], rhs=vp[:, c, 0:129], start=True, stop=False)
                return e.matmul(nd, lhsT=qkT[par][:, 0, :], rhs=Cb[cbi][:], start=False, stop=True)
            P.op('pe', fnd, reads=[f'PTs{par}', f'vp{c}', 'vp_ones', f'qkT{par}', f'Cb{cbi}'], writes=['pND'])
            P.op('dve', lambda e: e.tensor_scalar(out=dd[par][:, 0:1], in0=nd[:, 128:129], scalar1=-1.0, scalar2=None,
                                                  op0=ALU.mult),
                 reads=['pND'], writes=[f'dd{par}'])
            P.op('dve', lambda e: e.tensor_tensor(out=dd[par][:, 0:1], in0=nd[:, 128:129], in1=dd[par][:, 0:1],
                                                  op=ALU.max),
                 reads=['pND', f'dd{par}'], writes=[f'dd{par}'])
            P.op('dve', lambda e: e.tensor_scalar(out=dd[par][:, 0:1], in0=dd[par][:, 0:1], scalar1=sc_ap(g, 1, d, ci),
                                                  scalar2=None, op0=ALU.max),
                 reads=[f'dd{par}', f'scal{g}'], writes=[f'dd{par}'])
            P.op('dve', lambda e: e.reciprocal(out=dd[par][:, 1:2], in_=dd[par][:, 0:1]),
                 reads=[f'dd{par}'], writes=[f'ddr{par}'])
            P.op('dve', lambda e: e.scalar_tensor_tensor(out=Dst[:], in0=Dst[:], scalar=sc_ap(g, 3, d, ci, 64),
                                                         in1=kv, op0=ALU.mult, op1=ALU.add),
                 reads=[kvn, 'Dst', f'scal{g}'], writes=['Dst'])
            P.op('act', lambda e: e.activation(out=Cb[1 - cbi][:], in_=Dst[:], func=AF.Copy, scale=0.125),
                 reads=['Dst'], writes=[f'Cb{1 - cbi}'])

        def stageA(g):
            gp = g % 2
            for t in range(2):
                c = 2 * g + t
                xi = c % 4
                ld('sp', xt[xi][:], xb[c * 128:(c + 1) * 128, :], f'xt{xi}', f'D_x{xi}')
                ln_stats(xt[xi], f'xt{xi}', c, t)
                P.op('act', lambda e, xi=xi, t=t: e.activation(out=xn[t][:], in_=xt[xi][:], func=AF.Identity,
                                                               scale=rs[t][:, 0:1], bias=rs[t][:, 1:2]),
                     reads=[f'xt{xi}', f'rs{t}a', f'rs{t}b'], writes=[f'xn{t}'])
                def ftr(e, t=t):
                    inst = None
                    for j in range(8):
                        inst = e.transpose(out=pT[:, j // 4, j % 4, t, :], in_=xn[t][:, j * 128:(j + 1) * 128],
                                           identity=ident[:])
                    return inst
                P.op('pe', ftr, reads=[f'xn{t}', 'ident'], writes=['pTa', 'pTb'])
            for j in range(8):
                src = pT[:, j // 4, j % 4, :, :].rearrange("p a b -> p (a b)")
                dst = hT[gp][:, j, :]
                if j < 4:
                    P.op('act', lambda e, src=src, dst=dst, j=j: e.activation(out=dst, in_=src, func=AF.Identity,
                                                                              scale=sc1p[:, j:j + 1],
                                                                              bias=modT[:, j:j + 1]),
                         reads=['pTa', 'sc1p', 'modT0'], writes=[f'hT{gp}_{j}'])
                else:
                    P.op('dve', lambda e, src=src, dst=dst, j=j: e.tensor_scalar(out=dst, in0=src,
                                                                                 scalar1=sc1p[:, j:j + 1],
                                                                                 scalar2=modT[:, j:j + 1],
                                                                                 op0=ALU.mult, op1=ALU.add),
                         reads=['pTb', 'sc1p', 'modT0'], writes=[f'hT{gp}_{j}'])
            hnames = [f'hT{gp}_{j}' for j in range(8)]
            for t in range(2):
                c = 2 * g + t
                def ftm(e, t=t, gp=gp):
                    inst = None
                    for j in range(8):
                        inst = e.matmul(pTM[t][:, 0:388], lhsT=hT[gp][:, j, t * 128:(t + 1) * 128],
                                        rhs=w_sb[:, j, 0:388], start=(j == 0), stop=(j == 7))
                    return inst
                P.op('pe', ftm, reads=hnames + ['w_sb'], writes=['pTM'])
                P.op('act', lambda e, t=t, c=c: e.activation(out=vp[:, c, 0:128], in_=pTM[t][:, 128:256], func=AF.Copy),
                     reads=['pTM'], writes=[f'vp{c}'])
                P.op('act', lambda e, t=t: e.activation(out=so_t[t][:], in_=pTM[t][:, 256:384], func=AF.Sigmoid),
                     reads=['pTM'], writes=[f'so{t}'])
                P.op('dve', lambda e, t=t, c=c: e.tensor_copy(out=qk_tm[:, c, :], in_=pTM[t][:, 0:128]),
                     reads=['pTM'], writes=[f'qk{c}'])
                P.op('dve', lambda e, t=t, c=c: e.tensor_tensor(out=gt[:, c, :], in0=pTM[t][:, 384:388], in1=bg[:],
                                                                op=ALU.add),
                     reads=['pTM', 'bg'], writes=[f'gt{c}'])
                P.op('pool', lambda e, t=t, c=c: e.dma_start(out=so_d[c * 128:(c + 1) * 128, :], in_=so_t[t][:]),
                     reads=[f'so{t}'], writes=[f'so_d{c}'], dsem=f'D_so{t}')
                if t == 0:
                    def ffz(e, gp=gp):
                        inst = None
                        for j in range(8):
                            inst = e.matmul(pFZ[:, 0:256], lhsT=w_sb[:, j, 388:516], rhs=hT[gp][:, j, :],
                                            start=(j == 0), stop=(j == 7))
                        return inst
                    P.op('pe', ffz, reads=hnames + ['w_sb'], writes=['pFZ'])
                    P.op('act', lambda e, g=g: e.activation(out=fzT[:, g * 256:(g + 1) * 256], in_=pFZ[:, 0:256],
                                                            func=AF.Copy),
                         reads=['pFZ'], writes=[f'fzT{g}'])
            gate_scalars(g)
        def hf_out(c, nd, par):
            P.op('act', lambda e: e.activation(out=hf_t[par][:], in_=nd[:, 0:128], func=AF.Copy, scale=dd[par][:, 1:2]),
                 reads=['pND', f'ddr{par}'], writes=[f'hf{par}'])
            P.op('pool', lambda e: e.dma_start(out=hf_d[c * 128:(c + 1) * 128, :], in_=hf_t[par][:]),
                 reads=[f'hf{par}'], writes=[f'hf_d{c}'], dsem=f'D_hf{par}')

        P.op('pool', lambda e: e.memset(Dst[:], 0.0), writes=['Dst'])
        P.op('pool', lambda e: e.memset(Cb[0][:], 0.0), writes=['Cb0'])
        P.replay(P.record(lambda: stageA(0)))
        for g in range(NG):
            c0 = 2 * g
            f0 = P.record(lambda: mlstm_chunk(c0, 0, 'front'))
            t0_ = P.record(lambda: (mlstm_chunk(c0, 0, 'tail'), hf_out(c0, pND[:, 0:129], c0 % 2)))
            f1 = P.record(lambda: mlstm_chunk(c0 + 1, 0, 'front'))
            t1_ = P.record(lambda: (mlstm_chunk(c0 + 1, 0, 'tail'), hf_out(c0 + 1, pND[:, 0:129], (c0 + 1) % 2)))
            nxt = P.record(lambda: stageA(g + 1)) if g + 1 < NG else []
            n3 = len(nxt) // 3
            P.replay(f0, nxt[:n3])
            P.replay(t0_, f1, nxt[n3:2 * n3])
            P.replay(t1_, nxt[2 * n3:])
        if stop_after == 'P1':
            return finish()
        hs = [sb(f"hs{i}", [128, 128], F32) for i in range(2)]
        hn = [sb(f"hn{i}", [128, 128], F32) for i in range(2)]
        ym = [sb(f"ym{i}", [128, 128], BF16) for i in range(2)]
        sol = [sb(f"sol{i}", [128, 128], F32) for i in range(2)]
        hfl = [sb(f"hfl{i}", [128, 128], F32) for i in range(2)]
        bs2 = [sb(f"bs2{i}", [128, 6], F32) for i in range(2)]
        mv2 = [sb(f"mv2{i}", [128, 2], F32) for i in range(2)]
        rs2 = [sb(f"rs2{i}", [128, 2], F32) for i in range(2)]

        P.op('pool', lambda e: e.memset(Dst[:], 0.0), reads=[], writes=['Dst'])
        cb_first = (N1 - 1) % 2
        P.op('pool', lambda e: e.memset(Cb[cb_first][:], 0.0), writes=[f'Cb{cb_first}'])
        mt_m = dbg[:, 0:128] if not fused else mixg_m.ap()
        mt_f = dbg[:, 128:256] if not fused else mixg_f.ap()
        def s2_front(c):
            par = c % 2
            ld('sp', sol[par][:], so_d[c * 128:(c + 1) * 128, :], f'sol{par}', f'D_sol{par}', reads=[f'so_d{c}'])
            ld('sp', hfl[par][:], hf_d[c * 128:(c + 1) * 128, :], f'hfl{par}', f'D_hfl{par}', reads=[f'hf_d{c}'])
            P.op('pool', lambda e, par=par: e.tensor_tensor(out=sol[par][:], in0=sol[par][:], in1=nwb[:], op=ALU.mult),
                 reads=[f'sol{par}', 'nwb'], writes=[f'sol{par}'])
            mlstm_chunk(c, 1, 'front')

        def s2_tail(c):
            par = c % 2
            nd, _ = mlstm_chunk(c, 1, 'tail')
            P.op('dve', lambda e, nd=nd, par=par: e.scalar_tensor_tensor(out=hs[par][:], in0=nd[:, 0:128],
                                                                         scalar=dd[par][:, 1:2], in1=hfl[par][:],
                                                                         op0=ALU.mult, op1=ALU.add),
                 reads=['pND', f'ddr{par}', f'hfl{par}'], writes=[f'hs{par}'])
            P.op('dve', lambda e, par=par: e.bn_stats(out=bs2[par][:], in_=hs[par][:]),
                 reads=[f'hs{par}'], writes=[f'bs2{par}'])
            P.op('dve', lambda e, par=par: e.bn_aggr(out=mv2[par][:], in_=bs2[par][:]),
                 reads=[f'bs2{par}'], writes=[f'mv2{par}'])
            P.op('act', lambda e, par=par: e.activation(out=rs2[par][:, 0:1], in_=mv2[par][:, 1:2], func=AF.Sqrt,
                                                        bias=epsT[:, 0:1]),
                 reads=[f'mv2{par}', 'epsT'], writes=[f'rs2{par}a'])
            P.op('dve', lambda e, par=par: e.reciprocal(out=rs2[par][:, 0:1], in_=rs2[par][:, 0:1]),
                 reads=[f'rs2{par}a'], writes=[f'rs2{par}a'])
            P.op('dve', lambda e, par=par: e.tensor_scalar(out=rs2[par][:, 1:2], in0=mv2[par][:, 0:1],
                                                           scalar1=rs2[par][:, 0:1], scalar2=-1.0, op0=ALU.mult,
                                                           op1=ALU.mult),
                 reads=[f'rs2{par}a', f'mv2{par}'], writes=[f'rs2{par}b'])
            P.op('act', lambda e, par=par: e.activation(out=hn[par][:], in_=hs[par][:], func=AF.Identity,
                                                        scale=rs2[par][:, 0:1], bias=rs2[par][:, 1:2]),
                 reads=[f'hs{par}', f'rs2{par}a', f'rs2{par}b'], writes=[f'hn{par}'])
            P.op('pool', lambda e, par=par: e.tensor_tensor(out=ym[par][:], in0=hn[par][:], in1=sol[par][:],
                                                            op=ALU.mult),
                 reads=[f'hn{par}', f'sol{par}'], writes=[f'ym{par}'])
            P.op('pool', lambda e, par=par, c=c: e.dma_start(out=mt_m[c * 128:(c + 1) * 128, :],
                                                             in_=ym[par][:]),
                 reads=[f'ym{par}'], writes=[f'mixg_m_c{c}'], dsem=f'D_ym{par}')

        P.replay(P.record(lambda: s2_front(N1 - 1)))
        for c in range(N1 - 1, -1, -1):
            tl = P.record(lambda: s2_tail(c))
            fr = P.record(lambda: s2_front(c - 1)) if c > 0 else []
            P.replay(tl, fr)

        if stop_after == 'P2':
            return finish()
        flush()
        cur.pop(); sA.__exit__(None, None, None)
        sF = contextlib.ExitStack(); sF.__enter__(); cur.append(sF)
        piecesA = [i_ for i_ in range(NP_) if i_ % 2 == 0]
        piecesB = [i_ for i_ in range(NP_) if i_ % 2 == 1]

        def issue_ag(src, dst, i_, reads, wname):
            P.op('pool', lambda e: e.collective_compute(
                "AllGather", ALU.bypass, replica_groups=[[0, 1, 2, 3], [4, 5, 6, 7]],
                ins=[src.ap()[i_ * R_:(i_ + 1) * R_, :].opt()],
                outs=[dst.ap()[i_ * 4 * R_:(i_ + 1) * 4 * R_, :].opt()]),
                reads=reads, writes=[wname], dsem='CC', inc=1)
        if fused:
            for i_ in piecesA:
                issue_ag(mixg_m, gath_m, i_, [f'mixg_m_c{cc}' for cc in range(i_ * CPP, (i_ + 1) * CPP)], 'gath_m_A')
        cs_b = sb("cs_b", [128, 256], BF16)
        a1_b = sb("a1_b", [N1, 2 * N1], BF16)
        a2_b = sb("a2_b", [N1, 2 * N1], BF16)
        cn_b = sb("cn_b", [128, 256], BF16)
        tw = sb("tw", [128, 4 * N1], F32)
        ld('pool', cs_b[:], c_cs, 'cs_b', 'D_f0')
        ld('pool', a1_b[:], c_a1, 'a1_b', 'D_f1')
        ld('pool', a2_b[:], c_a2, 'a2_b', 'D_f2')
        ld('pool', cn_b[:], c_cn, 'cn_b', 'D_f3')
        ld('sp', tw[:], c_tw, 'tw', 'D_f4')
        G = sb("G", [128, 128, 256], BF16)
        Yr = sb("Yr", [128, N1, 128], BF16)
        Qp = sb("Qp", [128, N1, 128], BF16)
        yf = G[:].rearrange("p a b -> p (a b)")[:, 0:N1 * 128].rearrange("p (a b) -> p a b", b=128)
        t1 = [sb(f"t1_{i}", [128, 2 * N1], F32) for i in range(2)]
        t2 = [sb(f"t2_{i}", [128, 2 * N1], F32) for i in range(2)]
        pG = [pFZ, pTMb, pS, pKV]
        pGn = ['pFZ', 'pTM', 'pS', 'pKV']
        fz_all = [f'fzT{g}' for g in range(NG)]
        for i in range(64):
            k = i % 4
            eng = 'act' if k < 2 else 'dve'
            def f0(e, i=i, k=k):
                inst = None
                for q in range(2):
                    n2 = 2 * i + q
                    inst = e.matmul(pG[k][0:N1, q * 256:(q + 1) * 256], lhsT=fzT[:, n2:S:128], rhs=cs_b[:],
                                    start=True, stop=True)
                return inst
            P.op('pe', f0, reads=fz_all + ['cs_b'], writes=[pGn[k]])
            dst = G[0:N1, 2 * i:2 * i + 2, :].rearrange("p a b -> p (a b)")
            if eng == 'act':
                P.op('act', lambda e, k=k, dst=dst: e.activation(out=dst, in_=pG[k][0:N1, :], func=AF.Copy),
                     reads=[pGn[k]], writes=[f'G{i}'])
            else:
                P.op('dve', lambda e, k=k, dst=dst: e.tensor_copy(out=dst, in_=pG[k][0:N1, :]),
                     reads=[pGn[k]], writes=[f'G{i}'])
        g_all = [f'G{i}' for i in range(64)]
        pY = [pFZ, pTMb]
        pYn = ['pFZ', 'pTM']
        for j in range(128):
            k = j % 2
            def fa(e, j=j, k=k):
                e.matmul(pY[k][:, 0:2 * N1], lhsT=G[0:N1, :, j], rhs=a1_b[:], start=True, stop=False)
                return e.matmul(pY[k][:, 0:2 * N1], lhsT=G[0:N1, :, 128 + j], rhs=a2_b[:], start=False, stop=True)
            P.op('pe', fa, reads=g_all + ['a1_b', 'a2_b'], writes=[pYn[k]])
            P.op('dve', lambda e, k=k: e.tensor_tensor(out=t1[k][:], in0=pY[k][:, 0:2 * N1], in1=tw[:, 0:2 * N1],
                                                       op=ALU.mult),
                 reads=[pYn[k], 'tw'], writes=[f't1_{k}'])
            P.op('dve', lambda e, k=k: e.tensor_tensor(out=t2[k][:], in0=pY[k][:, 0:2 * N1],
                                                       in1=tw[:, 2 * N1:4 * N1], op=ALU.mult),
                 reads=[pYn[k], 'tw'], writes=[f't2_{k}'])
            P.op('pool', lambda e, k=k, j=j: e.tensor_tensor(out=Yr[:, :, j], in0=t1[k][:, 0:N1],
                                                             in1=t2[k][:, N1:2 * N1], op=ALU.subtract),
                 reads=[f't1_{k}', f't2_{k}'], writes=[f'Yr{j}'])
            P.op('pool', lambda e, k=k, j=j: e.tensor_tensor(out=Qp[:, :, j], in0=t1[k][:, N1:2 * N1],
                                                             in1=t2[k][:, 0:N1], op=ALU.add),
                 reads=[f't1_{k}', f't2_{k}'], writes=[f'Qp{j}'])
        y_all = [f'Yr{j}' for j in range(128)] + [f'Qp{j}' for j in range(128)]
        NB = N1 // 4
        pX = [pS, pKV]
        pXn = ['pS', 'pKV']
        for bi in range(NB):
            k = bi % 2
            def fc(e, bi=bi, k=k):
                e.matmul(pX[k][:, :], lhsT=cn_b[:, 0:128], rhs=Yr[:, 4 * bi:4 * bi + 4, :].rearrange("p a b -> p (a b)"),
                         start=True, stop=False)
                return e.matmul(pX[k][:, :], lhsT=cn_b[:, 128:256],
                                rhs=Qp[:, 4 * bi:4 * bi + 4, :].rearrange("p a b -> p (a b)"), start=False, stop=True)
            P.op('pe', fc, reads=y_all + ['cn_b'], writes=[pXn[k]])
            P.op('act', lambda e, bi=bi, k=k: e.activation(
                out=yf[:, 4 * bi:4 * bi + 4, :].rearrange("p a b -> p (a b)"), in_=pX[k][:, :], func=AF.Copy),
                reads=[pXn[k]], writes=[f'yf{bi}'])
        mt3 = mt_f.rearrange("(a b) j -> a b j", b=N1)
        npc = 4 if N1 >= 4 else 1
        for pc in range(npc):
            lo, hi = pc * N1 // npc, (pc + 1) * N1 // npc
            P.op('pool', lambda e, lo=lo, hi=hi: e.dma_start(out=mt3[:, lo:hi, :], in_=yf[:, lo:hi, :]),
                 reads=[f'yf{bi}' for bi in range(NB)], writes=['mixg_f'], dsem='D_yf')
        flush()
        cur.pop(); sF.__exit__(None, None, None)
        cur.pop(); sAF.__exit__(None, None, None)
        cur.pop(); sGA.__exit__(None, None, None)
        psA.__exit__(None, None, None)
        if not fused:
            return nc
        ctx = dict(nc=nc, P=P, es=es, cur=cur, flush=flush, ld=ld, sb=sb, xq=xq, w_out=w_out, ln1_g=ln1_g, ln1_b=ln1_b,
                   w_ff1=w_ff1, b_ff1T=b_ff1T, w_ff2=w_ff2, b_ff2=b_ff2, ln2_g=ln2_g, ln2_b=ln2_b, gidx=gidx, out=out,
                   gath_m=gath_m, gath_f=gath_f, mixg_f=mixg_f, mixg_m=mixg_m, issue_ag=issue_ag, piecesA=piecesA, piecesB=piecesB, CPP=CPP, ident=ident, ones=ones, epsT=epsT, modT=modT, sc2p=sc2p, g1row=g1row, g2row=g2row)
        build_B(N1, ctx)
        return nc


def build_B(N1, ctx=None):
    S = 128 * N1
    TQ = S // 4
    NT = TQ // 128
    NGB = NT // 2
    fusedB = ctx is not None
    if not fusedB:
        nc = bass.Bass("TRN2", target_bir_lowering=False)
        P = Prog()

        def din(name, shape, dt=F32):
            return nc.dram_tensor(name, shape, dt, kind="ExternalInput").ap()
        xq = din("xq", [TQ, D])
        mixq = din("mixq", [TQ, D], BF16)
        cT = din("cT", [128, 8])
        w_ada = din("w_ada", [D, 6 * D])
        b_adaT = din("b_adaT", [128, 48])
        b_ada_row = din("b_ada_row", [1, 6 * D])
        w_out = din("w_out", [D, D])
        ln1_g = din("ln1_g", [1, D]); ln1_b = din("ln1_b", [1, D])
        w_ff1 = din("w_ff1", [D, DFF]); b_ff1T = din("b_ff1T", [128, 32])
        w_ff2 = din("w_ff2", [DFF, D]); b_ff2 = din("b_ff2", [1, D])
        ln2_g = din("ln2_g", [1, D]); ln2_b = din("ln2_b", [1, D])
        c_ident = din("c_ident", [128, 128])
        out = nc.dram_tensor("out", [TQ, D], F32, kind="ExternalOutput").ap()
        es = contextlib.ExitStack()
        es.__enter__()
        cur = [es]
        sems = {}

        def sb(name, shape, dt=F32):
            return cur[-1].enter_context(nc.sbuf_tensor(name, shape, dt))

        def flush():
            P.barrier_all('sp')
            for sname in P.semnames:
                if sname not in sems:
                    sems[sname] = es.enter_context(nc.semaphore(sname))
            with nc.Block() as block:
                P.emit(nc, sems, block)
            for e_ in P.ENG:
                P.streams[e_] = []
            for e_ in P.ENG:
                P.barrier_all(e_)

        def ld(eng, dst, src, name, sem, reads=()):
            P.op(eng, lambda e: e.dma_start(out=dst, in_=src), reads=reads, writes=[name], dsem=sem)
    else:
        nc = ctx['nc']; P = ctx['P']; es = ctx['es']; cur = ctx['cur']; flush = ctx['flush']; ld = ctx['ld']; sb = ctx['sb']
        xq = ctx['xq']; w_out = ctx['w_out']; ln1_g = ctx['ln1_g']; ln1_b = ctx['ln1_b']; w_ff1 = ctx['w_ff1']
        b_ff1T = ctx['b_ff1T']; w_ff2 = ctx['w_ff2']; b_ff2 = ctx['b_ff2']; ln2_g = ctx['ln2_g']; ln2_b = ctx['ln2_b']
        gidx = ctx['gidx']; out = ctx['out']; gath_m = ctx['gath_m']; gath_f = ctx['gath_f']; mixg_f = ctx['mixg_f']
    psB = contextlib.ExitStack()
    psB.__enter__()

    def ps(name, shape, dt=F32):
        return psB.enter_context(nc.psum_tensor(name, shape, dt))
    if True:
        if fusedB:
            ident = ctx['ident']; ones = ctx['ones']; epsT = ctx['epsT']; modT = ctx['modT']; sc2p = ctx['sc2p']
            g1row = ctx['g1row']; g2row = ctx['g2row']
        else:
            ident = sb("ident", [128, 128], BF16)
            ones = sb("ones", [128, 128], F32)
            cact = sb("cact", [128, 8], F32)
            modT = sb("modT", [128, 48], F32)
            badaT = sb("badaT", [128, 48], F32)
            sc2p = sb("sc2p", [128, 8], F32)
            epsT = sb("epsT", [128, 1], F32)
            g1row = nc.dram_tensor("g1row", [1, 1024], F32).ap()
            g2row = nc.dram_tensor("g2row", [1, 1024], F32).ap()
        ones_b = sb("ones_b", [1, 128], BF16)
        b1T = sb("b1T", [128, 32], F32)
        bff2_b = sb("bff2_b", [1, 1024], BF16)
        l1g = sb("l1g", [128, 1024], F32); l1b = sb("l1b", [128, 1024], F32)
        l2g = sb("l2g", [128, 1024], F32); l2b = sb("l2b", [128, 1024], F32)
        pb_ = [ps(f"pb{i}", [128, 512], F32) for i in range(7)]
        pTr = ps("pTr", [128, 8, 128], BF16)
        P.psum_names |= {f'pb{i}' for i in range(7)} | {'pTr'}
        ld('sp', b1T[:], b_ff1T, 'b1T', 'D_b1')
        ld('pool', bff2_b[:], b_ff2, 'bff2_b', 'D_b2')
        ld('sp', l1g[:], ln1_g.broadcast_to([128, 1024]), 'l1g', 'D_l1g')
        ld('sp', l1b[:], ln1_b.broadcast_to([128, 1024]), 'l1b', 'D_l1b')
        ld('sp', l2g[:], ln2_g.broadcast_to([128, 1024]), 'l2g', 'D_l2g')
        ld('sp', l2b[:], ln2_b.broadcast_to([128, 1024]), 'l2b', 'D_l2b')
        P.op('pool', lambda e: e.memset(ones_b[:], 1.0), writes=['ones_b'])
        if fusedB:
            gix = sb("gix", [128, 4 * NT], mybir.dt.int32)
            ld('sp', gix[:], gidx, 'gix', 'D_gix')
        if not fusedB:
            ld('pool', ident[:], c_ident, 'ident', 'D_c0')
            ld('sp', cact[:], cT, 'cact', 'D_c7')
            ld('sp', badaT[:], b_adaT, 'badaT', 'D_c8')
            P.op('pool', lambda e: e.memset(ones[:], 1.0), writes=['ones'])
            P.op('pool', lambda e: e.memset(epsT[:], LN_EPS), writes=['epsT'])
            P.op('act', lambda e: e.activation(out=cact[:], in_=cact[:], func=AF.Silu), reads=['cact'], writes=['cact'])
            s0 = contextlib.ExitStack(); s0.__enter__(); cur.append(s0)
            wa = [sb(f"wa{i}", [128, 8, 1024], F32) for i in range(2)]
            crep = sb("crep", [128, 8, 128], F32)
            g1b = sb("g1b", [128, 1024], F32)
            g2b = sb("g2b", [128, 1024], F32)

            def mk_crep(e):
                i = None
                for j in range(8):
                    i = e.tensor_scalar(out=crep[:, j, :], in0=ones[:], scalar1=cact[:, j:j + 1], scalar2=None,
                                        op0=ALU.mult)
                return i
            P.op('dve', mk_crep, reads=['cact', 'ones'], writes=['crep'])

            def load_wa(i, part):
                ld('sp', wa[i][:], w_ada[:, part * 1024:(part + 1) * 1024].rearrange("(j p) n -> p j n", p=128),
                   f'wa{i}', f'D_wa{i}')

            def mod_featmajor(i, part):
                def f(e):
                    inst = None
                    for m in range(8):
                        for j in range(8):
                            inst = e.matmul(pb_[0][:, m:m + 1], lhsT=wa[i][:, j, m * 128:(m + 1) * 128],
                                            rhs=cact[:, j:j + 1], start=(j == 0), stop=(j == 7))
                    return inst
                P.op('pe', f, reads=[f'wa{i}', 'cact'], writes=['pb0'])
                P.op('dve', lambda e: e.tensor_tensor(out=modT[:, part * 8:(part + 1) * 8], in0=pb_[0][:, 0:8],
                                                      in1=badaT[:, part * 8:(part + 1) * 8], op=ALU.add),
                     reads=['pb0', 'badaT'], writes=[f'modT{part}'])

            def mod_rowbcast(i, part, dst, dname):
                ld('sp', dst[:], b_ada_row[:, part * 1024:(part + 1) * 1024].broadcast_to([128, 1024]), dname,
                   f'D_{dname}')
                for h in range(2):
                    def f(e, h=h):
                        inst = None
                        for j in range(8):
                            inst = e.matmul(pb_[1][:, :], lhsT=crep[:, j, :], rhs=wa[i][:, j, h * 512:(h + 1) * 512],
                                            start=(j == 0), stop=(j == 7))
                        return inst
                    P.op('pe', f, reads=[f'wa{i}', 'crep'], writes=['pb1'])
                    P.op('dve', lambda e, h=h: e.scalar_tensor_tensor(
                        out=dst[:, h * 512:(h + 1) * 512], in0=dst[:, h * 512:(h + 1) * 512], scalar=1.0,
                        in1=pb_[1][:, :], op0=ALU.add, op1=ALU.add), reads=['pb1', dname], writes=[dname])

            load_wa(0, 2); load_wa(1, 3)
            mod_rowbcast(0, 2, g1b, 'g1b')
            mod_featmajor(1, 3)
            load_wa(0, 4); load_wa(1, 5)
            mod_featmajor(0, 4)
            P.op('dve', lambda e: e.tensor_scalar(out=sc2p[:], in0=modT[:, 32:40], scalar1=1.0, scalar2=None,
                                                  op0=ALU.add), reads=['modT4'], writes=['sc2p'])
            mod_rowbcast(1, 5, g2b, 'g2b')
            P.op('sp', lambda e: e.dma_start(out=g1row, in_=g1b[0:1, :]), reads=['g1b'], writes=['g1row'], dsem='D_g1r')
            P.op('sp', lambda e: e.dma_start(out=g2row, in_=g2b[0:1, :]), reads=['g2b'], writes=['g2row'], dsem='D_g2r')
            flush()
            cur.pop(); s0.__exit__(None, None, None)

        wout = sb("wout", [128, 8, 1024], BF16)
        wff1 = sb("wff1", [128, 8, 4096], BF16)
        wff2 = sb("wff2", [128, 32, 1024], BF16)
        if fusedB:
            issue_ag = ctx['issue_ag']; mixg_m = ctx['mixg_m']; CPP = ctx['CPP']
            for i_ in ctx['piecesA']:
                issue_ag(mixg_f, gath_f, i_, ['mixg_f'], 'gath_f_A')
            for i_ in ctx['piecesB']:
                issue_ag(mixg_m, gath_m, i_, [f'mixg_m_c{cc}' for cc in range(i_ * CPP, (i_ + 1) * CPP)], 'gath_m_B')
            for i_ in ctx['piecesB']:
                issue_ag(mixg_f, gath_f, i_, ['mixg_f'], 'gath_f_B')
        sT = contextlib.ExitStack(); sT.__enter__(); cur.append(sT)
        tg1 = sb("tg1", [128, 1024], F32)
        tg2 = sb("tg2", [128, 1024], F32)
        stg = [sb(f"stg{i}", [128, 2048], F32) for i in range(3)]
        ld('sp', tg1[:], g1row.broadcast_to([128, 1024]), 'tg1', 'D_tg1', reads=['g1row'])
        ld('sp', tg2[:], g2row.broadcast_to([128, 1024]), 'tg2', 'D_tg2', reads=['g2row'])
        si = [0]

        def stage(src_ap, shape3=None):
            k = si[0] % 3
            si[0] += 1
            dst = stg[k][:] if shape3 is None else stg[k][:].rearrange("p (a b) -> p a b", a=shape3)
            ld('sp', dst, src_ap, f'stg{k}', f'D_stg{k}')
            return k
        for jj in range(4):
            k = stage(w_out[jj * 256:(jj + 1) * 256, :].rearrange("(a p) n -> p a n", p=128), 2)
            for a in range(2):
                P.op('dve', lambda e, k=k, a=a, jj=jj: e.tensor_tensor(out=wout[:, 2 * jj + a, :],
                                                                     in0=stg[k][:, a * 1024:(a + 1) * 1024],
                                                                     in1=tg1[:], op=ALU.mult),
                     reads=[f'stg{k}', 'tg1'], writes=['wout'])
        for j in range(8):
            for hh in range(2):
                k = stage(w_ff1[j * 128:(j + 1) * 128, hh * 2048:(hh + 1) * 2048])
                P.op('act', lambda e, k=k, j=j, hh=hh: e.activation(out=wff1[:, j, hh * 2048:(hh + 1) * 2048],
                                                                  in_=stg[k][:], func=AF.Copy),
                     reads=[f'stg{k}'], writes=['wff1'])
        for ff in range(16):
            k = stage(w_ff2[ff * 256:(ff + 1) * 256, :].rearrange("(a p) n -> p a n", p=128), 2)
            for a in range(2):
                f_ = 2 * ff + a
                eng_ = 'dve' if a == 0 else 'pool'
                P.op(eng_, lambda e, k=k, a=a, f_=f_: e.tensor_tensor(out=wff2[:, f_, :],
                                                                     in0=stg[k][:, a * 1024:(a + 1) * 1024],
                                                                     in1=tg2[:], op=ALU.mult),
                     reads=[f'stg{k}', 'tg2'], writes=[f'wff2_{f_}'])
        P.op('dve', lambda e: e.tensor_tensor(out=bff2_b[:], in0=bff2_b[:], in1=tg2[0:1, :], op=ALU.mult),
             reads=['bff2_b', 'tg2'], writes=['bff2_b'])
        w2n = [f'wff2_{f_}' for f_ in range(32)]
        flush()
        cur.pop(); sT.__exit__(None, None, None)

        xt = sb("xt", [128, 1024], F32)
        mxs = [sb(f"mx{i}", [128, 1024], BF16) for i in range(2)]
        x1 = sb("x1", [128, 4, 1024], F32)
        xn = sb("xn", [128, 1024], BF16)
        mixT = xn[:].rearrange("p (a b) -> p a b", b=128)
        h2T = [sb(f"h2TB{i}", [128, 8, 256], BF16) for i in range(2)]
        uT = sb("uT", [128, 8, 256], BF16)
        rt = [sb(f"rtB{i}", [128, 256], F32) for i in range(2)]
        bst = [sb(f"bstB{i}", [128, 2, 6], F32) for i in range(2)]
        mv = [sb(f"mvB{i}", [128, 2], F32) for i in range(2)]
        rs = [sb(f"rsB{i}", [128, 2], F32) for i in range(2)]

        def ln_stats(src, sname, k):
            def f(e):
                e.bn_stats(out=bst[k][:, 0, :], in_=src[:, 0:512])
                return e.bn_stats(out=bst[k][:, 1, :], in_=src[:, 512:1024])
            P.op('dve', f, reads=[sname], writes=[f'bst{k}'])
            yield
            P.op('dve', lambda e: e.bn_aggr(out=mv[k][:], in_=bst[k][:].rearrange("p a b -> p (a b)")),
                 reads=[f'bst{k}'], writes=[f'mv{k}'])
            yield
            P.op('act', lambda e: e.activation(out=rs[k][:, 0:1], in_=mv[k][:, 1:2], func=AF.Sqrt, bias=epsT[:, 0:1]),
                 reads=[f'mv{k}', 'epsT'], writes=[f'rsa{k}'])
            yield
            P.op('dve', lambda e: e.reciprocal(out=rs[k][:, 0:1], in_=rs[k][:, 0:1]), reads=[f'rsa{k}'],
                 writes=[f'rsa{k}'])
            yield
            P.op('dve', lambda e: e.tensor_scalar(out=rs[k][:, 1:2], in0=mv[k][:, 0:1], scalar1=rs[k][:, 0:1],
                                                  scalar2=-1.0, op0=ALU.mult, op1=ALU.mult),
                 reads=[f'rsa{k}', f'mv{k}'], writes=[f'rsb{k}'])
            yield

        def prologue(g):
            hp = g % 2
            for t in range(2):
                r0 = (2 * g + t) * 128
                mx = mxs[t]
                if not fusedB:
                    ld('sp', mx[:], mixq[r0:r0 + 128, :], f'mx{t}', f'D_m{t}')
                else:
                    T_ = 2 * g + t
                    for s_ in range(8):
                        gsrc = gath_m if s_ < 4 else gath_f
                        sr = s_ % 4
                        P.op('pool', lambda e, s_=s_, sr=sr, T_=T_, gsrc=gsrc, mx=mx: e.indirect_dma_start(
                            out=mx[:, s_ * 128:(s_ + 1) * 128], out_offset=None, in_=gsrc.ap()[:, :],
                            in_offset=bass.IndirectOffsetOnAxis(ap=gix[:, sr * NT + T_:sr * NT + T_ + 1], axis=0)),
                            reads=(['gath_m_A', 'gath_f_A', 'gix'] if (TQ == 2 * (S // max(1, S // 2048)) and T_ < NT // 2) else
                                   ['gath_m_A', 'gath_f_A', 'gath_m_B', 'gath_f_B', 'gix']),
                            writes=[f'mx{t}'], dsem=f'D_m{t}')
                yield
            for t in range(2):
                r0 = (2 * g + t) * 128
                xi = (2 * g + t) % 4
                x1t = x1[:, xi, :]
                xname = f'x1_{xi}'
                mx = mxs[t]
                ld('sp', xt[:], xq[r0:r0 + 128, :], 'xt', 'D_x')
                yield

                def ftr(e, mx=mx):
                    inst = None
                    for j in range(8):
                        inst = e.transpose(out=pTr[:, j, :], in_=mx[:, j * 128:(j + 1) * 128], identity=ident[:])
                    return inst
                P.op('pe', ftr, reads=[f'mx{t}', 'ident'], writes=['pTr'])
                yield
                P.op('act', lambda e: e.activation(out=xn[:], in_=pTr[:].rearrange("p a b -> p (a b)"), func=AF.Copy),
                     reads=['pTr'], writes=['xn'])
                yield
                for hf in range(2):
                    def fo(e, hf=hf):
                        inst = None
                        for j in range(8):
                            inst = e.matmul(pb_[0][:, :], lhsT=mixT[:, j, :], rhs=wout[:, j, hf * 512:(hf + 1) * 512],
                                            start=(j == 0), stop=(j == 7))
                        return inst
                    P.op('pe', fo, reads=['xn', 'wout'], writes=['pb0'])
                    yield
                    P.op('dve', lambda e, hf=hf: e.scalar_tensor_tensor(
                        out=xt[:, hf * 512:(hf + 1) * 512], in0=xt[:, hf * 512:(hf + 1) * 512], scalar=ALPHA,
                        in1=pb_[0][:, :], op0=ALU.mult, op1=ALU.add), reads=['pb0', 'xt'], writes=['xt'])
                    yield
                yield from ln_stats(xt, 'xt', 0)
                P.op('act', lambda e, x1t=x1t: e.activation(out=x1t, in_=xt[:], func=AF.Identity, scale=rs[0][:, 0:1],
                                                            bias=rs[0][:, 1:2]),
                     reads=['xt', 'rsa0', 'rsb0'], writes=[xname])
                yield
                P.op('dve', lambda e, x1t=x1t: e.tensor_tensor(out=x1t, in0=x1t, in1=l1g[:], op=ALU.mult),
                     reads=[xname, 'l1g'], writes=[xname])
                yield
                P.op('dve', lambda e, x1t=x1t: e.tensor_tensor(out=x1t, in0=x1t, in1=l1b[:], op=ALU.add),
                     reads=[xname, 'l1b'], writes=[xname])
                yield
                yield from ln_stats(x1t, xname, 0)
                P.op('act', lambda e, x1t=x1t: e.activation(out=xn[:], in_=x1t, func=AF.Identity, scale=rs[0][:, 0:1],
                                                            bias=rs[0][:, 1:2]),
                     reads=[xname, 'rsa0', 'rsb0'], writes=['xn'])
                yield

                def ftr2(e):
                    inst = None
                    for j in range(8):
                        inst = e.transpose(out=pTr[:, j, :], in_=xn[:, j * 128:(j + 1) * 128], identity=ident[:])
                    return inst
                P.op('pe', ftr2, reads=['xn', 'ident'], writes=['pTr'])
                yield
                for j in range(8):
                    P.op('act', lambda e, j=j, t=t, hp=hp: e.activation(out=h2T[hp][:, j, t * 128:(t + 1) * 128],
                                                                       in_=pTr[:, j, :], func=AF.Identity,
                                                                       scale=sc2p[:, j:j + 1],
                                                                       bias=modT[:, 24 + j:25 + j]),
                         reads=['pTr', 'sc2p', 'modT3'], writes=[f'h2T{hp}_{j}'])
                    if j % 2 == 1:
                        yield

        def ffn(g):
            hp = g % 2
            hn = [f'h2T{hp}_{j}' for j in range(8)]
            for fh in range(4):
                for fc in range(8):
                    f_ = fh * 8 + fc
                    k = 1 + fc % 2

                    def f1(e, f_=f_, k=k):
                        inst = None
                        for j in range(8):
                            inst = e.matmul(pb_[k][:, 0:256], lhsT=wff1[:, j, f_ * 128:(f_ + 1) * 128],
                                            rhs=h2T[hp][:, j, :], start=(j == 0), stop=(j == 7))
                        return inst
                    P.op('pe', f1, reads=hn + ['wff1'], writes=[f'pb{k}'])
                    P.op('act', lambda e, f_=f_, k=k: e.activation(out=rt[k - 1][:], in_=pb_[k][:, 0:256], func=AF.Relu,
                                                                   bias=b1T[:, f_:f_ + 1]),
                         reads=[f'pb{k}', 'b1T'], writes=[f'rt{k}'])
                    sq_eng = 'dve'
                    P.op(sq_eng, lambda e, fc=fc, k=k: e.tensor_tensor(out=uT[:, fc, :], in0=rt[k - 1][:],
                                                                      in1=rt[k - 1][:], op=ALU.mult),
                         reads=[f'rt{k}'], writes=[f'uT{fc}'])
                    yield
                un = [f'uT{fc}' for fc in range(8)]
                for t in range(2):
                    for ch in range(2):
                        bk = 3 + t * 2 + ch

                        def f2(e, t=t, ch=ch, bk=bk, fh=fh):
                            inst = None
                            for fc in range(8):
                                f_ = fh * 8 + fc
                                inst = e.matmul(pb_[bk][:, :], lhsT=uT[:, fc, t * 128:(t + 1) * 128],
                                                rhs=wff2[:, f_, ch * 512:(ch + 1) * 512],
                                                start=(f_ == 0), stop=False, skip_group_check=True)
                            if fh == 3:
                                inst = e.matmul(pb_[bk][:, :], lhsT=ones_b[0:1, :],
                                                rhs=bff2_b[0:1, ch * 512:(ch + 1) * 512],
                                                start=False, stop=True, skip_group_check=True)
                            return inst
                        P.op('pe', f2, reads=un + w2n + ['ones_b', 'bff2_b'], writes=[f'pb{bk}'])
                        yield

        def epilogue(g):
            for t in range(2):
                r0 = (2 * g + t) * 128
                xi = (2 * g + t) % 4
                x1t = x1[:, xi, :]
                xname = f'x1_{xi}'
                for ch in range(2):
                    bk = 3 + t * 2 + ch
                    P.op('dve', lambda e, ch=ch, bk=bk, xi=xi: e.scalar_tensor_tensor(
                        out=x1[:, xi, ch * 512:(ch + 1) * 512], in0=x1[:, xi, ch * 512:(ch + 1) * 512], scalar=ALPHA,
                        in1=pb_[bk][:, :], op0=ALU.mult, op1=ALU.add), reads=[f'pb{bk}', xname], writes=[xname])
                    yield
                yield from ln_stats(x1t, xname, 1)
                P.op('act', lambda e, x1t=x1t: e.activation(out=x1t, in_=x1t, func=AF.Identity, scale=rs[1][:, 0:1],
                                                            bias=rs[1][:, 1:2]),
                     reads=[xname, 'rsa1', 'rsb1'], writes=[xname])
                yield
                P.op('pool', lambda e, x1t=x1t: e.tensor_tensor(out=x1t, in0=x1t, in1=l2g[:], op=ALU.mult),
                     reads=[xname, 'l2g'], writes=[xname])
                yield
                P.op('pool', lambda e, x1t=x1t: e.tensor_tensor(out=x1t, in0=x1t, in1=l2b[:], op=ALU.add),
                     reads=[xname, 'l2b'], writes=[xname])
                yield
                P.op('sp', lambda e, r0=r0, x1t=x1t: e.dma_start(out=out[r0:r0 + 128, :], in_=x1t),
                     reads=[xname], writes=['out'], dsem=f'D_out{xi}')
                yield

        def interleave(*gens):
            gens = list(gens)
            while gens:
                for gen in list(gens):
                    try:
                        next(gen)
                    except StopIteration:
                        gens.remove(gen)

        interleave(prologue(0))
        for g in range(NGB):
            if g + 1 < NGB:
                interleave(ffn(g), prologue(g + 1))
            else:
                interleave(ffn(g))
            interleave(epilogue(g))
        flush()
    psB.__exit__(None, None, None)
    if not fusedB:
        es.__exit__(None, None, None)
    return nc


def _consts(N1):
    S = 128 * N1
    i = np.arange(128)
    ident = np.eye(128, dtype=np.float32)
    le = (i[:, None] <= i[None, :]).astype(np.float32)
    ge = (i[:, None] >= i[None, :]).astype(np.float32)
    ang = 2 * np.pi * np.outer(i, i) / 128.0
    c128, s128 = np.cos(ang), np.sin(ang)
    a = np.arange(N1)
    angA = 2 * np.pi * np.outer(a, a) / N1
    cA, sA = np.cos(angA), np.sin(angA)
    angT = 2 * np.pi * np.outer(i, a) / S
    tc, ts = np.cos(angT), np.sin(angT)
    nrm = 1.0 / np.sqrt(S * 128.0)
    return {
        "c_ident": ident, "c_maskf": 0.125 * le, "c_maskb": 0.125 * ge, "c_ule": le, "c_uge": ge,
        "c_cs": np.concatenate([c128, s128], 1).astype(np.float32),
        "c_a1": np.concatenate([cA, sA], 1).astype(np.float32),
        "c_a2": np.concatenate([-sA, cA], 1).astype(np.float32),
        "c_tw": np.concatenate([tc, tc, ts, ts], 1).astype(np.float32),
        "c_cn": (np.concatenate([c128, -s128], 1) * nrm).astype(np.float32),
    }


def make_in_maps(inp, N1, fused=False):
    S = 128 * N1
    TQ = S // 4
    cst = _consts(N1)
    f = lambda a: np.ascontiguousarray(np.asarray(a, dtype=np.float32))
    maps = []
    for core in range(NCORES):
        b, g = core // 4, core % 4
        cols = np.concatenate([
            np.arange(64) + 64 * g,
            256 + np.arange(64) + 64 * g,
            512 + np.arange(128) + 128 * g,
            1024 + np.arange(128) + 128 * g,
            2048 + np.array([g, 4 + g, 8 + g, 12 + g]),
            1536 + np.arange(128) + 128 * g,
        ])
        m = {
            "xb": f(inp["x"][b]),
            "xq": f(inp["x"][b, g * TQ:(g + 1) * TQ]),
            "cT": f(inp["c"][b].reshape(8, 128).T),
            "w_ada": f(inp["w_ada"][0]),
            "b_adaT": f(inp["b_ada"][0].reshape(48, 128).T),
            "b_ada_row": f(inp["b_ada"][0].reshape(1, -1)),
            "w_in": f(inp["w_in"][0][:, cols]),
            "b_gate": f(inp["b_gate"][0][[g, 4 + g, 8 + g, 12 + g]].reshape(1, 4)),
            "nw": f(inp["mlstm_norm_w"][0][128 * g:128 * (g + 1)].reshape(1, 128)),
            "w_out": f(inp["w_out"][0]),
            "ln1_g": f(inp["ln1_g"][0].reshape(1, -1)), "ln1_b": f(inp["ln1_b"][0].reshape(1, -1)),
            "w_ff1": f(inp["w_ff1"][0]), "b_ff1T": f(inp["b_ff1"][0].reshape(32, 128).T),
            "w_ff2": f(inp["w_ff2"][0]), "b_ff2": f(inp["b_ff2"][0].reshape(1, -1)),
            "ln2_g": f(inp["ln2_g"][0].reshape(1, -1)), "ln2_b": f(inp["ln2_b"][0].reshape(1, -1)),
        }
        if fused:
            NT = TQ // 128
            NP_ = max(1, S // 2048)
            R_ = S // NP_
            gi = np.empty((128, 4 * NT), np.int32)
            for s_ in range(4):
                for T_ in range(NT):
                    n_ = g * TQ + T_ * 128 + np.arange(128)
                    gi[:, s_ * NT + T_] = (n_ // R_) * 4 * R_ + s_ * R_ + n_ % R_
            m["gidx"] = gi
        m.update(cst)
        maps.append(m)
    return maps


def _bf16_to_mixq(results, N1):
    S = 128 * N1
    TQ = S // 4
    mixqs = []
    for core in range(NCORES):
        b, r = core // 4, core % 4
        parts_m = [results[4 * b + g]["dbg"][r * TQ:(r + 1) * TQ, 0:128] for g in range(4)]
        parts_f = [results[4 * b + g]["dbg"][r * TQ:(r + 1) * TQ, 128:256] for g in range(4)]
        mixqs.append(np.ascontiguousarray(np.concatenate(parts_m + parts_f, axis=1)))
    return mixqs


def kernel(**inputs):
    N1 = 128
    S = 128 * N1
    TQ = S // 4
    maps = make_in_maps(inputs, N1, fused=True)
    nc = build_nc(N1, fused=True)
    res = run_bass_kernel_spmd(nc, maps, core_ids=list(range(NCORES)))
    outp = np.empty((2, S, D), np.float32)
    for core in range(NCORES):
        b, g = core // 4, core % 4
        outp[b, g * TQ:(g + 1) * TQ] = np.asarray(res.results[core]["out"], dtype=np.float32)
    return outp
```
